# Optimizing a Trainium2 kernel written in Bass

```python
import math
import jax, jax.numpy as jnp
from jax import lax
import numpy as np

D_MODEL = 1024
BATCH = 2
SEQ = 8192
DEPTH = 2

MEM_LEN = 256
MLA_HEADS = 8
MLA_NOPE = 64
MLA_ROPE = 32
MLA_V = 64
MLA_Q_RANK = 384
MLA_KV_RANK = 256
ROPE_BASE = 10000.0
NSA_HEADS = 8
NSA_KV_HEADS = 2
NSA_GROUP = NSA_HEADS // NSA_KV_HEADS
NSA_DH = 64
CMP_STRIDE = 16
CMP_BLOCK = 2 * CMP_STRIDE
CMP_HIDDEN = 128
SEL_BLOCK = 64
N_SEL = 16
WINDOW = 512
FORCED_SCORE = 1e4
XA_HEADS = 4
XA_DH = D_MODEL // XA_HEADS
D_FF = 2816
CONV_W = 3
QB = 128
DEEPNORM_ALPHA = (2.0 * DEPTH) ** 0.25
DEEPNORM_BETA = (8.0 * DEPTH) ** -0.25
LN_EPS = 1e-5
RMS_EPS = 1e-6
NEG = -1e30

IN_SPLITS = (
    MLA_Q_RANK,
    MLA_KV_RANK,
    MLA_ROPE,
    NSA_HEADS * NSA_DH,
    NSA_KV_HEADS * NSA_DH,
    NSA_KV_HEADS * NSA_DH,
    NSA_KV_HEADS * NSA_DH,
    NSA_KV_HEADS * NSA_DH,
    NSA_KV_HEADS * NSA_DH,
    NSA_KV_HEADS * NSA_DH,
    NSA_HEADS * 3,
)
IN_COLS = sum(IN_SPLITS)
MIX_WIDTH = MLA_HEADS * MLA_V + NSA_HEADS * NSA_DH

kernel_name = 'hybrid_mla_nsa_deepnorm_trunk'


def layer_norm(x, g, b):
    xf = x.astype(jnp.float32)
    mu = jnp.mean(xf, -1, keepdims=True)
    var = jnp.mean(jnp.square(xf - mu), -1, keepdims=True)
    return ((xf - mu) * lax.rsqrt(var + LN_EPS) * g + b).astype(x.dtype)


def rms_norm(x, g):
    xf = x.astype(jnp.float32)
    return (xf * lax.rsqrt(jnp.mean(xf * xf, -1, keepdims=True) + RMS_EPS) * g).astype(x.dtype)


def masked_softmax(s, mask):
    s = jnp.where(mask, s, NEG)
    m = jnp.max(s, -1, keepdims=True)
    e = jnp.where(mask, jnp.exp(s - m), 0.0)
    d = jnp.sum(e, -1, keepdims=True)
    return e / jnp.maximum(d, 1e-30)


def alibi_slopes(n):
    return 2.0 ** (-8.0 * jnp.arange(1, n + 1, dtype=jnp.float32) / n)


def rope_tables(S):
    inv = ROPE_BASE ** (-jnp.arange(0, MLA_ROPE, 2, dtype=jnp.float32) / MLA_ROPE)
    ang = jnp.arange(S, dtype=jnp.float32)[:, None] * inv[None, :]
    return jnp.cos(ang), jnp.sin(ang)


def apply_rope(x, cos, sin):
    x1, x2 = jnp.split(x, 2, axis=-1)
    c = cos[None, :, None, :].astype(x.dtype)
    s = sin[None, :, None, :].astype(x.dtype)
    return jnp.concatenate([x1 * c - x2 * s, x1 * s + x2 * c], axis=-1)


def to_blocks(x):
    B, S = x.shape[:2]
    return x.reshape(B, S // QB, QB, *x.shape[2:]).swapaxes(0, 1)


def from_blocks(y):
    nq, B = y.shape[:2]
    return y.swapaxes(0, 1).reshape(B, nq * QB, *y.shape[3:])


def mla_mixer(cq, ckv, kr, cq_g, w_uq, ckv_g, w_ukv, cos, sin):
    B, S, _ = cq.shape
    q = (rms_norm(cq, cq_g) @ w_uq).reshape(B, S, MLA_HEADS, MLA_NOPE + MLA_ROPE)
    kv = (rms_norm(ckv, ckv_g) @ w_ukv).reshape(B, S, MLA_HEADS, MLA_NOPE + MLA_V)
    q_nope = q[..., :MLA_NOPE]
    q_rope = apply_rope(q[..., MLA_NOPE:], cos, sin)
    k_nope, v = kv[..., :MLA_NOPE], kv[..., MLA_NOPE:]
    k_rope = apply_rope(kr[:, :, None, :], cos, sin)[:, :, 0]
    scale = (MLA_NOPE + MLA_ROPE) ** -0.5
    kpos = jnp.arange(S)

    def block(args):
        i, qn, qr = args
        s = (jnp.einsum('bqhd,bkhd->bhqk', qn, k_nope)
             + jnp.einsum('bqhd,bkd->bhqk', qr, k_rope)).astype(jnp.float32) * scale
        qpos = i * QB + jnp.arange(QB)
        p = masked_softmax(s, kpos[None, :] <= qpos[:, None])
        return jnp.einsum('bhqk,bkhd->bqhd', p.astype(v.dtype), v)

    o = lax.map(block, (jnp.arange(S // QB), to_blocks(q_nope), to_blocks(q_rope)))
    return from_blocks(o).reshape(B, S, MLA_HEADS * MLA_V)


def compress_blocks(kv, pe, w1, b1, w2):
    B, S, G, D = kv.shape
    ch = kv.reshape(B, S // CMP_STRIDE, CMP_STRIDE, G, D)
    blk = jnp.concatenate([ch[:, :-1], ch[:, 1:]], axis=2) + pe[None, None, :, None, :]
    flat = blk.transpose(0, 1, 3, 2, 4).reshape(B, -1, G, CMP_BLOCK * D)
    return jax.nn.gelu(flat @ w1 + b1) @ w2


def nsa_mixer(q, k_cmp, v_cmp, k_slc, v_slc, k_win, v_win, gate_logits,
              pe_k, w1_k, b1_k, w2_k, pe_v, w1_v, b1_v, w2_v):
    B, S, _ = q.shape
    G, R, D = NSA_KV_HEADS, NSA_GROUP, NSA_DH
    scale = D ** -0.5
    slopes = alibi_slopes(NSA_HEADS).reshape(G, R)
    kv4 = lambda a: a.reshape(B, S, G, D)

    kc = compress_blocks(kv4(k_cmp), pe_k, w1_k, b1_k, w2_k)
    vc = compress_blocks(kv4(v_cmp), pe_v, w1_v, b1_v, w2_v)
    NC = kc.shape[1]
    c_start = jnp.arange(NC) * CMP_STRIDE
    c_end = c_start + CMP_BLOCK - 1

    NS = S // SEL_BLOCK
    n_sel = min(N_SEL, NS)
    s_start = jnp.arange(NS) * SEL_BLOCK
    overlap = jnp.clip(jnp.minimum(c_start[:, None] + CMP_BLOCK, s_start[None, :] + SEL_BLOCK)
                       - jnp.maximum(c_start[:, None], s_start[None, :]), 0).astype(jnp.float32) / CMP_BLOCK
    ks_blk = kv4(k_slc).reshape(B, NS, SEL_BLOCK, G, D).transpose(0, 3, 1, 2, 4).reshape(B, G, NS, SEL_BLOCK * D)
    vs_blk = kv4(v_slc).reshape(B, NS, SEL_BLOCK, G, D).transpose(0, 3, 1, 2, 4).reshape(B, G, NS, SEL_BLOCK * D)

    kw_pad = jnp.pad(kv4(k_win), ((0, 0), (WINDOW, 0), (0, 0), (0, 0)))
    vw_pad = jnp.pad(kv4(v_win), ((0, 0), (WINDOW, 0), (0, 0), (0, 0)))

    gates = jax.nn.sigmoid(gate_logits.astype(jnp.float32)).reshape(B, S, G, R, 3).astype(q.dtype)
    qg = q.reshape(B, S, G, R, D)
    M = n_sel * SEL_BLOCK
    blk_ids = jnp.arange(NS)

    def block(args):
        i, qi, gi = args
        t = i * QB + jnp.arange(QB)
        s = jnp.einsum('bqgrd,bngd->bgrqn', qi, kc).astype(jnp.float32) * scale
        s = s - slopes[:, :, None, None] * (t[:, None] - c_end[None, :]).astype(jnp.float32)
        p_c = masked_softmax(s, c_end[None, :] <= t[:, None])
        o_c = jnp.einsum('bgrqn,bngd->bqgrd', p_c.astype(vc.dtype), vc)
        imp = jnp.einsum('bgrqn,nj->bgqj', p_c, overlap)
        cur = t // SEL_BLOCK
        valid = blk_ids[None, :] <= cur[:, None]
        forced = (blk_ids[None, :] == 0) | (blk_ids[None, :] == cur[:, None]) | (blk_ids[None, :] == cur[:, None] - 1)
        score = jnp.where(forced, FORCED_SCORE, jnp.where(valid, imp, -1.0))
        top_score, idx = lax.top_k(score, n_sel)
        sel_ok = top_score >= 0
        flat_idx = idx.reshape(B, G, QB * n_sel)[..., None]
        kg = jnp.take_along_axis(ks_blk, flat_idx, axis=2).reshape(B, G, QB, M, D)
        vg = jnp.take_along_axis(vs_blk, flat_idx, axis=2).reshape(B, G, QB, M, D)
        pos5 = idx[..., None] * SEL_BLOCK + jnp.arange(SEL_BLOCK)
        dist5 = t[None, None, :, None, None] - pos5
        mask_s = (sel_ok[..., None] & (dist5 >= 0)).reshape(B, G, 1, QB, M)
        dist_s = dist5.reshape(B, G, 1, QB, M).astype(jnp.float32)
        s = jnp.einsum('bqgrd,bgqmd->bgrqm', qi, kg).astype(jnp.float32) * scale
        s = s - slopes[None, :, :, None, None] * dist_s
        p_s = masked_softmax(s, mask_s)
        o_s = jnp.einsum('bgrqm,bgqmd->bqgrd', p_s.astype(vg.dtype), vg)
        kw = lax.dynamic_slice_in_dim(kw_pad, i * QB, WINDOW + QB, axis=1)
        vw = lax.dynamic_slice_in_dim(vw_pad, i * QB, WINDOW + QB, axis=1)
        kpos = i * QB - WINDOW + jnp.arange(WINDOW + QB)
        dist = t[:, None] - kpos[None, :]
        s = jnp.einsum('bqgrd,bkgd->bgrqk', qi, kw).astype(jnp.float32) * scale
        s = s - slopes[:, :, None, None] * dist.astype(jnp.float32)
        p_w = masked_softmax(s, (dist >= 0) & (dist < WINDOW) & (kpos[None, :] >= 0))
        o_w = jnp.einsum('bgrqk,bkgd->bqgrd', p_w.astype(vw.dtype), vw)
        return gi[..., 0:1] * o_c + gi[..., 1:2] * o_s + gi[..., 2:3] * o_w

    o = lax.map(block, (jnp.arange(S // QB), to_blocks(qg), to_blocks(gates)))
    return from_blocks(o).reshape(B, S, NSA_HEADS * NSA_DH)


def memory_cross_attn(x, memn, wq, wkv, wo):
    B, S, _ = x.shape
    Mm = memn.shape[1]
    q = (x @ wq).reshape(B, S, XA_HEADS, XA_DH)
    k, v = jnp.split(memn @ wkv, 2, axis=-1)
    k = k.reshape(B, Mm, XA_HEADS, XA_DH)
    v = v.reshape(B, Mm, XA_HEADS, XA_DH)
    s = jnp.einsum('bshd,bmhd->bhsm', q, k).astype(jnp.float32) * (XA_DH ** -0.5)
    p = jax.nn.softmax(s, axis=-1)
    o = jnp.einsum('bhsm,bmhd->bshd', p.astype(v.dtype), v).reshape(B, S, XA_HEADS * XA_DH)
    return o @ wo


def conv_ffn(x, w_up, conv_w, conv_b, w_down):
    u = x @ w_up
    C = u.shape[-1]
    u = lax.conv_general_dilated(u, conv_w[:, None, :].astype(u.dtype), window_strides=(1,),
                                 padding=[(CONV_W - 1, 0)],
                                 dimension_numbers=('NWC', 'WIO', 'NWC'),
                                 feature_group_count=C) + conv_b
    a, g = jnp.split(u, 2, axis=-1)
    return (a * jax.nn.silu(g)) @ w_down


def setup_inputs(seed: int = 0) -> dict:
    key = jax.random.key(seed)
    keys = iter(jax.random.split(key, 40))
    L = DEPTH

    def nrm(shape, scale):
        return jax.random.normal(next(keys), shape, jnp.float32) * scale

    def gain(shape):
        return 1.0 + nrm(shape, 0.02)

    cmp_in = CMP_BLOCK * NSA_DH
    return {
        'x': nrm((BATCH, SEQ, D_MODEL), 1.0),
        'mem': nrm((BATCH, MEM_LEN, D_MODEL), 1.0),
        'ln_in_g': gain((D_MODEL,)),
        'ln_in_b': nrm((D_MODEL,), 0.02),
        'ln_mem_g': gain((D_MODEL,)),
        'ln_mem_b': nrm((D_MODEL,), 0.02),
        'w_in': nrm((L, D_MODEL, IN_COLS), D_MODEL ** -0.5),
        'mla_cq_g': gain((L, MLA_Q_RANK)),
        'mla_w_uq': nrm((L, MLA_Q_RANK, MLA_HEADS * (MLA_NOPE + MLA_ROPE)), MLA_Q_RANK ** -0.5),
        'mla_ckv_g': gain((L, MLA_KV_RANK)),
        'mla_w_ukv': nrm((L, MLA_KV_RANK, MLA_HEADS * (MLA_NOPE + MLA_V)), MLA_KV_RANK ** -0.5),
        'nsa_pe_k': nrm((L, CMP_BLOCK, NSA_DH), 0.02),
        'nsa_w1_k': nrm((L, cmp_in, CMP_HIDDEN), cmp_in ** -0.5),
        'nsa_b1_k': nrm((L, CMP_HIDDEN), 0.02),
        'nsa_w2_k': nrm((L, CMP_HIDDEN, NSA_DH), CMP_HIDDEN ** -0.5),
        'nsa_pe_v': nrm((L, CMP_BLOCK, NSA_DH), 0.02),
        'nsa_w1_v': nrm((L, cmp_in, CMP_HIDDEN), cmp_in ** -0.5),
        'nsa_b1_v': nrm((L, CMP_HIDDEN), 0.02),
        'nsa_w2_v': nrm((L, CMP_HIDDEN, NSA_DH), CMP_HIDDEN ** -0.5),
        'w_out': nrm((L, MIX_WIDTH, D_MODEL), DEEPNORM_BETA * MIX_WIDTH ** -0.5),
        'ln1_g': gain((L, D_MODEL)),
        'ln1_b': nrm((L, D_MODEL), 0.02),
        'xa_wq': nrm((L, D_MODEL, XA_HEADS * XA_DH), D_MODEL ** -0.5),
        'xa_wkv': nrm((L, D_MODEL, 2 * XA_HEADS * XA_DH), D_MODEL ** -0.5),
        'xa_wo': nrm((L, XA_HEADS * XA_DH, D_MODEL), DEEPNORM_BETA * (XA_HEADS * XA_DH) ** -0.5),
        'ln2_g': gain((L, D_MODEL)),
        'ln2_b': nrm((L, D_MODEL), 0.02),
        'ffn_w_up': nrm((L, D_MODEL, 2 * D_FF), D_MODEL ** -0.5),
        'ffn_conv_w': nrm((L, CONV_W, 2 * D_FF), CONV_W ** -0.5),
        'ffn_conv_b': nrm((L, 2 * D_FF), 0.02),
        'ffn_w_down': nrm((L, D_FF, D_MODEL), DEEPNORM_BETA * D_FF ** -0.5),
        'ln3_g': gain((L, D_MODEL)),
        'ln3_b': nrm((L, D_MODEL), 0.02),
    }


def reference(x, mem, ln_in_g, ln_in_b, ln_mem_g, ln_mem_b, w_in,
              mla_cq_g, mla_w_uq, mla_ckv_g, mla_w_ukv,
              nsa_pe_k, nsa_w1_k, nsa_b1_k, nsa_w2_k,
              nsa_pe_v, nsa_w1_v, nsa_b1_v, nsa_w2_v,
              w_out, ln1_g, ln1_b, xa_wq, xa_wkv, xa_wo, ln2_g, ln2_b,
              ffn_w_up, ffn_conv_w, ffn_conv_b, ffn_w_down, ln3_g, ln3_b):
    S = x.shape[1]
    x = layer_norm(x, ln_in_g, ln_in_b)
    memn = layer_norm(mem, ln_mem_g, ln_mem_b)
    cos, sin = rope_tables(S)
    split_at = [int(v) for v in np.cumsum(IN_SPLITS)[:-1]]
    for l in range(DEPTH):
        h = x @ w_in[l]
        cq, ckv, kr, nq, kc, vc, ks, vs, kw, vw, gl = jnp.split(h, split_at, axis=-1)
        o_a = mla_mixer(cq, ckv, kr, mla_cq_g[l], mla_w_uq[l], mla_ckv_g[l], mla_w_ukv[l], cos, sin)
        o_b = nsa_mixer(nq, kc, vc, ks, vs, kw, vw, gl,
                        nsa_pe_k[l], nsa_w1_k[l], nsa_b1_k[l], nsa_w2_k[l],
                        nsa_pe_v[l], nsa_w1_v[l], nsa_b1_v[l], nsa_w2_v[l])
        mix = jnp.concatenate([o_a, o_b], axis=-1) @ w_out[l]
        x = layer_norm(DEEPNORM_ALPHA * x + mix, ln1_g[l], ln1_b[l])
        xa = memory_cross_attn(x, memn, xa_wq[l], xa_wkv[l], xa_wo[l])
        x = layer_norm(DEEPNORM_ALPHA * x + xa, ln2_g[l], ln2_b[l])
        f = conv_ffn(x, ffn_w_up[l], ffn_conv_w[l], ffn_conv_b[l], ffn_w_down[l])
        x = layer_norm(DEEPNORM_ALPHA * x + f, ln3_g[l], ln3_b[l])
    return x
```

```python
import math
import numpy as np
import ml_dtypes
import concourse.bass as bass
import concourse.mybir as mybir
from concourse.bass_utils import run_bass_kernel_spmd

F32 = mybir.dt.float32
BF16 = mybir.dt.bfloat16
AF = mybir.ActivationFunctionType
ALU = mybir.AluOpType
NPBF = ml_dtypes.bfloat16

S = 8192
D = 1024
NCORES = 8
ALPHA = (2.0 * 2) ** 0.25
MLA_SCALE = 96 ** -0.5


class Buf:
    __slots__ = ("w", "r", "psum")

    def __init__(self, psum=False):
        self.w = []
        self.r = {}
        self.psum = psum


class Rot:
    def __init__(self, items):
        self.items = items
        self.i = 0

    def next(self):
        it = self.items[self.i]
        self.i = (self.i + 1) % len(self.items)
        return it


class FW:
    COMPUTE = ("pe", "act", "dve", "pool")

    def __init__(self, nc, n_dma_sems=32):
        self.nc = nc
        self.E = {"pe": nc.tensor, "act": nc.scalar, "dve": nc.vector,
                  "pool": nc.gpsimd, "sp": nc.sync}
        self.sem = {}
        self.cnt = {}
        for e in self.COMPUTE:
            self.sem[e] = nc.alloc_semaphore(name="s_" + e)
            self.cnt[e] = 0
        self.waited = {e: {} for e in self.E}
        self.dsem = []
        for i in range(n_dma_sems):
            key = "d%d" % i
            self.sem[key] = nc.alloc_semaphore(name="s_" + key)
            self.cnt[key] = 0
            self.dsem.append(key)
        self.dnext = 0

    def _wait(self, eng, key, val):
        if val <= 0 or self.waited[eng].get(key, 0) >= val:
            return
        self.E[eng].wait_ge(self.sem[key], val)
        self.waited[eng][key] = val

    def _deps(self, eng, reads, writes):
        deps = {}
        for b in reads:
            for (k, v) in b.w:
                if deps.get(k, 0) < v:
                    deps[k] = v
            if b.psum:
                for k, v in b.r.items():
                    if k != eng and deps.get(k, 0) < v:
                        deps[k] = v
        for b in writes:
            for (k, v) in b.w:
                if deps.get(k, 0) < v:
                    deps[k] = v
            for k, v in b.r.items():
                if deps.get(k, 0) < v:
                    deps[k] = v
        for k, v in deps.items():
            if k == eng:
                if eng == "pe":
                    continue
                if v <= self.cnt[eng] - 3:
                    continue
            self._wait(eng, k, v)

    def _mark(self, tok, reads, writes):
        k, v = tok
        for b in reads:
            if b.r.get(k, 0) < v:
                b.r[k] = v
        for b in writes:
            b.w = [tok]
            b.r = {}

    def op(self, eng, fn, reads=(), writes=()):
        self._deps(eng, reads, writes)
        ins = fn(self.E[eng])
        self.cnt[eng] += 1
        ins.then_inc(self.sem[eng], 1)
        self._mark((eng, self.cnt[eng]), reads, writes)
        return ins

    def dma(self, q, out, in_, reads=(), writes=(), **kw):
        self._deps(q, reads, writes)
        key = self.dsem[self.dnext]
        self.dnext = (self.dnext + 1) % len(self.dsem)
        self._wait(q, key, self.cnt[key])
        ins = self.E[q].dma_start(out=out, in_=in_, **kw)
        self.cnt[key] += 16
        ins.then_inc(self.sem[key], 16)
        self._mark((key, self.cnt[key]), reads, writes)

    def finish(self, eng="sp"):
        for key in self.dsem:
            self._wait(eng, key, self.cnt[key])


def _consts_B():
    i = np.arange(128)[:, None]
    u = np.arange(512)[None, :]
    masks = np.zeros((13, 128, 512), np.float32)
    for m in range(4):
        masks[m] = (u - i - 128 * m >= 0)
    for m in range(4):
        o = -512 + 128 * m
        masks[4 + m] = (u - i - o < 512)
    for m in range(5):
        masks[8 + m] = (16 * i + 31 - 512 * m <= u)
    n = np.arange(512)
    cs = 16 * n
    j = np.arange(128)
    ov = np.clip(np.minimum(cs[:, None] + 32, 64 * j[None, :] + 64)
                 - np.maximum(cs[:, None], 64 * j[None, :]), 0, None).astype(np.float32) / 32.0
    ov[511] = 0.0
    ovo = np.concatenate([ov, np.ones((512, 1), np.float32)], axis=1)
    ovo[511] = 0.0
    ovo = ovo.reshape(4, 128, 129)
    p = np.arange(128)[:, None]
    hp = (p >= 64).astype(np.int64)
    m = np.arange(256)[None, :] - 127
    vw = (m <= hp).astype(np.float32)
    cw = 2e4 * (m == hp) + 4e4 * (m == hp - 1) + (vw - 1.0)
    bf0 = np.zeros((128, 128), np.float32)
    bf0[:, 0] = 1e4
    wide = np.concatenate([vw, cw.astype(np.float32), bf0], axis=1)
    ew = (np.arange(128)[:, None] == (np.arange(S)[None, :] // 64)).astype(np.float32)
    ident = np.eye(128, dtype=np.float32)
    ce = 16 * np.arange(512) + 31
    kaugc = np.stack([np.ones(512), np.ones(512), 128.0 * (ce // 128), (ce % 128).astype(np.float64)]).astype(np.float32)
    return dict(cmask=((masks - 1.0) * 30000.0).astype(NPBF), ov=ovo.astype(NPBF), wide=wide.astype(np.float32),
                ew=ew.astype(NPBF), ident=ident.astype(NPBF), kaugc=kaugc.astype(NPBF))


def _aug_tables():
    t = np.arange(S)
    a = (t // 128).astype(np.float64)
    b = (t % 128).astype(np.float64)
    slopes = 2.0 ** (-(np.arange(1, 9)))
    qa = np.zeros((8, 4, S), np.float32)
    for h in range(8):
        s8 = 8.0 * slopes[h]
        qa[h, 0] = -s8 * 128.0 * a
        qa[h, 1] = -s8 * b
        qa[h, 2] = s8
        qa[h, 3] = s8
    ka = np.stack([np.ones(S), np.ones(S), 128.0 * a, b]).astype(np.float32)
    return qa.astype(NPBF), ka.astype(NPBF)


def build_B():
    nc = bass.Bass("TRN2", target_bir_lowering=False)
    fw = FW(nc)

    def din(name, shape, dt):
        return nc.dram_tensor(name, shape, dt, kind="ExternalInput").ap()

    mq = din("mq", [2, 96, S], BF16)
    mk = din("mk", [2, 96, S], BF16)
    mv = din("mv", [2, 128, 64, 64], BF16)
    nq = din("nq", [4, 68, S], BF16)
    nkc = din("nkc", [2, 64, S], BF16)
    nks = din("nks", [68, S], BF16)
    nvs = din("nvs", [128, 64, 64], BF16)
    nkw = din("nkw", [68, S], BF16)
    nvw = din("nvw", [128, 64, 64], BF16)
    gates_h = nc.dram_tensor("gates", [6, S], F32, kind="ExternalInput")
    peT = din("peT", [2, 64, 32], F32)
    w1 = din("w1", [2, 2048, 128], F32)
    b1 = din("b1", [2, 128, 1], F32)
    w2 = din("w2", [2, 128, 64], F32)
    c_cmask = din("cmask_c", [13, 128, 512], BF16)
    c_ov = din("ov_c", [4, 128, 129], BF16)
    c_wide = din("wide_c", [128, 640], F32)
    c_ew = din("ew_c", [128, S], BF16)
    c_ident = din("ident_c", [128, 128], BF16)
    c_kaugc = din("kaugc_c", [4, 512], BF16)
    out = nc.dram_tensor("o", [4, 64, S], BF16, kind="ExternalOutput").ap()

    def sb(name, shape, dt):
        return nc.alloc_sbuf_tensor(name, shape, dt)

    big = [sb("big%d" % i, [128, S], BF16) for i in range(4)]
    bbig = [Buf() for _ in range(4)]
    selT = sb("selT", [128, S], BF16)
    b_selT = Buf()
    ew = sb("ew", [128, S], BF16)
    cmask = sb("cmask_s", [128, 13, 512], BF16)
    ov = sb("ov_s", [128, 4, 129], BF16)
    wide = sb("wide_s", [128, 640], F32)
    ident = sb("ident_s", [128, 128], BF16)
    b_const = Buf()
    cin = big[2]
    b_cin = bbig[2]
    w1s = sb("w1s", [64, 32, 128], BF16)
    w2s = sb("w2s", [128, 64], BF16)
    b1s = sb("b1s", [128, 1], F32)
    peTs = sb("peTs", [64, 32], BF16)
    b_cw = Buf()
    cbias = sb("cbias", [128, 1], F32)
    b_cbias = Buf()
    zf = sb("zf", [128, 512], F32)
    zt = sb("zt", [128, 512], F32)
    zs = sb("zs", [128, 512], F32)
    b_zf, b_zt, b_zs = Buf(), Buf(), Buf()
    hid = sb("hid", [128, 512], BF16)
    b_hid = Buf()
    kcaug = sb("kcaug", [68, 512], BF16)
    b_kc = Buf()
    vcaug = sb("vcaug", [128, 4, 128], BF16)
    b_vc = Buf()
    pc = sb("pc", [128, 16, 512], BF16)
    b_pc = [Buf() for _ in range(16)]
    qch = [sb("qch%d" % i, [96, 512], BF16) for i in range(6)]
    qrot = Rot([(qch[i], Buf()) for i in range(6)])
    pts = [sb("pt%d" % i, [128, 512], BF16) for i in range(4)]
    prot = Rot([(pts[i], Buf()) for i in range(4)])
    pre = [sb("pre%d" % i, [128, 512], F32) for i in range(3)]
    prerot = Rot([(pre[i], Buf()) for i in range(3)])
    mex = [sb("mex%d" % i, [128, 512], BF16) for i in range(3)]
    mrot = Rot([(mex[i], Buf()) for i in range(3)])
    gbc = [sb("gbc%d" % i, [64, 6, 512], F32) for i in range(1)]
    grot = Rot([(gbc[i], Buf()) for i in range(1)])
    rd = [sb("rd%d" % i, [64, 512], F32) for i in range(2)]
    rrot = Rot([(rd[i], Buf()) for i in range(2)])
    tmpc = [sb("tmpc%d" % i, [64, 512], F32) for i in range(2)]
    trot = Rot([(tmpc[i], Buf()) for i in range(2)])
    oacc = [sb("oacc%d" % i, [64, 512], F32) for i in range(4)]
    oarot = Rot([(oacc[i], Buf()) for i in range(4)])
    obf = [sb("obf%d" % i, [64, 512], BF16) for i in range(4)]
    obrot = Rot([(obf[i], Buf()) for i in range(4)])
    imp = sb("imp", [128, 128], F32)
    sc_t = sb("sc_t", [128, 128], F32)
    sc_u = sb("sc_u", [128, 128], F32)
    m8a = sb("m8a", [128, 8], F32)
    m8b = sb("m8b", [128, 8], F32)
    rdc = sb("rdc", [128, 1], F32)
    selb = sb("selb", [128, 128], BF16)
    b_imp, b_sct, b_scu, b_m8a, b_m8b, b_rdc, b_selb = [Buf() for _ in range(7)]

    banks = [nc.alloc_psum_tensor("bank%d" % i, [128, 512], F32) for i in range(8)]
    work = Rot([(banks[i], Buf(True)) for i in range(4)])
    accr = Rot([(banks[4 + i], Buf(True)) for i in range(4)])
    b_out = Buf()

    fw.dma("sp", cmask[:, :, :], c_cmask.rearrange("m p u -> p m u"), writes=[b_const])
    fw.dma("sp", ov[:, :, :], c_ov.rearrange("c p j -> p c j"), writes=[b_const])
    fw.dma("sp", wide[:, :], c_wide, writes=[b_const])
    fw.dma("sp", ew[:, :], c_ew, writes=[b_const])
    fw.dma("sp", ident[:, :], c_ident, writes=[b_const])
    for bi in (1, 3):
        v4 = big[bi][:, :].rearrange("p (k c) -> p k c", c=128)
        fw.op("pool", lambda e: e.memset(v4[:, :, 64:128], 1.0), writes=[bbig[bi]])

    def load_kv(bk, bv, ksrc, vsrc, krows):
        fw.dma("sp", big[bk][0:krows, :], ksrc, writes=[bbig[bk]])
        v4 = big[bv][:, :].rearrange("p (k c) -> p k c", c=128)
        fw.dma("sp", v4[:, :, 0:64], vsrc, writes=[bbig[bv]])

    load_kv(0, 1, mk[0], mv[0], 96)

    def branch(K, q_ap, b_q, kt_fn, b_k, v_fn, b_v, blocks, scale, acc, b_acc):
        n = len(blocks)
        scs = [None] * n

        def qk(i):
            w, bw = work.next()
            fw.op("pe", lambda e: e.matmul(w[:, :], lhsT=kt_fn(blocks[i][0]), rhs=q_ap,
                                            start=True, stop=True),
                  reads=[b_k, b_q], writes=[bw])
            scs[i] = (w, bw)

        for i in range(min(2, n)):
            qk(i)
        for i in range(n):
            w, bw = scs[i]
            p, bp = prot.next()
            m = blocks[i][1]
            if m is not None:
                t, bt = prerot.next()
                fw.op("dve", lambda e: e.tensor_tensor(out=t[:, :], in0=w[:, :], in1=cmask[:, m, :],
                                                       op=ALU.add),
                      reads=[bw, b_const], writes=[bt])
                fw.op("act", lambda e: e.activation(out=p[:, :], in_=t[:, :], func=AF.Exp, scale=scale),
                      reads=[bt], writes=[bp])
            else:
                fw.op("act", lambda e: e.activation(out=p[:, :], in_=w[:, :], func=AF.Exp, scale=scale),
                      reads=[bw], writes=[bp])
            if i + 2 < n:
                qk(i + 2)
            fw.op("pe", lambda e: e.matmul(acc[:, :], lhsT=v_fn(blocks[i][0]), rhs=p[:, :],
                                            start=(i == 0), stop=(i == n - 1)),
                  reads=[b_v, bp], writes=[b_acc])

    def recip_den(acc, b_acc):
        r, br = rrot.next()
        fw.op("dve", lambda e: e.tensor_scalar(out=r[:, :], in0=acc[64:128, :], scalar1=1e-30,
                                               scalar2=None, op0=ALU.max),
              reads=[b_acc], writes=[br])
        fw.op("dve", lambda e: e.reciprocal(out=r[:, :], in_=r[:, :]), reads=[br], writes=[br])
        return r, br

    def gated_accum(acc, b_acc, g_ap, b_g, dst, b_dst, first):
        r, br = recip_den(acc, b_acc)
        fw.op("dve", lambda e: e.tensor_tensor(out=r[:, :], in0=r[:, :], in1=g_ap, op=ALU.mult),
              reads=[br, b_g], writes=[br])
        if first:
            fw.op("dve", lambda e: e.tensor_tensor(out=dst[:, :], in0=acc[0:64, :], in1=r[:, :],
                                                   op=ALU.mult),
                  reads=[b_acc, br], writes=[b_dst])
        else:
            t, bt = trot.next()
            fw.op("dve", lambda e: e.tensor_tensor(out=t[:, :], in0=acc[0:64, :], in1=r[:, :],
                                                   op=ALU.mult),
                  reads=[b_acc, br], writes=[bt])
            fw.op("pool", lambda e: e.tensor_tensor(out=dst[:, :], in0=dst[:, :], in1=t[:, :],
                                                    op=ALU.add),
                  reads=[bt, b_dst], writes=[b_dst])

    def vaug(bi):
        v4 = big[bi][:, :].rearrange("p (k c) -> p k c", c=128)
        return lambda kb: v4[:, kb, :]

    def mla_head(hi, bk, bv):
        vf = vaug(bv)
        ql = {}

        def loadq(c):
            q, bq = qrot.next()
            fw.dma("sp", q[0:96, :], mq[hi, :, c * 512:(c + 1) * 512], writes=[bq])
            ql[c] = (q, bq)

        loadq(0)
        for qc in range(16):
            if qc + 1 < 16:
                loadq(qc + 1)
            q, bq = ql.pop(qc)
            blocks = [(kb, (kb - 4 * qc) if kb >= 4 * qc else None) for kb in range(4 * qc + 4)]
            acc, bacc = accr.next()
            branch(96, q[0:96, :], bq, lambda kb: big[bk][0:96, kb * 128:(kb + 1) * 128], bbig[bk],
                   vf, bbig[bv], blocks, MLA_SCALE, acc, bacc)
            r, br = recip_den(acc, bacc)
            o, bo = obrot.next()
            fw.op("dve", lambda e: e.tensor_tensor(out=o[:, :], in0=acc[0:64, :], in1=r[:, :],
                                                   op=ALU.mult),
                  reads=[bacc, br], writes=[bo])
            fw.dma("sp", out[hi, :, qc * 512:(qc + 1) * 512], o[:, :], reads=[bo], writes=[Buf()])

    def compress(which):
        fw.dma("sp", cin[0:64, :], nkc[which], writes=[b_cin])
        fw.dma("pool", w1s[:, :, :], w1[which].rearrange("(j d) m -> d j m", d=64), writes=[b_cw])
        fw.dma("pool", w2s[:, :], w2[which], writes=[b_cw])
        fw.dma("sp", b1s[:, :], b1[which], writes=[b_cw])
        fw.dma("pool", peTs[:, :], peT[which], writes=[b_cw])
        wb, bwb = work.next()
        for j in range(32):
            fw.op("pe", lambda e: e.matmul(wb[:, 0:1], lhsT=w1s[:, j, :], rhs=peTs[:, j:j + 1],
                                            start=(j == 0), stop=(j == 31)),
                  reads=[b_cw], writes=[bwb])
        fw.op("dve", lambda e: e.tensor_tensor(out=cbias[:, :], in0=wb[:, 0:1], in1=b1s[:, :], op=ALU.add),
              reads=[bwb, b_cw], writes=[b_cbias])
        c3 = cin[0:64, :].rearrange("p (n s) -> p n s", s=16)
        wh, bwh = work.next()
        for j in range(32):
            rhs = c3[:, 0:511, j] if j < 16 else c3[:, 1:512, j - 16]
            fw.op("pe", lambda e: e.matmul(wh[:, 0:511], lhsT=w1s[:, j, :], rhs=rhs,
                                            start=(j == 0), stop=(j == 31)),
                  reads=[b_cw, b_cin], writes=[bwh])
        fw.op("act", lambda e: e.activation(out=zf[:, 0:511], in_=wh[:, 0:511], func=AF.Identity,
                                            bias=cbias[:, 0:1], scale=1.0),
              reads=[bwh, b_cbias], writes=[b_zf])
        fw.op("dve", lambda e: e.tensor_tensor(out=zt[:, 0:511], in0=zf[:, 0:511], in1=zf[:, 0:511],
                                               op=ALU.mult), reads=[b_zf], writes=[b_zt])
        fw.op("dve", lambda e: e.tensor_scalar(out=zt[:, 0:511], in0=zt[:, 0:511], scalar1=0.044715,
                                               scalar2=1.0, op0=ALU.mult, op1=ALU.add),
              reads=[b_zt], writes=[b_zt])
        fw.op("dve", lambda e: e.tensor_tensor(out=zt[:, 0:511], in0=zt[:, 0:511], in1=zf[:, 0:511],
                                               op=ALU.mult), reads=[b_zt, b_zf], writes=[b_zt])
        fw.op("act", lambda e: e.activation(out=zs[:, 0:511], in_=zt[:, 0:511], func=AF.Sigmoid,
                                            scale=1.5957691216057308),
              reads=[b_zt], writes=[b_zs])
        fw.op("pool", lambda e: e.memset(hid[:, :], 0.0), writes=[b_hid])
        fw.op("dve", lambda e: e.tensor_tensor(out=hid[:, 0:511], in0=zf[:, 0:511], in1=zs[:, 0:511],
                                               op=ALU.mult), reads=[b_zf, b_zs], writes=[b_hid])
        if which == 0:
            fw.op("pool", lambda e: e.memset(kcaug[:, :], 0.0), writes=[b_kc])
            wk, bwk = work.next()
            fw.op("pe", lambda e: e.matmul(wk[0:64, 0:511], lhsT=w2s[:, :], rhs=hid[:, 0:511],
                                            start=True, stop=True), reads=[b_cw, b_hid], writes=[bwk])
            fw.op("dve", lambda e: e.tensor_copy(kcaug[0:64, 0:511], wk[0:64, 0:511]),
                  reads=[bwk], writes=[b_kc])
            fw.dma("sp", kcaug[64:68, :], c_kaugc, writes=[b_kc])
        else:
            fw.op("pool", lambda e: e.memset(vcaug[:, :, 0:64], 0.0), writes=[b_vc])
            fw.op("pool", lambda e: e.memset(vcaug[:, :, 64:128], 1.0), writes=[b_vc])
            for c in range(4):
                rows = 128 if c < 3 else 127
                wv, bwv = work.next()
                fw.op("pe", lambda e: e.matmul(wv[0:rows, 0:64], lhsT=hid[:, c * 128:c * 128 + rows],
                                                rhs=w2s[:, :], start=True, stop=True),
                      reads=[b_cw, b_hid], writes=[bwv])
                fw.op("dve", lambda e: e.tensor_copy(vcaug[0:rows, c, 0:64], wv[0:rows, 0:64]),
                      reads=[bwv], writes=[b_vc])

    def cmp_blocks(qc):
        return [(c, (8 + qc - 4 * c) if (qc - 4 * c) <= 4 else None) for c in range(qc // 4 + 1)]

    def select_phase():
        for qc in range(16):
            blocks = cmp_blocks(qc)
            for r in range(4):
                q, bq = qrot.next()
                fw.dma("sp", q[0:68, :], nq[r, :, qc * 512:(qc + 1) * 512], writes=[bq])
                for (c, m) in blocks:
                    w, bw = work.next()
                    fw.op("pe", lambda e: e.matmul(w[:, :], lhsT=kcaug[:, c * 128:(c + 1) * 128],
                                                    rhs=q[0:68, :], start=True, stop=True),
                          reads=[b_kc, bq], writes=[bw])
                    pi = r * 4 + c
                    if m is not None:
                        t, bt = prerot.next()
                        fw.op("dve", lambda e: e.tensor_tensor(out=t[:, :], in0=w[:, :], in1=cmask[:, m, :],
                                                               op=ALU.add),
                              reads=[bw, b_const], writes=[bt])
                        fw.op("act", lambda e: e.activation(out=pc[:, pi, :], in_=t[:, :], func=AF.Exp,
                                                            scale=0.125),
                              reads=[bt], writes=[b_pc[pi]])
                    else:
                        fw.op("act", lambda e: e.activation(out=pc[:, pi, :], in_=w[:, :], func=AF.Exp,
                                                            scale=0.125),
                              reads=[bw], writes=[b_pc[pi]])
            for tb in range(4):
                gi = qc * 4 + tb
                for r in range(4):
                    w, bw = work.next()
                    nb = len(blocks)
                    for bi, (c, m) in enumerate(blocks):
                        pi = r * 4 + c
                        fw.op("pe", lambda e: e.matmul(w[:, 0:129], lhsT=pc[:, pi, tb * 128:(tb + 1) * 128],
                                                        rhs=ov[:, c, :], start=(bi == 0), stop=(bi == nb - 1)),
                              reads=[b_pc[pi], b_const], writes=[bw])
                    fw.op("dve", lambda e: e.tensor_scalar(out=rdc[:, :], in0=w[:, 128:129], scalar1=1e-30,
                                                           scalar2=None, op0=ALU.max),
                          reads=[bw], writes=[b_rdc])
                    fw.op("dve", lambda e: e.reciprocal(out=rdc[:, :], in_=rdc[:, :]),
                          reads=[b_rdc], writes=[b_rdc])
                    if r == 0:
                        fw.op("dve", lambda e: e.tensor_scalar(out=imp[:, :], in0=w[:, 0:128],
                                                               scalar1=rdc[:, 0:1], scalar2=None,
                                                               op0=ALU.mult),
                              reads=[bw, b_rdc], writes=[b_imp])
                    else:
                        fw.op("dve", lambda e: e.scalar_tensor_tensor(out=imp[:, :], in0=w[:, 0:128],
                                                                      scalar=rdc[:, 0:1], in1=imp[:, :],
                                                                      op0=ALU.mult, op1=ALU.add),
                              reads=[bw, b_rdc, b_imp], writes=[b_imp])
                st = 127 - 2 * gi
                vsl = wide[:, st:st + 128]
                csl = wide[:, 256 + st:256 + st + 128]
                fw.op("dve", lambda e: e.tensor_tensor(out=sc_t[:, :], in0=imp[:, :], in1=vsl, op=ALU.mult),
                      reads=[b_imp, b_const], writes=[b_sct])
                fw.op("dve", lambda e: e.tensor_tensor(out=sc_t[:, :], in0=sc_t[:, :], in1=csl, op=ALU.add),
                      reads=[b_sct, b_const], writes=[b_sct])
                fw.op("dve", lambda e: e.tensor_tensor(out=sc_t[:, :], in0=sc_t[:, :], in1=wide[:, 512:640],
                                                       op=ALU.add),
                      reads=[b_sct, b_const], writes=[b_sct])
                fw.op("dve", lambda e: e.max(out=m8a[:, :], in_=sc_t[:, :]), reads=[b_sct], writes=[b_m8a])
                fw.op("dve", lambda e: e.match_replace(out=sc_u[:, :], in_to_replace=m8a[:, :],
                                                       in_values=sc_t[:, :], imm_value=-2.0),
                      reads=[b_sct, b_m8a], writes=[b_scu])
                fw.op("dve", lambda e: e.max(out=m8b[:, :], in_=sc_u[:, :]), reads=[b_scu], writes=[b_m8b])
                fw.op("dve", lambda e: e.scalar_tensor_tensor(out=selb[:, :], in0=sc_t[:, :],
                                                              scalar=m8b[:, 7:8], in1=vsl,
                                                              op0=ALU.is_ge, op1=ALU.mult),
                      reads=[b_sct, b_m8b, b_const], writes=[b_selb])
                w, bw = work.next()
                fw.op("pe", lambda e: e.matmul(w[:, 0:128], lhsT=selb[:, :], rhs=ident[:, :],
                                                start=True, stop=True),
                      reads=[b_selb, b_const], writes=[bw])
                fw.op("act", lambda e: e.copy(out=selT[:, gi * 128:(gi + 1) * 128], in_=w[:, 0:128]),
                      reads=[bw], writes=[b_selT])

    def nsa_phase():
        vfs = vaug(1)
        vfw = vaug(3)
        for qc in range(16):
            g, bg = grot.next()
            fw.dma("sp", g[:, :, :], bass.AP(gates_h, qc * 512, [[0, 64], [S, 6], [1, 512]]), writes=[bg])
            qs = []
            for h in range(2):
                q, bq = qrot.next()
                fw.dma("sp", q[0:68, :], nq[h, :, qc * 512:(qc + 1) * 512], writes=[bq])
                qs.append((q, bq))
            dst = [oarot.next() for _ in range(2)]
            for h in range(2):
                acc, bacc = accr.next()
                branch(68, qs[h][0][0:68, :], qs[h][1], lambda c: kcaug[:, c * 128:(c + 1) * 128], b_kc,
                       lambda c: vcaug[:, c, :], b_vc, cmp_blocks(qc), 0.125, acc, bacc)
                gated_accum(acc, bacc, g[:, 3 * h + 0, :], bg, dst[h][0], dst[h][1], True)
            nkb = 4 * qc + 4
            accs = [accr.next() for _ in range(2)]
            pend = {}

            def issue_front(kb):
                se, bse = work.next()
                fw.op("pe", lambda e: e.matmul(se[:, :], lhsT=ew[:, kb * 128:(kb + 1) * 128],
                                                rhs=selT[:, qc * 512:(qc + 1) * 512], start=True, stop=True),
                      reads=[b_const, b_selT], writes=[bse])
                scs = []
                for h in range(2):
                    w, bw = work.next()
                    fw.op("pe", lambda e: e.matmul(w[:, :], lhsT=big[0][0:68, kb * 128:(kb + 1) * 128],
                                                    rhs=qs[h][0][0:68, :], start=True, stop=True),
                          reads=[bbig[0], qs[h][1]], writes=[bw])
                    scs.append((w, bw))
                pend[kb] = (se, bse, scs)

            issue_front(0)
            for kb in range(nkb):
                se, bse, scs = pend.pop(kb)
                mx, bmx = mrot.next()
                fw.op("dve", lambda e: e.tensor_copy(mx[:, :], se[:, :]), reads=[bse], writes=[bmx])
                ps = []
                for h in range(2):
                    p, bp = prot.next()
                    if kb >= 4 * qc:
                        t, bt = prerot.next()
                        fw.op("dve", lambda e: e.tensor_tensor(out=t[:, :], in0=scs[h][0][:, :],
                                                               in1=cmask[:, kb - 4 * qc, :], op=ALU.add),
                              reads=[scs[h][1], b_const], writes=[bt])
                        fw.op("act", lambda e: e.activation(out=p[:, :], in_=t[:, :], func=AF.Exp,
                                                            scale=0.125),
                              reads=[bt], writes=[bp])
                    else:
                        fw.op("act", lambda e: e.activation(out=p[:, :], in_=scs[h][0][:, :], func=AF.Exp,
                                                            scale=0.125),
                              reads=[scs[h][1]], writes=[bp])
                    fw.op("dve", lambda e: e.tensor_tensor(out=p[:, :], in0=p[:, :], in1=mx[:, :],
                                                           op=ALU.mult),
                          reads=[bp, bmx], writes=[bp])
                    ps.append((p, bp))
                if kb + 1 < nkb:
                    issue_front(kb + 1)
                for h in range(2):
                    fw.op("pe", lambda e: e.matmul(accs[h][0][:, :], lhsT=vfs(kb), rhs=ps[h][0][:, :],
                                                    start=(kb == 0), stop=(kb == nkb - 1)),
                          reads=[bbig[1], ps[h][1]], writes=[accs[h][1]])
            for h in range(2):
                gated_accum(accs[h][0], accs[h][1], g[:, 3 * h + 1, :], bg, dst[h][0], dst[h][1], False)
            wblocks = []
            for kb in range(max(0, 4 * qc - 4), 4 * qc + 4):
                o = 128 * kb - 512 * qc
                wblocks.append((kb, (o // 128) if o >= 0 else 4 + (o + 512) // 128))
            for h in range(2):
                acc, bacc = accr.next()
                branch(68, qs[h][0][0:68, :], qs[h][1], lambda kb: big[2][0:68, kb * 128:(kb + 1) * 128],
                       bbig[2], vfw, bbig[3], wblocks, 0.125, acc, bacc)
                gated_accum(acc, bacc, g[:, 3 * h + 2, :], bg, dst[h][0], dst[h][1], False)
                o, bo = obrot.next()
                fw.op("act", lambda e: e.copy(out=o[:, :], in_=dst[h][0][:, :]), reads=[dst[h][1]], writes=[bo])
                fw.dma("sp", out[2 + h, :, qc * 512:(qc + 1) * 512], o[:, :], reads=[bo], writes=[Buf()])

    compress(0)
    compress(1)
    load_kv(2, 3, mk[1], mv[1], 96)
    mla_head(0, 0, 1)
    select_phase()
    load_kv(0, 1, nks, nvs, 68)
    mla_head(1, 2, 3)
    load_kv(2, 3, nkw, nvw, 68)
    nsa_phase()
    fw.finish()
    return nc


_CB = None


def _prep_B(ao, l, P):
    global _CB
    if _CB is None:
        _CB = _consts_B()
        _CB["qa"], _CB["ka"] = _aug_tables()
    C = _CB
    qa, ka = C["qa"], C["ka"]
    maps = []
    pe = np.stack([P["nsa_pe_k"][l].T, P["nsa_pe_v"][l].T]).astype(np.float32)
    w1 = np.stack([P["nsa_w1_k"][l], P["nsa_w1_v"][l]]).astype(np.float32)
    b1 = np.stack([P["nsa_b1_k"][l], P["nsa_b1_v"][l]]).astype(np.float32)[:, :, None]
    w2 = np.stack([P["nsa_w2_k"][l], P["nsa_w2_v"][l]]).astype(np.float32)
    for b in range(2):
        cs = [ao[4 * b + i] for i in range(4)]
        cat = lambda key, axis: np.concatenate([c[key] for c in cs], axis=axis)
        qm = cat("qm", 2)
        kn = cat("kn", 2)
        kr = cat("kr", 1)
        vm = cat("vm", 0)
        nq = cat("nq", 2)
        nkf = cat("nkf", 3)
        nvt = cat("nvt", 0)
        gl = cat("gl", 1)
        tokmaj = lambda a: np.ascontiguousarray(a.reshape(64, 128, 64).transpose(1, 0, 2))
        for j in range(4):
            heads = [2 * j, 2 * j + 1]
            g = j // 2
            r0 = 2 * (j % 2)
            order = [r0, r0 + 1] + [r for r in range(4) if r not in (r0, r0 + 1)]
            m = dict(
                mq=np.ascontiguousarray(qm[heads]),
                mk=np.stack([np.concatenate([kn[h], kr], axis=0) for h in heads]),
                mv=np.stack([tokmaj(vm[:, h * 64:(h + 1) * 64]) for h in heads]),
                nq=np.stack([np.concatenate([nq[4 * g + r], qa[4 * g + r]], axis=0) for r in order]),
                nkc=np.stack([nkf[0, g], nkf[1, g]]),
                nks=np.concatenate([nkf[2, g], ka], axis=0),
                nkw=np.concatenate([nkf[3, g], ka], axis=0),
                nvs=tokmaj(nvt[:, 0, g]),
                nvw=tokmaj(nvt[:, 1, g]),
                gates=np.stack([gl[(4 * g + r) * 3 + br] for r in (r0, r0 + 1) for br in range(3)]).astype(np.float32),
                peT=pe, w1=w1, b1=b1, w2=w2,
                cmask_c=C["cmask"], ov_c=C["ov"], wide_c=C["wide"], ew_c=C["ew"], ident_c=C["ident"], kaugc_c=C["kaugc"],
            )
            maps.append({k: np.ascontiguousarray(v) for k, v in m.items()})
    return maps


def _gather_B(res):
    outs = []
    for b in range(2):
        mixT = np.zeros((1024, S), NPBF)
        for j in range(4):
            o = res[4 * b + j]["o"]
            g = j // 2
            r0 = 2 * (j % 2)
            for i, h in enumerate((2 * j, 2 * j + 1)):
                mixT[h * 64:(h + 1) * 64] = o[i]
            for i, r in enumerate((r0, r0 + 1)):
                hh = 4 * g + r
                mixT[512 + hh * 64:512 + (hh + 1) * 64] = o[2 + i]
        for i in range(4):
            outs.append(np.ascontiguousarray(mixT[:, i * 2048:(i + 1) * 2048]))
    return outs


IN_OFF = dict(cq=0, ckv=384, kr=640, nq=672, kc=1184, vc=1312, ks=1440, vs=1568, kw=1696, vw=1824, gl=1952)
TT = 2048


class _Stop(Exception):
    pass


def build_T(do_ln_in, do_C, do_A, dbg=None):
    nc = bass.Bass("TRN2", target_bir_lowering=False)
    fw = FW(nc)
    HALO = 2 if do_C else 0
    TW = TT + HALO

    def din(name, shape, dt):
        return nc.dram_tensor(name, shape, dt, kind="ExternalInput").ap()

    def dout(name, shape, dt):
        return nc.dram_tensor(name, shape, dt, kind="ExternalOutput").ap()

    def sb(name, shape, dt):
        return nc.alloc_sbuf_tensor(name, shape, dt)

    xT_d = din("xT", [D, TW], F32)
    vec_d = din("vec", [128, 264], F32)
    if do_C:
        oT_d = din("oT", [D, TW], BF16)
        flag_d = din("flag", [128, 1], F32)
        memT_d = din("memT", [D, 256], F32)
        w_out_d = din("w_out", [D, D], F32)
        wq_d = din("xa_wq", [D, D], F32)
        wkv_d = din("xa_wkv", [D, 2 * D], F32)
        wo_d = din("xa_wo", [D, D], F32)
        wup_d = din("w_up", [D, 5632], F32)
        wdn_d = din("w_down", [2816, D], F32)
    if do_A:
        w_in_d = din("w_in", [D, 1976], F32)
        wuq_d = din("w_uq", [384, 768], F32)
        wukv_d = din("w_ukv", [256, 1024], F32)
        ropeC_d = din("ropeC", [32, TT], F32)
        ropeS_d = din("ropeS", [32, TT], F32)
        qm_d = dout("qm", [8, 96, TT], BF16)
        kn_d = dout("kn", [8, 64, TT], BF16)
        kr_d = dout("kr", [32, TT], BF16)
        vm_d = dout("vm", [TT, 512], BF16)
        nq_d = dout("nq", [512, TT], BF16)
        nkf_d = dout("nkf", [4, 128, TT], BF16)
        nvt_d = dout("nvt", [TT, 256], BF16)
        gl_d = dout("gl", [24, TT], F32)
    xo_d = dout("xo", [D, TT], F32)

    V_LNIN, V_LN1, V_LN2, V_LN3, V_LNM, V_CONV, V_CQG, V_CKVG = 0, 16, 32, 48, 64, 80, 256, 259

    vec = sb("vec_s", [128, 264], F32)
    b_vec = Buf()
    fw.dma("sp", vec[:, :], vec_d, writes=[b_vec])
    ones_b = sb("ones_b", [128, 128], BF16)
    b_ones = Buf()
    fw.op("pool", lambda e: e.memset(ones_b[:, :], 1.0), writes=[b_ones])

    banks = [nc.alloc_psum_tensor("bank%d" % i, [128, 512], F32) for i in range(8)]
    bbank = [Buf(True) for _ in range(8)]
    psr = Rot([(banks[i], bbank[i]) for i in range(8)])

    xr = sb("xr", [128, 8, 512], F32)
    bxr = [Buf() for _ in range(8)]
    xb = sb("xb", [128, 8, 512], BF16)
    bxb = [Buf() for _ in range(8)]
    lnt = Rot([(sb("lnt%d" % i, [128, 512], BF16), Buf()) for i in range(4)])
    st_mean = sb("st_mean", [128, 512], F32)
    st_m2 = sb("st_m2", [128, 512], F32)
    st_rstd = sb("st_rstd", [128, 512], F32)
    st_nmr = sb("st_nmr", [128, 512], F32)
    b_mean, b_m2, b_rstd, b_nmr = Buf(), Buf(), Buf(), Buf()
    wst = Rot([(sb("wst%d" % i, [128, 8, 512], BF16), Buf()) for i in range(2)])

    def layer_norm(n, gi, src=xr, bsrc=bxr, dstb=xb, bdstb=bxb, nch=8, nfeat=1024.0, eps=1e-5):
        bs, bbs = psr.next()
        bq, bbq = psr.next()
        for c in range(nch):
            tb, btb = lnt.next()
            fw.op("act", lambda e: e.copy(out=tb[:, :n], in_=src[:, c, :n]), reads=[bsrc[c]], writes=[btb])
            fw.op("pe", lambda e: e.matmul(bs[:, :n], lhsT=ones_b[:, :], rhs=tb[:, :n], start=(c == 0),
                                            stop=(c == nch - 1)), reads=[btb, b_ones], writes=[bbs])
            tq, btq = lnt.next()
            fw.op("act", lambda e: e.activation(out=tq[:, :n], in_=src[:, c, :n], func=AF.Square),
                  reads=[bsrc[c]], writes=[btq])
            fw.op("pe", lambda e: e.matmul(bq[:, :n], lhsT=ones_b[:, :], rhs=tq[:, :n], start=(c == 0),
                                            stop=(c == nch - 1)), reads=[btq, b_ones], writes=[bbq])
        fw.op("dve", lambda e: e.tensor_scalar(out=st_mean[:, :n], in0=bs[:, :n], scalar1=1.0 / nfeat,
                                               scalar2=None, op0=ALU.mult), reads=[bbs], writes=[b_mean])
        fw.op("dve", lambda e: e.tensor_tensor(out=st_m2[:, :n], in0=st_mean[:, :n], in1=st_mean[:, :n],
                                               op=ALU.mult), reads=[b_mean], writes=[b_m2])
        fw.op("dve", lambda e: e.scalar_tensor_tensor(out=st_m2[:, :n], in0=bq[:, :n], scalar=1.0 / nfeat,
                                                      in1=st_m2[:, :n], op0=ALU.mult, op1=ALU.subtract),
              reads=[bbq, b_m2], writes=[b_m2])
        fw.op("dve", lambda e: e.tensor_scalar(out=st_m2[:, :n], in0=st_m2[:, :n], scalar1=eps,
                                               scalar2=None, op0=ALU.add), reads=[b_m2], writes=[b_m2])
        fw.op("act", lambda e: e.activation(out=st_rstd[:, :n], in_=st_m2[:, :n], func=AF.Sqrt),
              reads=[b_m2], writes=[b_rstd])
        fw.op("dve", lambda e: e.reciprocal(out=st_rstd[:, :n], in_=st_rstd[:, :n]), reads=[b_rstd],
              writes=[b_rstd])
        fw.op("dve", lambda e: e.scalar_tensor_tensor(out=st_nmr[:, :n], in0=st_mean[:, :n], scalar=-1.0,
                                                      in1=st_rstd[:, :n], op0=ALU.mult, op1=ALU.mult),
              reads=[b_mean, b_rstd], writes=[b_nmr])
        for c in range(nch):
            fw.op("dve", lambda e: e.tensor_tensor(out=src[:, c, :n], in0=src[:, c, :n], in1=st_rstd[:, :n],
                                                   op=ALU.mult), reads=[bsrc[c], b_rstd], writes=[bsrc[c]])
            fw.op("pool", lambda e: e.tensor_tensor(out=src[:, c, :n], in0=src[:, c, :n], in1=st_nmr[:, :n],
                                                    op=ALU.add), reads=[bsrc[c], b_nmr], writes=[bsrc[c]])
            fw.op("act", lambda e: e.activation(out=src[:, c, :n], in_=src[:, c, :n], func=AF.Identity,
                                                bias=vec[:, gi + 8 + c:gi + 9 + c], scale=vec[:, gi + c:gi + c + 1]),
                  reads=[bsrc[c], b_vec], writes=[bsrc[c]])
            fw.op("pool", lambda e: e.tensor_copy(dstb[:, c, :n], src[:, c, :n]), reads=[bsrc[c]],
                  writes=[bdstb[c]])

    def proj_resid(n, w_s, b_w, rhs, brhs):
        for oc in range(8):
            bk, bbk = psr.next()
            for kc in range(8):
                fw.op("pe", lambda e: e.matmul(bk[:, :n], lhsT=w_s[:, kc, oc * 128:(oc + 1) * 128],
                                                rhs=rhs[:, kc, :n], start=(kc == 0), stop=(kc == 7)),
                      reads=[b_w, brhs[kc]], writes=[bbk])
            fw.op("dve", lambda e: e.scalar_tensor_tensor(out=xr[:, oc, :n], in0=xr[:, oc, :n], scalar=ALPHA,
                                                          in1=bk[:, :n], op0=ALU.mult, op1=ALU.add),
                  reads=[bxr[oc], bbk], writes=[bxr[oc]])

    def load_w(dst3, src2, bdst, q="pool"):
        fw.dma(q, dst3, src2.rearrange("(kc p) n -> p kc n", p=128), writes=[bdst])

    if do_C:
        wout_s = sb("wout_s", [128, 8, 1024], BF16)
        wq_s = sb("wq_s", [128, 8, 1024], BF16)
        wo_s = sb("wo_s", [128, 8, 1024], BF16)
        b_wout, b_wq, b_wo = Buf(), Buf(), Buf()
        load_w(wout_s[:, :, :], w_out_d, b_wout)
        load_w(wq_s[:, :, :], wq_d, b_wq)
        load_w(wo_s[:, :, :], wo_d, b_wo)
        flag = sb("flag_s", [128, 1], F32)
        b_flag = Buf()
        fw.dma("sp", flag[:, :], flag_d, writes=[b_flag])
        hb = sb("hb", [128, 22, 512], BF16)
        bhb = [Buf() for _ in range(22)]
        ot = hb[:, 0:8, :]
        bot = bhb[0:8]
        qb = hb[:, 8:16, :]
        bqb = bhb[8:16]
        oab = hb[:, 0:8, :]
        boab = bhb[0:8]
        kT = sb("kT", [128, 8, 256], BF16)
        b_kT = Buf()
        vmem = sb("vmem", [128, 2, 1024], BF16)
        b_vmem = Buf()
        carry = sb("carry", [128, 44, 2], F32)
        bcar = [Buf() for _ in range(44)]
        ptr = Rot([(sb("xpt%d" % i, [128, 512], BF16), Buf()) for i in range(4)])
        rdx = Rot([(sb("rdx%d" % i, [128, 512], F32), Buf()) for i in range(2)])
        uer = Rot([(sb("ue%d" % i, [128, 516], F32), Buf()) for i in range(2)])
        yr_items = [(sb("y%d" % i, [128, 512], F32), Buf()) for i in range(3)]
        yr = Rot(yr_items)
        sgr = Rot([(sb("sg%d" % i, [128, 512], F32), Buf()) for i in range(2)])
        wdn = Rot([(sb("wdn%d" % i, [128, 2, 1024], BF16), Buf()) for i in range(2)])

        mr = xr[:, :, 0:256]
        bmr = bxr
        mb = xb[:, :, 0:256]
        bmb = bxb
        fw.dma("sp", mr, memT_d.rearrange("(c p) t -> p c t", p=128), writes=bmr)
        layer_norm(256, V_LNM, src=mr, bsrc=bmr, dstb=mb, bdstb=bmb)
        for piece in range(4):
            w, bw = wst.next()
            load_w(w[:, :, :], wkv_d[:, piece * 512:(piece + 1) * 512], bw)
            if piece < 2:
                for o4 in range(4):
                    oc = piece * 4 + o4
                    bk, bbk = psr.next()
                    for kc in range(8):
                        fw.op("pe", lambda e: e.matmul(bk[:, :256], lhsT=w[:, kc, o4 * 128:(o4 + 1) * 128],
                                                        rhs=mb[:, kc, :], start=(kc == 0), stop=(kc == 7)),
                              reads=[bw, bmb[kc]], writes=[bbk])
                    fw.op("act", lambda e: e.copy(out=kT[:, oc, :], in_=bk[:, :256]), reads=[bbk], writes=[b_kT])
            else:
                half = piece - 2
                for mc in range(2):
                    bk, bbk = psr.next()
                    for kc in range(8):
                        fw.op("pe", lambda e: e.matmul(bk[:, :], lhsT=mb[:, kc, mc * 128:(mc + 1) * 128],
                                                        rhs=w[:, kc, :], start=(kc == 0), stop=(kc == 7)),
                              reads=[bw, bmb[kc]], writes=[bbk])
                    fw.op("act", lambda e: e.copy(out=vmem[:, mc, half * 512:(half + 1) * 512], in_=bk[:, :]),
                          reads=[bbk], writes=[b_vmem])

        def conv(bk, bbk, fc, n):
            ue, bue = uer.next()
            fw.op("pool", lambda e: e.tensor_copy(ue[:, 0:2], carry[:, fc, :]), reads=[bcar[fc]], writes=[bue])
            fw.op("dve", lambda e: e.tensor_copy(ue[:, 2:2 + n], bk[:, :n]), reads=[bbk], writes=[bue])
            y, by = yr.next()
            cv = V_CONV + fc
            fw.op("act", lambda e: e.activation(out=y[:, :n], in_=ue[:, 2:2 + n], func=AF.Identity,
                                                bias=vec[:, cv + 132:cv + 133], scale=vec[:, cv + 88:cv + 89]),
                  reads=[bue, b_vec], writes=[by])
            fw.op("dve", lambda e: e.scalar_tensor_tensor(out=y[:, :n], in0=ue[:, 1:1 + n],
                                                          scalar=vec[:, cv + 44:cv + 45], in1=y[:, :n],
                                                          op0=ALU.mult, op1=ALU.add),
                  reads=[bue, by, b_vec], writes=[by])
            fw.op("dve", lambda e: e.scalar_tensor_tensor(out=y[:, :n], in0=ue[:, 0:n],
                                                          scalar=vec[:, cv:cv + 1], in1=y[:, :n],
                                                          op0=ALU.mult, op1=ALU.add),
                  reads=[bue, by, b_vec], writes=[by])
            fw.op("pool", lambda e: e.tensor_copy(carry[:, fc, :], ue[:, n:n + 2]), reads=[bue], writes=[bcar[fc]])
            return y, by

        cur_halo = [True]

        def ck(k):
            if dbg == k + (0 if cur_halo[0] else 10):
                raise _Stop()

        def c_tile(n, col0, is_halo):
            cur_halo[0] = is_halo
            ck(1)
            fw.dma("sp", ot[:, :, :n], oT_d[:, col0:col0 + n].rearrange("(c p) t -> p c t", p=128), writes=bot)
            fw.dma("sp", xr[:, :, :n], xT_d[:, col0:col0 + n].rearrange("(c p) t -> p c t", p=128), writes=bxr)
            ck(2)
            proj_resid(n, wout_s, b_wout, ot, bot)
            ck(3)
            layer_norm(n, V_LN1)
            ck(4)
            for oc in range(8):
                bk, bbk = psr.next()
                for kc in range(8):
                    fw.op("pe", lambda e: e.matmul(bk[:, :n], lhsT=wq_s[:, kc, oc * 128:(oc + 1) * 128],
                                                    rhs=xb[:, kc, :n], start=(kc == 0), stop=(kc == 7)),
                          reads=[b_wq, bxb[kc]], writes=[bbk])
                fw.op("act", lambda e: e.copy(out=qb[:, oc, :n], in_=bk[:, :n]), reads=[bbk], writes=[bqb[oc]])
            for h in range(4):
                pts_ = []
                for mc in range(2):
                    bk, bbk = psr.next()
                    for half in range(2):
                        fw.op("pe", lambda e: e.matmul(bk[:, :n], lhsT=kT[:, 2 * h + half, mc * 128:(mc + 1) * 128],
                                                        rhs=qb[:, 2 * h + half, :n], start=(half == 0), stop=(half == 1)),
                              reads=[b_kT, bqb[2 * h + half]], writes=[bbk])
                    p, bp = ptr.next()
                    fw.op("act", lambda e: e.activation(out=p[:, :n], in_=bk[:, :n], func=AF.Exp, scale=1.0 / 16.0),
                          reads=[bbk], writes=[bp])
                    pts_.append((p, bp))
                bd, bbd = psr.next()
                for mc in range(2):
                    fw.op("pe", lambda e: e.matmul(bd[:, :n], lhsT=ones_b[:, :], rhs=pts_[mc][0][:, :n],
                                                    start=(mc == 0), stop=(mc == 1)),
                          reads=[b_ones, pts_[mc][1]], writes=[bbd])
                r, br = rdx.next()
                fw.op("dve", lambda e: e.reciprocal(out=r[:, :n], in_=bd[:, :n]), reads=[bbd], writes=[br])
                for dvh in range(2):
                    bk, bbk = psr.next()
                    for mc in range(2):
                        fw.op("pe", lambda e: e.matmul(bk[:, :n], lhsT=vmem[:, mc, h * 256 + dvh * 128:h * 256 + dvh * 128 + 128],
                                                        rhs=pts_[mc][0][:, :n], start=(mc == 0), stop=(mc == 1)),
                              reads=[b_vmem, pts_[mc][1]], writes=[bbk])
                    fw.op("dve", lambda e: e.tensor_tensor(out=oab[:, 2 * h + dvh, :n], in0=bk[:, :n], in1=r[:, :n],
                                                           op=ALU.mult),
                          reads=[bbk, br], writes=[boab[2 * h + dvh]])
            ck(5)
            proj_resid(n, wo_s, b_wo, oab, boab)
            layer_norm(n, V_LN2)
            ck(6)
            for grp in range(11):
                w, bw = wst.next()
                fw.dma("pool", w[:, :, 0:256], wup_d[:, grp * 256:(grp + 1) * 256].rearrange("(kc p) n -> p kc n", p=128),
                       writes=[bw])
                fw.dma("pool", w[:, :, 256:512],
                       wup_d[:, 2816 + grp * 256:2816 + (grp + 1) * 256].rearrange("(kc p) n -> p kc n", p=128),
                       writes=[bw])
                for jj in range(2):
                    j = 2 * grp + jj
                    ba, bba = psr.next()
                    for kc in range(8):
                        fw.op("pe", lambda e: e.matmul(ba[:, :n], lhsT=w[:, kc, jj * 128:(jj + 1) * 128],
                                                        rhs=xb[:, kc, :n], start=(kc == 0), stop=(kc == 7)),
                              reads=[bw, bxb[kc]], writes=[bba])
                    bg_, bbg = psr.next()
                    for kc in range(8):
                        fw.op("pe", lambda e: e.matmul(bg_[:, :n], lhsT=w[:, kc, 256 + jj * 128:256 + (jj + 1) * 128],
                                                        rhs=xb[:, kc, :n], start=(kc == 0), stop=(kc == 7)),
                              reads=[bw, bxb[kc]], writes=[bbg])
                    if is_halo:
                        fw.op("dve", lambda e: e.tensor_scalar(out=carry[:, j, :], in0=ba[:, 0:2], scalar1=flag[:, 0:1],
                                                               scalar2=None, op0=ALU.mult),
                              reads=[bba, b_flag], writes=[bcar[j]])
                        fw.op("dve", lambda e: e.tensor_scalar(out=carry[:, 22 + j, :], in0=bg_[:, 0:2],
                                                               scalar1=flag[:, 0:1], scalar2=None, op0=ALU.mult),
                              reads=[bbg, b_flag], writes=[bcar[22 + j]])
                    else:
                        ya, bya = conv(ba, bba, j, n)
                        yg, byg = conv(bg_, bbg, 22 + j, n)
                        sg, bsg = sgr.next()
                        fw.op("act", lambda e: e.activation(out=sg[:, :n], in_=yg[:, :n], func=AF.Silu),
                              reads=[byg], writes=[bsg])
                        fw.op("pool", lambda e: e.tensor_tensor(out=hb[:, j, :n], in0=ya[:, :n], in1=sg[:, :n],
                                                                op=ALU.mult),
                              reads=[bya, bsg], writes=[bhb[j]])
            if is_halo:
                ck(7)
                return
            ck(8)
            for grp in range(11):
                w, bw = wdn.next()
                fw.dma("pool", w[:, :, :], wdn_d[grp * 256:(grp + 1) * 256, :].rearrange("(jj p) n -> p jj n", p=128),
                       writes=[bw])
                for jj in range(2):
                    j = 2 * grp + jj
                    for oc in range(8):
                        fw.op("pe", lambda e: e.matmul(banks[oc][:, :n], lhsT=w[:, jj, oc * 128:(oc + 1) * 128],
                                                        rhs=hb[:, j, :n], start=(j == 0), stop=(j == 21)),
                              reads=[bw, bhb[j]], writes=[bbank[oc]])
            for oc in range(8):
                fw.op("dve", lambda e: e.scalar_tensor_tensor(out=xr[:, oc, :n], in0=xr[:, oc, :n], scalar=ALPHA,
                                                              in1=banks[oc][:, :n], op0=ALU.mult, op1=ALU.add),
                      reads=[bxr[oc], bbank[oc]], writes=[bxr[oc]])
            layer_norm(n, V_LN3)

    if do_A:
        wuq_s = sb("wuq_s", [128, 3, 768], BF16)
        wuqr_s = sb("wuqr_s", [128, 3, 768], BF16)
        wukv_s = sb("wukv_s", [128, 2, 1024], BF16)
        b_wuq, b_wuqr, b_wukv = Buf(), Buf(), Buf()
        load_w(wuq_s[:, :, :], wuq_d, b_wuq)
        load_w(wukv_s[:, :, :], wukv_d, b_wukv)
        for kc in range(3):
            fw.op("pool", lambda e: e.tensor_scalar(out=wuq_s[:, kc, :], in0=wuq_s[:, kc, :],
                                                    scalar1=vec[:, V_CQG + kc:V_CQG + kc + 1], scalar2=None, op0=ALU.mult),
                  reads=[b_wuq, b_vec], writes=[b_wuq])
        for kc in range(2):
            fw.op("pool", lambda e: e.tensor_scalar(out=wukv_s[:, kc, :], in0=wukv_s[:, kc, :],
                                                    scalar1=vec[:, V_CKVG + kc:V_CKVG + kc + 1], scalar2=None, op0=ALU.mult),
                  reads=[b_wukv, b_vec], writes=[b_wukv])
        fw.op("pool", lambda e: e.memset(wuqr_s[:, :, :], 0.0), writes=[b_wuqr])
        w4 = wuq_s[:, :, :].rearrange("p k (h c) -> p k h c", c=96)
        r4 = wuqr_s[:, :, :].rearrange("p k (h c) -> p k h c", c=96)
        for kc in range(3):
            fw.op("pool", lambda e: e.tensor_scalar(out=r4[:, kc, :, 64:80], in0=w4[:, kc, :, 80:96], scalar1=-1.0,
                                                    scalar2=None, op0=ALU.mult), reads=[b_wuq], writes=[b_wuqr])
            fw.op("pool", lambda e: e.tensor_copy(r4[:, kc, :, 80:96], w4[:, kc, :, 64:80]), reads=[b_wuq],
                  writes=[b_wuqr])
        wkrr = sb("wkrr", [128, 8, 32], BF16)
        b_wkrr = Buf()
        craw = sb("craw", [128, 5, 512], F32)
        bcraw = [Buf() for _ in range(5)]
        cn = sb("cn", [128, 5, 512], BF16)
        bcn = [Buf() for _ in range(5)]
        st_r = sb("st_r", [128, 512], F32)
        b_str = Buf()
        tabq = sb("tabq", [128, 2, 512], F32)
        tabk = tabq[0:32, :, :]
        b_tab = Buf()
        osb = Rot([(sb("osb%d" % i, [128, 512], BF16), Buf()) for i in range(4)])
        if do_C:
            rt = Rot(yr_items)
        else:
            rt = Rot([(sb("rt%d" % i, [128, 512], F32), Buf()) for i in range(4)])
        glo = Rot([(sb("glo%d" % i, [24, 512], F32), Buf()) for i in range(1)])

        def evac_out(bk, bbk, rows, n, dst, eng="act"):
            o, bo = osb.next()
            if eng == "act":
                fw.op("act", lambda e: e.copy(out=o[0:rows, :n], in_=bk[0:rows, :n]), reads=[bbk], writes=[bo])
            else:
                fw.op("dve", lambda e: e.tensor_copy(o[0:rows, :n], bk[0:rows, :n]), reads=[bbk], writes=[bo])
            fw.dma("sp", dst, o[0:rows, :n], reads=[bo], writes=[Buf()])

        def rms(chunks, nfeat, n):
            bs, bbs = psr.next()
            for i, c in enumerate(chunks):
                tq, btq = lnt.next()
                fw.op("act", lambda e: e.activation(out=tq[:, :n], in_=craw[:, c, :n], func=AF.Square),
                      reads=[bcraw[c]], writes=[btq])
                fw.op("pe", lambda e: e.matmul(bs[:, :n], lhsT=ones_b[:, :], rhs=tq[:, :n], start=(i == 0),
                                                stop=(i == len(chunks) - 1)), reads=[btq, b_ones], writes=[bbs])
            fw.op("dve", lambda e: e.tensor_scalar(out=st_r[:, :n], in0=bs[:, :n], scalar1=1.0 / nfeat, scalar2=1e-6,
                                                   op0=ALU.mult, op1=ALU.add), reads=[bbs], writes=[b_str])
            fw.op("act", lambda e: e.activation(out=st_r[:, :n], in_=st_r[:, :n], func=AF.Sqrt),
                  reads=[b_str], writes=[b_str])
            fw.op("dve", lambda e: e.reciprocal(out=st_r[:, :n], in_=st_r[:, :n]), reads=[b_str], writes=[b_str])
            for c in chunks:
                fw.op("dve", lambda e: e.tensor_tensor(out=cn[:, c, :n], in0=craw[:, c, :n], in1=st_r[:, :n],
                                                       op=ALU.mult), reads=[bcraw[c], b_str], writes=[bcn[c]])

        def a_tile(t0):
            n = 512
            fw.dma("sp", tabq[64:96, 0, :], ropeC_d[:, t0:t0 + n], writes=[b_tab])
            fw.dma("sp", tabq[64:96, 1, :], ropeS_d[:, t0:t0 + n], writes=[b_tab])
            fw.dma("sp", tabk[:, 0, :], ropeC_d[:, t0:t0 + n], writes=[b_tab])
            fw.dma("sp", tabk[:, 1, :], ropeS_d[:, t0:t0 + n], writes=[b_tab])

            def hproj(w, bw, c0, m):
                bk, bbk = psr.next()
                for kc in range(8):
                    fw.op("pe", lambda e: e.matmul(bk[0:m, :n], lhsT=w[:, kc, c0:c0 + m], rhs=xb[:, kc, :n],
                                                    start=(kc == 0), stop=(kc == 7)), reads=[bw, bxb[kc]], writes=[bbk])
                return bk, bbk

            w, bw = wst.next()
            load_w(w[:, :, 0:384], w_in_d[:, 0:384], bw)
            for c in range(3):
                bk, bbk = hproj(w, bw, c * 128, 128)
                fw.op("act", lambda e: e.copy(out=craw[:, c, :n], in_=bk[:, :n]), reads=[bbk], writes=[bcraw[c]])
            rms([0, 1, 2], 384.0, n)
            w, bw = wst.next()
            load_w(w[:, :, 0:288], w_in_d[:, 384:672], bw)
            for c in range(2):
                bk, bbk = hproj(w, bw, c * 128, 128)
                fw.op("act", lambda e: e.copy(out=craw[:, 3 + c, :n], in_=bk[:, :n]), reads=[bbk], writes=[bcraw[3 + c]])
            rms([3, 4], 256.0, n)
            fw.op("pool", lambda e: e.tensor_scalar(out=wkrr[:, :, 0:16], in0=w[:, :, 272:288], scalar1=-1.0,
                                                    scalar2=None, op0=ALU.mult), reads=[bw], writes=[b_wkrr])
            fw.op("pool", lambda e: e.tensor_copy(wkrr[:, :, 16:32], w[:, :, 256:272]), reads=[bw], writes=[b_wkrr])
            bk, bbk = hproj(w, bw, 256, 32)
            bk2, bbk2 = hproj(wkrr, b_wkrr, 0, 32)
            t1, bt1 = rt.next()
            t2, bt2 = rt.next()
            fw.op("dve", lambda e: e.tensor_tensor(out=t1[0:32, :n], in0=bk[0:32, :n], in1=tabk[:, 0, :n], op=ALU.mult),
                  reads=[bbk, b_tab], writes=[bt1])
            fw.op("dve", lambda e: e.tensor_tensor(out=t2[0:32, :n], in0=bk2[0:32, :n], in1=tabk[:, 1, :n], op=ALU.mult),
                  reads=[bbk2, b_tab], writes=[bt2])
            o, bo = osb.next()
            fw.op("pool", lambda e: e.tensor_tensor(out=o[0:32, :n], in0=t1[0:32, :n], in1=t2[0:32, :n], op=ALU.add),
                  reads=[bt1, bt2], writes=[bo])
            fw.dma("sp", kr_d[:, t0:t0 + n], o[0:32, :n], reads=[bo], writes=[Buf()])
            for h in range(8):
                bk, bbk = psr.next()
                bk2, bbk2 = psr.next()
                for kc in range(3):
                    fw.op("pe", lambda e: e.matmul(bk[0:96, :n], lhsT=wuq_s[:, kc, h * 96:(h + 1) * 96], rhs=cn[:, kc, :n],
                                                    start=(kc == 0), stop=(kc == 2)), reads=[b_wuq, bcn[kc]], writes=[bbk])
                for kc in range(3):
                    fw.op("pe", lambda e: e.matmul(bk2[0:96, :n], lhsT=wuqr_s[:, kc, h * 96:(h + 1) * 96], rhs=cn[:, kc, :n],
                                                    start=(kc == 0), stop=(kc == 2)), reads=[b_wuqr, bcn[kc]], writes=[bbk2])
                o, bo = osb.next()
                fw.op("act", lambda e: e.copy(out=o[0:64, :n], in_=bk[0:64, :n]), reads=[bbk], writes=[bo])
                t1, bt1 = rt.next()
                t2, bt2 = rt.next()
                fw.op("dve", lambda e: e.tensor_tensor(out=t1[64:96, :n], in0=bk[64:96, :n], in1=tabq[64:96, 0, :n],
                                                       op=ALU.mult), reads=[bbk, b_tab], writes=[bt1])
                fw.op("dve", lambda e: e.tensor_tensor(out=t2[64:96, :n], in0=bk2[64:96, :n], in1=tabq[64:96, 1, :n],
                                                       op=ALU.mult), reads=[bbk2, b_tab], writes=[bt2])
                fw.op("pool", lambda e: e.tensor_tensor(out=o[64:96, :n], in0=t1[64:96, :n], in1=t2[64:96, :n],
                                                        op=ALU.add), reads=[bt1, bt2, bo], writes=[bo])
                fw.dma("sp", qm_d[h, :, t0:t0 + n], o[0:96, :n], reads=[bo], writes=[Buf()])
            for h in range(8):
                bk, bbk = psr.next()
                for kc in range(2):
                    fw.op("pe", lambda e: e.matmul(bk[0:64, :n], lhsT=wukv_s[:, kc, h * 128:h * 128 + 64],
                                                    rhs=cn[:, 3 + kc, :n], start=(kc == 0), stop=(kc == 1)),
                          reads=[b_wukv, bcn[3 + kc]], writes=[bbk])
                evac_out(bk, bbk, 64, n, kn_d[h, :, t0:t0 + n], eng=("act" if h % 2 else "dve"))
            wv4 = wukv_s[:, :, :].rearrange("p k (h c) -> p k h c", c=128)
            for tb in range(4):
                bk, bbk = psr.next()
                for kc in range(2):
                    fw.op("pe", lambda e: e.matmul(bk[:, :].rearrange("p (h c) -> p h c", c=64),
                                                    lhsT=cn[:, 3 + kc, tb * 128:(tb + 1) * 128],
                                                    rhs=wv4[:, kc, :, 64:128], start=(kc == 0), stop=(kc == 1)),
                          reads=[b_wukv, bcn[3 + kc]], writes=[bbk])
                evac_out(bk, bbk, 128, 512, vm_d[t0 + tb * 128:t0 + (tb + 1) * 128, :], eng="dve")
            w, bw = wst.next()
            load_w(w[:, :, 0:512], w_in_d[:, 672:1184], bw)
            for c in range(4):
                bk, bbk = hproj(w, bw, c * 128, 128)
                evac_out(bk, bbk, 128, n, nq_d[c * 128:(c + 1) * 128, t0:t0 + n], eng=("act" if c % 2 else "dve"))
            w, bw = wst.next()
            load_w(w[:, :, 0:512], w_in_d[:, 1184:1696], bw)
            for c in range(3):
                bk, bbk = hproj(w, bw, c * 128, 128)
                evac_out(bk, bbk, 128, n, nkf_d[c, :, t0:t0 + n], eng=("act" if c % 2 else "dve"))
            wC, bwC = w, bw
            w, bw = wst.next()
            load_w(w[:, :, 0:280], w_in_d[:, 1696:1976], bw)
            bk, bbk = hproj(w, bw, 0, 128)
            evac_out(bk, bbk, 128, n, nkf_d[3, :, t0:t0 + n])
            bk, bbk = hproj(w, bw, 256, 24)
            g, bg = glo.next()
            fw.op("act", lambda e: e.activation(out=g[:, :n], in_=bk[0:24, :n], func=AF.Sigmoid), reads=[bbk], writes=[bg])
            fw.dma("sp", gl_d[:, t0:t0 + n], g[:, :n], reads=[bg], writes=[Buf()])
            for tb in range(4):
                bk, bbk = psr.next()
                for kc in range(8):
                    fw.op("pe", lambda e: e.matmul(bk[:, 0:128], lhsT=xb[:, kc, tb * 128:(tb + 1) * 128],
                                                    rhs=wC[:, kc, 384:512], start=(kc == 0), stop=(kc == 7)),
                          reads=[bwC, bxb[kc]], writes=[bbk])
                for kc in range(8):
                    fw.op("pe", lambda e: e.matmul(bk[:, 128:256], lhsT=xb[:, kc, tb * 128:(tb + 1) * 128],
                                                    rhs=w[:, kc, 128:256], start=(kc == 0), stop=(kc == 7)),
                          reads=[bw, bxb[kc]], writes=[bbk])
                o, bo = osb.next()
                fw.op("dve", lambda e: e.tensor_copy(o[:, 0:256], bk[:, 0:256]), reads=[bbk], writes=[bo])
                fw.dma("sp", nvt_d[t0 + tb * 128:t0 + (tb + 1) * 128, :], o[:, 0:256], reads=[bo], writes=[Buf()])

    def _program():
        if do_C:
            c_tile(2, 0, True)
        for ti in range(TT // 512):
            t0 = ti * 512
            if do_C:
                c_tile(512, HALO + t0, False)
            else:
                fw.dma("sp", xr[:, :, :], xT_d[:, t0:t0 + 512].rearrange("(c p) t -> p c t", p=128), writes=bxr)
                if do_ln_in:
                    layer_norm(512, V_LNIN)
            fw.dma("sp", xo_d[:, t0:t0 + 512].rearrange("(c p) t -> p c t", p=128), xr[:, :, :], reads=bxr,
                   writes=[Buf()])
            if do_A:
                a_tile(t0)

    try:
        _program()
    except _Stop:
        pass
    fw.finish()
    return nc


def _colvec(v):
    v = np.asarray(v, np.float32)
    return v.reshape(-1, 128).T


def _rope_tabs():
    inv = (10000.0 ** (-np.arange(0, 32, 2, dtype=np.float32) / np.float32(32))).astype(np.float32)
    ang = np.arange(S, dtype=np.float32)[:, None] * inv[None, :]
    c = np.cos(ang).astype(np.float32).T
    s = np.sin(ang).astype(np.float32).T
    return np.concatenate([c, c], 0), np.concatenate([s, s], 0)


def _prep_T(P, lC, lA, xT_list, oT_list, do_ln_in):
    vec = np.zeros((128, 264), np.float32)
    if do_ln_in:
        vec[:, 0:8] = _colvec(P["ln_in_g"])
        vec[:, 8:16] = _colvec(P["ln_in_b"])
    if lC is not None:
        for base, nm in ((16, "ln1"), (32, "ln2"), (48, "ln3")):
            vec[:, base:base + 8] = _colvec(P[nm + "_g"][lC])
            vec[:, base + 8:base + 16] = _colvec(P[nm + "_b"][lC])
        vec[:, 64:72] = _colvec(P["ln_mem_g"])
        vec[:, 72:80] = _colvec(P["ln_mem_b"])
        for k in range(3):
            vec[:, 80 + 44 * k:80 + 44 * (k + 1)] = _colvec(P["ffn_conv_w"][lC][k])
        vec[:, 80 + 132:80 + 176] = _colvec(P["ffn_conv_b"][lC])
    if lA is not None:
        vec[:, 256:259] = _colvec(P["mla_cq_g"][lA])
        vec[:, 259:261] = _colvec(P["mla_ckv_g"][lA])
        rc, rs = _rope_tabs()
    maps = []
    f32 = lambda a: np.ascontiguousarray(a, dtype=np.float32)
    for c in range(NCORES):
        b, i = c // 4, c % 4
        m = dict(vec=vec)
        if lC is not None:
            if i == 0:
                halo_x = np.zeros((D, 2), np.float32)
                halo_o = np.zeros((D, 2), NPBF)
            else:
                halo_x = xT_list[c - 1][:, -2:]
                halo_o = oT_list[c - 1][:, -2:]
            m["xT"] = np.ascontiguousarray(np.concatenate([halo_x, xT_list[c]], axis=1))
            m["oT"] = np.ascontiguousarray(np.concatenate([halo_o, oT_list[c]], axis=1))
            m["flag"] = np.full((128, 1), 0.0 if i == 0 else 1.0, np.float32)
            m["memT"] = f32(P["mem"][b].T)
            m["w_out"] = f32(P["w_out"][lC])
            m["xa_wq"] = f32(P["xa_wq"][lC])
            m["xa_wkv"] = f32(P["xa_wkv"][lC])
            m["xa_wo"] = f32(P["xa_wo"][lC])
            m["w_up"] = f32(P["ffn_w_up"][lC])
            m["w_down"] = f32(P["ffn_w_down"][lC])
        else:
            m["xT"] = xT_list[c]
        if lA is not None:
            m["w_in"] = f32(P["w_in"][lA])
            m["w_uq"] = f32(P["mla_w_uq"][lA])
            m["w_ukv"] = f32(P["mla_w_ukv"][lA])
            m["ropeC"] = np.ascontiguousarray(rc[:, i * TT:(i + 1) * TT])
            m["ropeS"] = np.ascontiguousarray(rs[:, i * TT:(i + 1) * TT])
        maps.append(m)
    return maps


def _A_outs(res):
    ao = []
    for r in res:
        ao.append(dict(qm=r["qm"], kn=r["kn"], kr=r["kr"], vm=r["vm"],
                       nq=r["nq"].reshape(8, 64, TT), nkf=r["nkf"].reshape(4, 2, 64, TT),
                       nvt=r["nvt"].reshape(TT, 2, 2, 64), gl=r["gl"]))
    return ao


_PROGS = {}


def _prog(key):
    if key not in _PROGS:
        if key == "B":
            _PROGS[key] = build_B()
        else:
            _PROGS[key] = build_T(*key)
    return _PROGS[key]


def _run(key, maps):
    res = run_bass_kernel_spmd(_prog(key), maps, core_ids=list(range(NCORES)))
    return res.results


def kernel(**inputs):
    P = {k: np.asarray(v) for k, v in inputs.items()}
    x = P["x"]
    xT = [np.ascontiguousarray(x[c // 4, (c % 4) * TT:(c % 4 + 1) * TT].T, dtype=np.float32) for c in range(NCORES)]
    res = _run((True, False, True), _prep_T(P, None, 0, xT, None, True))
    xn = [r["xo"] for r in res]
    ao = _A_outs(res)
    out = np.zeros((2, S, D), np.float32)
    for l in range(2):
        resB = _run("B", _prep_B(ao, l, P))
        oT = _gather_B(resB)
        if l == 0:
            res = _run((False, True, True), _prep_T(P, 0, 1, xn, oT, False))
            xn = [r["xo"] for r in res]
            ao = _A_outs(res)
        else:
            res = _run((False, True, False), _prep_T(P, 1, None, xn, oT, False))
            for c in range(NCORES):
                out[c // 4, (c % 4) * TT:(c % 4 + 1) * TT, :] = res[c]["xo"].T
    return out
```

```python
import math
import numpy as np
import ml_dtypes
import concourse.bass as bass
import concourse.mybir as mybir
from concourse.bass_utils import run_bass_kernel_spmd

F32 = mybir.dt.float32
BF16 = mybir.dt.bfloat16
AF = mybir.ActivationFunctionType
ALU = mybir.AluOpType
NPBF = ml_dtypes.bfloat16

S = 8192
D = 1024
NCORES = 8
ALPHA = (2.0 * 2) ** 0.25
MLA_SCALE = 96 ** -0.5


class Buf:
    __slots__ = ("w", "r", "psum")

    def __init__(self, psum=False):
        self.w = []
        self.r = {}
        self.psum = psum


class Rot:
    def __init__(self, items):
        self.items = items
        self.i = 0

    def next(self):
        it = self.items[self.i]
        self.i = (self.i + 1) % len(self.items)
        return it


class FW:
    COMPUTE = ("pe", "act", "dve", "pool")

    def __init__(self, nc, n_dma_sems=32):
        self.nc = nc
        self.E = {"pe": nc.tensor, "act": nc.scalar, "dve": nc.vector,
                  "pool": nc.gpsimd, "sp": nc.sync}
        self.sem = {}
        self.cnt = {}
        for e in self.COMPUTE:
            self.sem[e] = nc.alloc_semaphore(name="s_" + e)
            self.cnt[e] = 0
        self.waited = {e: {} for e in self.E}
        self.dsem = []
        for i in range(n_dma_sems):
            key = "d%d" % i
            self.sem[key] = nc.alloc_semaphore(name="s_" + key)
            self.cnt[key] = 0
            self.dsem.append(key)
        self.dnext = 0

    def _wait(self, eng, key, val):
        if val <= 0 or self.waited[eng].get(key, 0) >= val:
            return
        self.E[eng].wait_ge(self.sem[key], val)
        self.waited[eng][key] = val

    def _deps(self, eng, reads, writes):
        deps = {}
        for b in reads:
            for (k, v) in b.w:
                if deps.get(k, 0) < v:
                    deps[k] = v
            if b.psum:
                for k, v in b.r.items():
                    if k != eng and deps.get(k, 0) < v:
                        deps[k] = v
        for b in writes:
            for (k, v) in b.w:
                if deps.get(k, 0) < v:
                    deps[k] = v
            for k, v in b.r.items():
                if deps.get(k, 0) < v:
                    deps[k] = v
        for k, v in deps.items():
            if k == eng:
                if eng == "pe":
                    continue
                if v <= self.cnt[eng] - 3:
                    continue
            self._wait(eng, k, v)

    def _mark(self, tok, reads, writes):
        k, v = tok
        for b in reads:
            if b.r.get(k, 0) < v:
                b.r[k] = v
        for b in writes:
            b.w = [tok]
            b.r = {}

    def op(self, eng, fn, reads=(), writes=()):
        self._deps(eng, reads, writes)
        ins = fn(self.E[eng])
        self.cnt[eng] += 1
        ins.then_inc(self.sem[eng], 1)
        self._mark((eng, self.cnt[eng]), reads, writes)
        return ins

    def dma(self, q, out, in_, reads=(), writes=(), **kw):
        self._deps(q, reads, writes)
        key = self.dsem[self.dnext]
        self.dnext = (self.dnext + 1) % len(self.dsem)
        self._wait(q, key, self.cnt[key])
        ins = self.E[q].dma_start(out=out, in_=in_, **kw)
        self.cnt[key] += 16
        ins.then_inc(self.sem[key], 16)
        self._mark((key, self.cnt[key]), reads, writes)

    def finish(self, eng="sp"):
        for key in self.dsem:
            self._wait(eng, key, self.cnt[key])


def _consts_B():
    i = np.arange(128)[:, None]
    u = np.arange(512)[None, :]
    masks = np.zeros((13, 128, 512), np.float32)
    for m in range(4):
        masks[m] = (u - i - 128 * m >= 0)
    for m in range(4):
        o = -512 + 128 * m
        masks[4 + m] = (u - i - o < 512)
    for m in range(5):
        masks[8 + m] = (16 * i + 31 - 512 * m <= u)
    n = np.arange(512)
    cs = 16 * n
    j = np.arange(128)
    ov = np.clip(np.minimum(cs[:, None] + 32, 64 * j[None, :] + 64)
                 - np.maximum(cs[:, None], 64 * j[None, :]), 0, None).astype(np.float32) / 32.0
    ov[511] = 0.0
    ovo = np.concatenate([ov, np.ones((512, 1), np.float32)], axis=1)
    ovo[511] = 0.0
    ovo = ovo.reshape(4, 128, 129)
    p = np.arange(128)[:, None]
    hp = (p >= 64).astype(np.int64)
    m = np.arange(256)[None, :] - 127
    vw = (m <= hp).astype(np.float32)
    cw = 2e4 * (m == hp) + 4e4 * (m == hp - 1) + (vw - 1.0)
    bf0 = np.zeros((128, 128), np.float32)
    bf0[:, 0] = 1e4
    wide = np.concatenate([vw, cw.astype(np.float32), bf0], axis=1)
    ew = (np.arange(128)[:, None] == (np.arange(S)[None, :] // 64)).astype(np.float32)
    ident = np.eye(128, dtype=np.float32)
    ce = 16 * np.arange(512) + 31
    kaugc = np.stack([np.ones(512), np.ones(512), 128.0 * (ce // 128), (ce % 128).astype(np.float64)]).astype(np.float32)
    return dict(cmask=((masks - 1.0) * 30000.0).astype(NPBF), ov=ovo.astype(NPBF), wide=wide.astype(np.float32),
                ew=ew.astype(NPBF), ident=ident.astype(NPBF), kaugc=kaugc.astype(NPBF))


def _aug_tables():
    t = np.arange(S)
    a = (t // 128).astype(np.float64)
    b = (t % 128).astype(np.float64)
    slopes = 2.0 ** (-(np.arange(1, 9)))
    qa = np.zeros((8, 4, S), np.float32)
    for h in range(8):
        s8 = 8.0 * slopes[h]
        qa[h, 0] = -s8 * 128.0 * a
        qa[h, 1] = -s8 * b
        qa[h, 2] = s8
        qa[h, 3] = s8
    ka = np.stack([np.ones(S), np.ones(S), 128.0 * a, b]).astype(np.float32)
    return qa.astype(NPBF), ka.astype(NPBF)


def build_B():
    nc = bass.Bass("TRN2", target_bir_lowering=False)
    fw = FW(nc)

    def din(name, shape, dt):
        return nc.dram_tensor(name, shape, dt, kind="ExternalInput").ap()

    mq = din("mq", [2, 96, S], BF16)
    mk = din("mk", [2, 96, S], BF16)
    mv = din("mv", [2, 128, 64, 64], BF16)
    nq = din("nq", [4, 68, S], BF16)
    nkc = din("nkc", [2, 64, S], BF16)
    nks = din("nks", [68, S], BF16)
    nvs = din("nvs", [128, 64, 64], BF16)
    nkw = din("nkw", [68, S], BF16)
    nvw = din("nvw", [128, 64, 64], BF16)
    gates_h = nc.dram_tensor("gates", [6, S], F32, kind="ExternalInput")
    peT = din("peT", [2, 64, 32], F32)
    w1 = din("w1", [2, 2048, 128], F32)
    b1 = din("b1", [2, 128, 1], F32)
    w2 = din("w2", [2, 128, 64], F32)
    c_cmask = din("cmask_c", [13, 128, 512], BF16)
    c_ov = din("ov_c", [4, 128, 129], BF16)
    c_wide = din("wide_c", [128, 640], F32)
    c_ew = din("ew_c", [128, S], BF16)
    c_ident = din("ident_c", [128, 128], BF16)
    c_kaugc = din("kaugc_c", [4, 512], BF16)
    out = nc.dram_tensor("o", [4, 64, S], BF16, kind="ExternalOutput").ap()

    def sb(name, shape, dt):
        return nc.alloc_sbuf_tensor(name, shape, dt)

    big = [sb("big%d" % i, [128, S], BF16) for i in range(4)]
    bbig = [Buf() for _ in range(4)]
    selT = sb("selT", [128, S], BF16)
    b_selT = Buf()
    ew = sb("ew", [128, S], BF16)
    cmask = sb("cmask_s", [128, 13, 512], BF16)
    ov = sb("ov_s", [128, 4, 129], BF16)
    wide = sb("wide_s", [128, 640], F32)
    ident = sb("ident_s", [128, 128], BF16)
    b_const = Buf()
    cin = big[2]
    b_cin = bbig[2]
    w1s = sb("w1s", [64, 32, 128], BF16)
    w2s = sb("w2s", [128, 64], BF16)
    b1s = sb("b1s", [128, 1], F32)
    peTs = sb("peTs", [64, 32], BF16)
    b_cw = Buf()
    cbias = sb("cbias", [128, 1], F32)
    b_cbias = Buf()
    zf = sb("zf", [128, 512], F32)
    zt = sb("zt", [128, 512], F32)
    zs = sb("zs", [128, 512], F32)
    b_zf, b_zt, b_zs = Buf(), Buf(), Buf()
    hid = sb("hid", [128, 512], BF16)
    b_hid = Buf()
    kcaug = sb("kcaug", [68, 512], BF16)
    b_kc = Buf()
    vcaug = sb("vcaug", [128, 4, 128], BF16)
    b_vc = Buf()
    pc = sb("pc", [128, 16, 512], BF16)
    b_pc = [Buf() for _ in range(16)]
    qch = [sb("qch%d" % i, [96, 512], BF16) for i in range(8)]
    qrot = Rot([(qch[i], Buf()) for i in range(8)])
    pts = [sb("pt%d" % i, [128, 512], BF16) for i in range(4)]
    prot = Rot([(pts[i], Buf()) for i in range(4)])
    pre = [sb("pre%d" % i, [128, 512], F32) for i in range(2)]
    prerot = Rot([(pre[i], Buf()) for i in range(2)])
    gbc = [sb("gbc%d" % i, [64, 6, 512], F32) for i in range(2)]
    grot = Rot([(gbc[i], Buf()) for i in range(2)])
    rd = [sb("rd%d" % i, [64, 512], F32) for i in range(2)]
    rrot = Rot([(rd[i], Buf()) for i in range(2)])
    tmpc = [sb("tmpc%d" % i, [64, 512], F32) for i in range(2)]
    trot = Rot([(tmpc[i], Buf()) for i in range(2)])
    oacc = [sb("oacc%d" % i, [64, 512], F32) for i in range(3)]
    oarot = Rot([(oacc[i], Buf()) for i in range(3)])
    obf = [sb("obf%d" % i, [64, 512], BF16) for i in range(3)]
    obrot = Rot([(obf[i], Buf()) for i in range(3)])
    negb = sb("negb", [128, 1], F32)
    b_negb = Buf()
    fw.op("pool", lambda e: e.memset(negb[:, :], -30000.0), writes=[b_negb])
    imp = sb("imp", [128, 128], F32)
    sc_t = sb("sc_t", [128, 128], F32)
    sc_u = sb("sc_u", [128, 128], F32)
    m8a = sb("m8a", [128, 8], F32)
    m8b = sb("m8b", [128, 8], F32)
    rdc = sb("rdc", [128, 1], F32)
    selb = sb("selb", [128, 128], BF16)
    b_imp, b_sct, b_scu, b_m8a, b_m8b, b_rdc, b_selb = [Buf() for _ in range(7)]

    banks = [nc.alloc_psum_tensor("bank%d" % i, [128, 512], F32) for i in range(8)]
    work = Rot([(banks[i], Buf(True)) for i in range(4)])
    accr = Rot([(banks[4 + i], Buf(True)) for i in range(4)])
    b_out = Buf()

    fw.dma("sp", cmask[:, :, :], c_cmask.rearrange("m p u -> p m u"), writes=[b_const])
    fw.dma("sp", ov[:, :, :], c_ov.rearrange("c p j -> p c j"), writes=[b_const])
    fw.dma("sp", wide[:, :], c_wide, writes=[b_const])
    fw.dma("sp", ew[:, :], c_ew, writes=[b_const])
    fw.dma("sp", ident[:, :], c_ident, writes=[b_const])
    for bi in (1, 3):
        v4 = big[bi][:, :].rearrange("p (k c) -> p k c", c=128)
        fw.op("pool", lambda e: e.memset(v4[:, :, 64:128], 1.0), writes=[bbig[bi]])

    def load_kv(bk, bv, ksrc, vsrc, krows):
        fw.dma("sp", big[bk][0:krows, :], ksrc, writes=[bbig[bk]])
        v4 = big[bv][:, :].rearrange("p (k c) -> p k c", c=128)
        fw.dma("sp", v4[:, :, 0:64], vsrc, writes=[bbig[bv]])

    load_kv(0, 1, mk[0], mv[0], 96)

    def branch(K, q_ap, b_q, kt_fn, b_k, v_fn, b_v, blocks, scale, acc, b_acc, extra=None):
        n = len(blocks)
        scs = [None] * n

        def qk(i):
            w, bw = work.next()
            fw.op("pe", lambda e: e.matmul(w[:, :], lhsT=kt_fn(blocks[i][0]), rhs=q_ap,
                                            start=True, stop=(extra is None)),
                  reads=[b_k, b_q], writes=[bw])
            if extra is not None:
                el, er, ebufs = extra(blocks[i][0])
                fw.op("pe", lambda e: e.matmul(w[:, :], lhsT=el, rhs=er, start=False, stop=True),
                      reads=ebufs, writes=[bw])
            scs[i] = (w, bw)

        for i in range(min(2, n)):
            qk(i)
        for i in range(n):
            w, bw = scs[i]
            p, bp = prot.next()
            m = blocks[i][1]
            if m is not None:
                t, bt = prerot.next()
                fw.op("dve", lambda e: e.tensor_tensor(out=t[:, :], in0=w[:, :], in1=cmask[:, m, :],
                                                       op=ALU.add),
                      reads=[bw, b_const], writes=[bt])
                fw.op("act", lambda e: e.activation(out=p[:, :], in_=t[:, :], func=AF.Exp, scale=scale),
                      reads=[bt], writes=[bp])
            else:
                fw.op("act", lambda e: e.activation(out=p[:, :], in_=w[:, :], func=AF.Exp, scale=scale),
                      reads=[bw], writes=[bp])
            if i + 2 < n:
                qk(i + 2)
            fw.op("pe", lambda e: e.matmul(acc[:, :], lhsT=v_fn(blocks[i][0]), rhs=p[:, :],
                                            start=(i == 0), stop=(i == n - 1)),
                  reads=[b_v, bp], writes=[b_acc])

    def recip_den(acc, b_acc):
        r, br = rrot.next()
        fw.op("dve", lambda e: e.tensor_scalar(out=r[:, :], in0=acc[64:128, :], scalar1=1e-30,
                                               scalar2=None, op0=ALU.max),
              reads=[b_acc], writes=[br])
        fw.op("dve", lambda e: e.reciprocal(out=r[:, :], in_=r[:, :]), reads=[br], writes=[br])
        return r, br

    def gated_accum(acc, b_acc, g_ap, b_g, dst, b_dst, first):
        r, br = recip_den(acc, b_acc)
        fw.op("dve", lambda e: e.tensor_tensor(out=r[:, :], in0=r[:, :], in1=g_ap, op=ALU.mult),
              reads=[br, b_g], writes=[br])
        if first:
            fw.op("dve", lambda e: e.tensor_tensor(out=dst[:, :], in0=acc[0:64, :], in1=r[:, :],
                                                   op=ALU.mult),
                  reads=[b_acc, br], writes=[b_dst])
        else:
            t, bt = trot.next()
            fw.op("dve", lambda e: e.tensor_tensor(out=t[:, :], in0=acc[0:64, :], in1=r[:, :],
                                                   op=ALU.mult),
                  reads=[b_acc, br], writes=[bt])
            fw.op("pool", lambda e: e.tensor_tensor(out=dst[:, :], in0=dst[:, :], in1=t[:, :],
                                                    op=ALU.add),
                  reads=[bt, b_dst], writes=[b_dst])

    def vaug(bi):
        v4 = big[bi][:, :].rearrange("p (k c) -> p k c", c=128)
        return lambda kb: v4[:, kb, :]

    def mla_head(hi, bk, bv):
        vf = vaug(bv)
        ql = {}

        def loadq(c):
            q, bq = qrot.next()
            fw.dma("sp", q[0:96, :], mq[hi, :, c * 512:(c + 1) * 512], writes=[bq])
            ql[c] = (q, bq)

        loadq(0)
        for qc in range(16):
            if qc + 1 < 16:
                loadq(qc + 1)
            q, bq = ql.pop(qc)
            blocks = [(kb, (kb - 4 * qc) if kb >= 4 * qc else None) for kb in range(4 * qc + 4)]
            acc, bacc = accr.next()
            branch(96, q[0:96, :], bq, lambda kb: big[bk][0:96, kb * 128:(kb + 1) * 128], bbig[bk],
                   vf, bbig[bv], blocks, MLA_SCALE, acc, bacc)
            r, br = recip_den(acc, bacc)
            o, bo = obrot.next()
            fw.op("dve", lambda e: e.tensor_tensor(out=o[:, :], in0=acc[0:64, :], in1=r[:, :],
                                                   op=ALU.mult),
                  reads=[bacc, br], writes=[bo])
            fw.dma("sp", out[hi, :, qc * 512:(qc + 1) * 512], o[:, :], reads=[bo], writes=[Buf()])

    def compress(which):
        fw.dma("sp", cin[0:64, :], nkc[which], writes=[b_cin])
        fw.dma("pool", w1s[:, :, :], w1[which].rearrange("(j d) m -> d j m", d=64), writes=[b_cw])
        fw.dma("pool", w2s[:, :], w2[which], writes=[b_cw])
        fw.dma("sp", b1s[:, :], b1[which], writes=[b_cw])
        fw.dma("pool", peTs[:, :], peT[which], writes=[b_cw])
        wb, bwb = work.next()
        for j in range(32):
            fw.op("pe", lambda e: e.matmul(wb[:, 0:1], lhsT=w1s[:, j, :], rhs=peTs[:, j:j + 1],
                                            start=(j == 0), stop=(j == 31)),
                  reads=[b_cw], writes=[bwb])
        fw.op("dve", lambda e: e.tensor_tensor(out=cbias[:, :], in0=wb[:, 0:1], in1=b1s[:, :], op=ALU.add),
              reads=[bwb, b_cw], writes=[b_cbias])
        c3 = cin[0:64, :].rearrange("p (n s) -> p n s", s=16)
        wh, bwh = work.next()
        for j in range(32):
            rhs = c3[:, 0:511, j] if j < 16 else c3[:, 1:512, j - 16]
            fw.op("pe", lambda e: e.matmul(wh[:, 0:511], lhsT=w1s[:, j, :], rhs=rhs,
                                            start=(j == 0), stop=(j == 31)),
                  reads=[b_cw, b_cin], writes=[bwh])
        fw.op("act", lambda e: e.activation(out=zf[:, 0:511], in_=wh[:, 0:511], func=AF.Identity,
                                            bias=cbias[:, 0:1], scale=1.0),
              reads=[bwh, b_cbias], writes=[b_zf])
        fw.op("dve", lambda e: e.tensor_tensor(out=zt[:, 0:511], in0=zf[:, 0:511], in1=zf[:, 0:511],
                                               op=ALU.mult), reads=[b_zf], writes=[b_zt])
        fw.op("dve", lambda e: e.tensor_scalar(out=zt[:, 0:511], in0=zt[:, 0:511], scalar1=0.044715,
                                               scalar2=1.0, op0=ALU.mult, op1=ALU.add),
              reads=[b_zt], writes=[b_zt])
        fw.op("dve", lambda e: e.tensor_tensor(out=zt[:, 0:511], in0=zt[:, 0:511], in1=zf[:, 0:511],
                                               op=ALU.mult), reads=[b_zt, b_zf], writes=[b_zt])
        fw.op("act", lambda e: e.activation(out=zs[:, 0:511], in_=zt[:, 0:511], func=AF.Sigmoid,
                                            scale=1.5957691216057308),
              reads=[b_zt], writes=[b_zs])
        fw.op("pool", lambda e: e.memset(hid[:, :], 0.0), writes=[b_hid])
        fw.op("dve", lambda e: e.tensor_tensor(out=hid[:, 0:511], in0=zf[:, 0:511], in1=zs[:, 0:511],
                                               op=ALU.mult), reads=[b_zf, b_zs], writes=[b_hid])
        if which == 0:
            fw.op("pool", lambda e: e.memset(kcaug[:, :], 0.0), writes=[b_kc])
            wk, bwk = work.next()
            fw.op("pe", lambda e: e.matmul(wk[0:64, 0:511], lhsT=w2s[:, :], rhs=hid[:, 0:511],
                                            start=True, stop=True), reads=[b_cw, b_hid], writes=[bwk])
            fw.op("dve", lambda e: e.tensor_copy(kcaug[0:64, 0:511], wk[0:64, 0:511]),
                  reads=[bwk], writes=[b_kc])
            fw.dma("sp", kcaug[64:68, :], c_kaugc, writes=[b_kc])
        else:
            fw.op("pool", lambda e: e.memset(vcaug[:, :, 0:64], 0.0), writes=[b_vc])
            fw.op("pool", lambda e: e.memset(vcaug[:, :, 64:128], 1.0), writes=[b_vc])
            for c in range(4):
                rows = 128 if c < 3 else 127
                wv, bwv = work.next()
                fw.op("pe", lambda e: e.matmul(wv[0:rows, 0:64], lhsT=hid[:, c * 128:c * 128 + rows],
                                                rhs=w2s[:, :], start=True, stop=True),
                      reads=[b_cw, b_hid], writes=[bwv])
                fw.op("dve", lambda e: e.tensor_copy(vcaug[0:rows, c, 0:64], wv[0:rows, 0:64]),
                      reads=[bwv], writes=[b_vc])

    def cmp_blocks(qc):
        return [(c, (8 + qc - 4 * c) if (qc - 4 * c) <= 4 else None) for c in range(qc // 4 + 1)]

    def select_phase():
        preq = {}

        def load_sq(c_):
            preq[c_] = []
            for r in range(4):
                q, bq = qrot.next()
                fw.dma("sp", q[0:68, :], nq[r, :, c_ * 512:(c_ + 1) * 512], writes=[bq])
                preq[c_].append((q, bq))

        load_sq(0)
        for qc in range(16):
            if qc + 1 < 16:
                load_sq(qc + 1)
            blocks = cmp_blocks(qc)
            qlist = preq.pop(qc)
            for r in range(4):
                q, bq = qlist[r]
                for (c, m) in blocks:
                    w, bw = work.next()
                    fw.op("pe", lambda e: e.matmul(w[:, :], lhsT=kcaug[:, c * 128:(c + 1) * 128],
                                                    rhs=q[0:68, :], start=True, stop=True),
                          reads=[b_kc, bq], writes=[bw])
                    pi = r * 4 + c
                    if m is not None:
                        t, bt = prerot.next()
                        fw.op("dve", lambda e: e.tensor_tensor(out=t[:, :], in0=w[:, :], in1=cmask[:, m, :],
                                                               op=ALU.add),
                              reads=[bw, b_const], writes=[bt])
                        fw.op("act", lambda e: e.activation(out=pc[:, pi, :], in_=t[:, :], func=AF.Exp,
                                                            scale=0.125),
                              reads=[bt], writes=[b_pc[pi]])
                    else:
                        fw.op("act", lambda e: e.activation(out=pc[:, pi, :], in_=w[:, :], func=AF.Exp,
                                                            scale=0.125),
                              reads=[bw], writes=[b_pc[pi]])
            for tb in range(4):
                gi = qc * 4 + tb
                for r in range(4):
                    w, bw = work.next()
                    nb = len(blocks)
                    for bi, (c, m) in enumerate(blocks):
                        pi = r * 4 + c
                        fw.op("pe", lambda e: e.matmul(w[:, 0:129], lhsT=pc[:, pi, tb * 128:(tb + 1) * 128],
                                                        rhs=ov[:, c, :], start=(bi == 0), stop=(bi == nb - 1)),
                              reads=[b_pc[pi], b_const], writes=[bw])
                    fw.op("dve", lambda e: e.tensor_scalar(out=rdc[:, :], in0=w[:, 128:129], scalar1=1e-30,
                                                           scalar2=None, op0=ALU.max),
                          reads=[bw], writes=[b_rdc])
                    fw.op("dve", lambda e: e.reciprocal(out=rdc[:, :], in_=rdc[:, :]),
                          reads=[b_rdc], writes=[b_rdc])
                    if r == 0:
                        fw.op("dve", lambda e: e.tensor_scalar(out=imp[:, :], in0=w[:, 0:128],
                                                               scalar1=rdc[:, 0:1], scalar2=None,
                                                               op0=ALU.mult),
                              reads=[bw, b_rdc], writes=[b_imp])
                    else:
                        fw.op("dve", lambda e: e.scalar_tensor_tensor(out=imp[:, :], in0=w[:, 0:128],
                                                                      scalar=rdc[:, 0:1], in1=imp[:, :],
                                                                      op0=ALU.mult, op1=ALU.add),
                              reads=[bw, b_rdc, b_imp], writes=[b_imp])
                st = 127 - 2 * gi
                vsl = wide[:, st:st + 128]
                csl = wide[:, 256 + st:256 + st + 128]
                fw.op("dve", lambda e: e.tensor_tensor(out=sc_t[:, :], in0=imp[:, :], in1=vsl, op=ALU.mult),
                      reads=[b_imp, b_const], writes=[b_sct])
                fw.op("dve", lambda e: e.tensor_tensor(out=sc_t[:, :], in0=sc_t[:, :], in1=csl, op=ALU.add),
                      reads=[b_sct, b_const], writes=[b_sct])
                fw.op("dve", lambda e: e.tensor_tensor(out=sc_t[:, :], in0=sc_t[:, :], in1=wide[:, 512:640],
                                                       op=ALU.add),
                      reads=[b_sct, b_const], writes=[b_sct])
                fw.op("dve", lambda e: e.max(out=m8a[:, :], in_=sc_t[:, :]), reads=[b_sct], writes=[b_m8a])
                fw.op("dve", lambda e: e.match_replace(out=sc_u[:, :], in_to_replace=m8a[:, :],
                                                       in_values=sc_t[:, :], imm_value=-2.0),
                      reads=[b_sct, b_m8a], writes=[b_scu])
                fw.op("dve", lambda e: e.max(out=m8b[:, :], in_=sc_u[:, :]), reads=[b_scu], writes=[b_m8b])
                fw.op("dve", lambda e: e.scalar_tensor_tensor(out=selb[:, :], in0=sc_t[:, :],
                                                              scalar=m8b[:, 7:8], in1=vsl,
                                                              op0=ALU.is_ge, op1=ALU.mult),
                      reads=[b_sct, b_m8b, b_const], writes=[b_selb])
                w, bw = work.next()
                fw.op("pe", lambda e: e.matmul(w[:, 0:128], lhsT=selb[:, :], rhs=ident[:, :],
                                                start=True, stop=True),
                      reads=[b_selb, b_const], writes=[bw])
                fw.op("act", lambda e: e.activation(out=selT[:, gi * 128:(gi + 1) * 128], in_=w[:, 0:128],
                                                    func=AF.Identity, bias=negb[:, 0:1], scale=30000.0),
                      reads=[bw, b_negb], writes=[b_selT])

    def nsa_phase():
        vfs = vaug(1)
        vfw = vaug(3)
        pre_ = {}

        def load_qc(c):
            g_, bg_ = grot.next()
            fw.dma("sp", g_[:, :, :], bass.AP(gates_h, c * 512, [[0, 64], [S, 6], [1, 512]]), writes=[bg_])
            qs_ = []
            for h in range(2):
                q, bq = qrot.next()
                fw.dma("sp", q[0:68, :], nq[h, :, c * 512:(c + 1) * 512], writes=[bq])
                qs_.append((q, bq))
            pre_[c] = (g_, bg_, qs_)

        load_qc(0)
        for qc in range(16):
            if qc + 1 < 16:
                load_qc(qc + 1)
            g, bg, qs = pre_.pop(qc)
            dst = [oarot.next() for _ in range(2)]
            for h in range(2):
                acc, bacc = accr.next()
                branch(68, qs[h][0][0:68, :], qs[h][1], lambda c: kcaug[:, c * 128:(c + 1) * 128], b_kc,
                       lambda c: vcaug[:, c, :], b_vc, cmp_blocks(qc), 0.125, acc, bacc)
                gated_accum(acc, bacc, g[:, 3 * h + 0, :], bg, dst[h][0], dst[h][1], True)
            sblocks = [(kb, (kb - 4 * qc) if kb >= 4 * qc else None) for kb in range(4 * qc + 4)]
            for h in range(2):
                acc, bacc = accr.next()
                branch(68, qs[h][0][0:68, :], qs[h][1], lambda kb: big[0][0:68, kb * 128:(kb + 1) * 128],
                       bbig[0], vfs, bbig[1], sblocks, 0.125, acc, bacc,
                       extra=lambda kb: (ew[:, kb * 128:(kb + 1) * 128], selT[:, qc * 512:(qc + 1) * 512],
                                         [b_const, b_selT]))
                gated_accum(acc, bacc, g[:, 3 * h + 1, :], bg, dst[h][0], dst[h][1], False)
            wblocks = []
            for kb in range(max(0, 4 * qc - 4), 4 * qc + 4):
                o = 128 * kb - 512 * qc
                wblocks.append((kb, (o // 128) if o >= 0 else 4 + (o + 512) // 128))
            for h in range(2):
                acc, bacc = accr.next()
                branch(68, qs[h][0][0:68, :], qs[h][1], lambda kb: big[2][0:68, kb * 128:(kb + 1) * 128],
                       bbig[2], vfw, bbig[3], wblocks, 0.125, acc, bacc)
                gated_accum(acc, bacc, g[:, 3 * h + 2, :], bg, dst[h][0], dst[h][1], False)
                o, bo = obrot.next()
                fw.op("act", lambda e: e.copy(out=o[:, :], in_=dst[h][0][:, :]), reads=[dst[h][1]], writes=[bo])
                fw.dma("sp", out[2 + h, :, qc * 512:(qc + 1) * 512], o[:, :], reads=[bo], writes=[Buf()])

    compress(0)
    compress(1)
    load_kv(2, 3, mk[1], mv[1], 96)
    mla_head(0, 0, 1)
    select_phase()
    load_kv(0, 1, nks, nvs, 68)
    mla_head(1, 2, 3)
    load_kv(2, 3, nkw, nvw, 68)
    nsa_phase()
    fw.finish()
    return nc


_CB = None


def _prep_B(ao, l, P):
    global _CB
    if _CB is None:
        _CB = _consts_B()
        _CB["qa"], _CB["ka"] = _aug_tables()
    C = _CB
    qa, ka = C["qa"], C["ka"]
    maps = []
    pe = np.stack([P["nsa_pe_k"][l].T, P["nsa_pe_v"][l].T]).astype(np.float32)
    w1 = np.stack([P["nsa_w1_k"][l], P["nsa_w1_v"][l]]).astype(np.float32)
    b1 = np.stack([P["nsa_b1_k"][l], P["nsa_b1_v"][l]]).astype(np.float32)[:, :, None]
    w2 = np.stack([P["nsa_w2_k"][l], P["nsa_w2_v"][l]]).astype(np.float32)
    for b in range(2):
        cs = [ao[4 * b + i] for i in range(4)]
        cat = lambda key, axis: np.concatenate([c[key] for c in cs], axis=axis)
        qm = cat("qm", 2)
        kn = cat("kn", 2)
        kr = cat("kr", 1)
        vm = cat("vm", 0)
        nq = cat("nq", 2)
        nkf = cat("nkf", 3)
        nvt = cat("nvt", 0)
        gl = cat("gl", 1)
        tokmaj = lambda a: np.ascontiguousarray(a.reshape(64, 128, 64).transpose(1, 0, 2))
        for j in range(4):
            heads = [2 * j, 2 * j + 1]
            g = j // 2
            r0 = 2 * (j % 2)
            order = [r0, r0 + 1] + [r for r in range(4) if r not in (r0, r0 + 1)]
            m = dict(
                mq=np.ascontiguousarray(qm[heads]),
                mk=np.stack([np.concatenate([kn[h], kr], axis=0) for h in heads]),
                mv=np.stack([tokmaj(vm[:, h * 64:(h + 1) * 64]) for h in heads]),
                nq=np.stack([np.concatenate([nq[4 * g + r], qa[4 * g + r]], axis=0) for r in order]),
                nkc=np.stack([nkf[0, g], nkf[1, g]]),
                nks=np.concatenate([nkf[2, g], ka], axis=0),
                nkw=np.concatenate([nkf[3, g], ka], axis=0),
                nvs=tokmaj(nvt[:, 0, g]),
                nvw=tokmaj(nvt[:, 1, g]),
                gates=np.stack([gl[(4 * g + r) * 3 + br] for r in (r0, r0 + 1) for br in range(3)]).astype(np.float32),
                peT=pe, w1=w1, b1=b1, w2=w2,
                cmask_c=C["cmask"], ov_c=C["ov"], wide_c=C["wide"], ew_c=C["ew"], ident_c=C["ident"], kaugc_c=C["kaugc"],
            )
            maps.append({k: np.ascontiguousarray(v) for k, v in m.items()})
    return maps


def _gather_B(res):
    outs = []
    for b in range(2):
        mixT = np.zeros((1024, S), NPBF)
        for j in range(4):
            o = res[4 * b + j]["o"]
            g = j // 2
            r0 = 2 * (j % 2)
            for i, h in enumerate((2 * j, 2 * j + 1)):
                mixT[h * 64:(h + 1) * 64] = o[i]
            for i, r in enumerate((r0, r0 + 1)):
                hh = 4 * g + r
                mixT[512 + hh * 64:512 + (hh + 1) * 64] = o[2 + i]
        for i in range(4):
            outs.append(np.ascontiguousarray(mixT[:, i * 2048:(i + 1) * 2048]))
    return outs


IN_OFF = dict(cq=0, ckv=384, kr=640, nq=672, kc=1184, vc=1312, ks=1440, vs=1568, kw=1696, vw=1824, gl=1952)
TT = 2048


class _Stop(Exception):
    pass


def build_T(do_ln_in, do_C, do_A, dbg=None):
    nc = bass.Bass("TRN2", target_bir_lowering=False)
    fw = FW(nc)
    HALO = 2 if do_C else 0
    TW = TT + HALO

    def din(name, shape, dt):
        return nc.dram_tensor(name, shape, dt, kind="ExternalInput").ap()

    def dout(name, shape, dt):
        return nc.dram_tensor(name, shape, dt, kind="ExternalOutput").ap()

    def sb(name, shape, dt):
        return nc.alloc_sbuf_tensor(name, shape, dt)

    xT_d = din("xT", [D, TW], F32)
    vec_d = din("vec", [128, 264], F32)
    if do_C:
        oT_d = din("oT", [D, TW], BF16)
        flag_d = din("flag", [128, 1], F32)
        memT_d = din("memT", [D, 256], F32)
        w_out_d = din("w_out", [D, D], F32)
        wq_d = din("xa_wq", [D, D], F32)
        wkv_d = din("xa_wkv", [D, 2 * D], F32)
        wo_d = din("xa_wo", [D, D], F32)
        wup_d = din("w_up", [D, 5632], F32)
        wdn_d = din("w_down", [2816, D], F32)
    if do_A:
        w_in_d = din("w_in", [D, 1976], F32)
        wuq_d = din("w_uq", [384, 768], F32)
        wukv_d = din("w_ukv", [256, 1024], F32)
        ropeC_d = din("ropeC", [32, TT], F32)
        ropeS_d = din("ropeS", [32, TT], F32)
        qm_d = dout("qm", [8, 96, TT], BF16)
        kn_d = dout("kn", [8, 64, TT], BF16)
        kr_d = dout("kr", [32, TT], BF16)
        vm_d = dout("vm", [TT, 512], BF16)
        nq_d = dout("nq", [512, TT], BF16)
        nkf_d = dout("nkf", [4, 128, TT], BF16)
        nvt_d = dout("nvt", [TT, 256], BF16)
        gl_d = dout("gl", [24, TT], F32)
    xo_d = dout("xo", [D, TT], F32)

    V_LNIN, V_LN1, V_LN2, V_LN3, V_LNM, V_CONV, V_CQG, V_CKVG = 0, 16, 32, 48, 64, 80, 256, 259

    vec = sb("vec_s", [128, 264], F32)
    b_vec = Buf()
    fw.dma("sp", vec[:, :], vec_d, writes=[b_vec])
    ones_b = sb("ones_b", [128, 128], BF16)
    b_ones = Buf()
    fw.op("pool", lambda e: e.memset(ones_b[:, :], 1.0), writes=[b_ones])

    banks = [nc.alloc_psum_tensor("bank%d" % i, [128, 512], F32) for i in range(8)]
    bbank = [Buf(True) for _ in range(8)]
    psr = Rot([(banks[i], bbank[i]) for i in range(8)])

    xr = sb("xr", [128, 8, 512], F32)
    bxr = [Buf() for _ in range(8)]
    xb = sb("xb", [128, 8, 512], BF16)
    bxb = [Buf() for _ in range(8)]
    lnt = Rot([(sb("lnt%d" % i, [128, 512], BF16), Buf()) for i in range(4)])
    st_mean = sb("st_mean", [128, 512], F32)
    st_m2 = sb("st_m2", [128, 512], F32)
    st_rstd = sb("st_rstd", [128, 512], F32)
    st_nmr = sb("st_nmr", [128, 512], F32)
    b_mean, b_m2, b_rstd, b_nmr = Buf(), Buf(), Buf(), Buf()
    wst = Rot([(sb("wst%d" % i, [128, 8, 512], BF16), Buf()) for i in range(2)])

    def layer_norm(n, gi, src=xr, bsrc=bxr, dstb=xb, bdstb=bxb, nch=8, nfeat=1024.0, eps=1e-5):
        bs, bbs = psr.next()
        bq, bbq = psr.next()
        for c in range(nch):
            tb, btb = lnt.next()
            fw.op("act", lambda e: e.copy(out=tb[:, :n], in_=src[:, c, :n]), reads=[bsrc[c]], writes=[btb])
            fw.op("pe", lambda e: e.matmul(bs[:, :n], lhsT=ones_b[:, :], rhs=tb[:, :n], start=(c == 0),
                                            stop=(c == nch - 1)), reads=[btb, b_ones], writes=[bbs])
            tq, btq = lnt.next()
            fw.op("act", lambda e: e.activation(out=tq[:, :n], in_=src[:, c, :n], func=AF.Square),
                  reads=[bsrc[c]], writes=[btq])
            fw.op("pe", lambda e: e.matmul(bq[:, :n], lhsT=ones_b[:, :], rhs=tq[:, :n], start=(c == 0),
                                            stop=(c == nch - 1)), reads=[btq, b_ones], writes=[bbq])
        fw.op("dve", lambda e: e.tensor_scalar(out=st_mean[:, :n], in0=bs[:, :n], scalar1=1.0 / nfeat,
                                               scalar2=None, op0=ALU.mult), reads=[bbs], writes=[b_mean])
        fw.op("dve", lambda e: e.tensor_tensor(out=st_m2[:, :n], in0=st_mean[:, :n], in1=st_mean[:, :n],
                                               op=ALU.mult), reads=[b_mean], writes=[b_m2])
        fw.op("dve", lambda e: e.scalar_tensor_tensor(out=st_m2[:, :n], in0=bq[:, :n], scalar=1.0 / nfeat,
                                                      in1=st_m2[:, :n], op0=ALU.mult, op1=ALU.subtract),
              reads=[bbq, b_m2], writes=[b_m2])
        fw.op("dve", lambda e: e.tensor_scalar(out=st_m2[:, :n], in0=st_m2[:, :n], scalar1=eps,
                                               scalar2=None, op0=ALU.add), reads=[b_m2], writes=[b_m2])
        fw.op("act", lambda e: e.activation(out=st_rstd[:, :n], in_=st_m2[:, :n], func=AF.Sqrt),
              reads=[b_m2], writes=[b_rstd])
        fw.op("dve", lambda e: e.reciprocal(out=st_rstd[:, :n], in_=st_rstd[:, :n]), reads=[b_rstd],
              writes=[b_rstd])
        fw.op("dve", lambda e: e.scalar_tensor_tensor(out=st_nmr[:, :n], in0=st_mean[:, :n], scalar=-1.0,
                                                      in1=st_rstd[:, :n], op0=ALU.mult, op1=ALU.mult),
              reads=[b_mean, b_rstd], writes=[b_nmr])
        for c in range(nch):
            fw.op("dve", lambda e: e.tensor_tensor(out=src[:, c, :n], in0=src[:, c, :n], in1=st_rstd[:, :n],
                                                   op=ALU.mult), reads=[bsrc[c], b_rstd], writes=[bsrc[c]])
            fw.op("dve", lambda e: e.tensor_tensor(out=src[:, c, :n], in0=src[:, c, :n], in1=st_nmr[:, :n],
                                                   op=ALU.add), reads=[bsrc[c], b_nmr], writes=[bsrc[c]])
            fw.op("act", lambda e: e.activation(out=src[:, c, :n], in_=src[:, c, :n], func=AF.Identity,
                                                bias=vec[:, gi + 8 + c:gi + 9 + c], scale=vec[:, gi + c:gi + c + 1]),
                  reads=[bsrc[c], b_vec], writes=[bsrc[c]])
            fw.op("act", lambda e: e.copy(out=dstb[:, c, :n], in_=src[:, c, :n]), reads=[bsrc[c]],
                  writes=[bdstb[c]])

    def proj_resid(n, w_s, b_w, rhs, brhs):
        for oc in range(8):
            bk, bbk = psr.next()
            for kc in range(8):
                fw.op("pe", lambda e: e.matmul(bk[:, :n], lhsT=w_s[:, kc, oc * 128:(oc + 1) * 128],
                                                rhs=rhs[:, kc, :n], start=(kc == 0), stop=(kc == 7)),
                      reads=[b_w, brhs[kc]], writes=[bbk])
            fw.op("dve", lambda e: e.scalar_tensor_tensor(out=xr[:, oc, :n], in0=xr[:, oc, :n], scalar=ALPHA,
                                                          in1=bk[:, :n], op0=ALU.mult, op1=ALU.add),
                  reads=[bxr[oc], bbk], writes=[bxr[oc]])

    def load_w(dst3, src2, bdst, q="pool"):
        fw.dma(q, dst3, src2.rearrange("(kc p) n -> p kc n", p=128), writes=[bdst])

    if do_C:
        wout_s = sb("wout_s", [128, 8, 1024], BF16)
        wq_s = sb("wq_s", [128, 8, 1024], BF16)
        wo_s = sb("wo_s", [128, 8, 1024], BF16)
        b_wout, b_wq, b_wo = Buf(), Buf(), Buf()
        load_w(wout_s[:, :, :], w_out_d, b_wout)
        load_w(wq_s[:, :, :], wq_d, b_wq)
        load_w(wo_s[:, :, :], wo_d, b_wo)
        flag = sb("flag_s", [128, 1], F32)
        b_flag = Buf()
        fw.dma("sp", flag[:, :], flag_d, writes=[b_flag])
        hb = sb("hb", [128, 22, 512], BF16)
        bhb = [Buf() for _ in range(22)]
        ot = hb[:, 0:8, :]
        bot = bhb[0:8]
        qb = hb[:, 8:16, :]
        bqb = bhb[8:16]
        oab = hb[:, 0:8, :]
        boab = bhb[0:8]
        kT = sb("kT", [128, 8, 256], BF16)
        b_kT = Buf()
        vmem = sb("vmem", [128, 2, 1024], BF16)
        b_vmem = Buf()
        carry = sb("carry", [128, 44, 2], F32)
        bcar = [Buf() for _ in range(44)]
        ptr = Rot([(sb("xpt%d" % i, [128, 512], BF16), Buf()) for i in range(4)])
        rdx = Rot([(sb("rdx%d" % i, [128, 512], F32), Buf()) for i in range(2)])
        uer = Rot([(sb("ue%d" % i, [128, 516], F32), Buf()) for i in range(2)])
        yr_items = [(sb("y%d" % i, [128, 512], F32), Buf()) for i in range(3)]
        yr = Rot(yr_items)
        sgr = Rot([(sb("sg%d" % i, [128, 512], F32), Buf()) for i in range(2)])
        wdn = Rot([(sb("wdn%d" % i, [128, 2, 1024], BF16), Buf()) for i in range(2)])

        mr = xr[:, :, 0:256]
        bmr = bxr
        mb = xb[:, :, 0:256]
        bmb = bxb
        fw.dma("sp", mr, memT_d.rearrange("(c p) t -> p c t", p=128), writes=bmr)
        layer_norm(256, V_LNM, src=mr, bsrc=bmr, dstb=mb, bdstb=bmb)
        for piece in range(4):
            w, bw = wst.next()
            load_w(w[:, :, :], wkv_d[:, piece * 512:(piece + 1) * 512], bw)
            if piece < 2:
                for o4 in range(4):
                    oc = piece * 4 + o4
                    bk, bbk = psr.next()
                    for kc in range(8):
                        fw.op("pe", lambda e: e.matmul(bk[:, :256], lhsT=w[:, kc, o4 * 128:(o4 + 1) * 128],
                                                        rhs=mb[:, kc, :], start=(kc == 0), stop=(kc == 7)),
                              reads=[bw, bmb[kc]], writes=[bbk])
                    fw.op("act", lambda e: e.copy(out=kT[:, oc, :], in_=bk[:, :256]), reads=[bbk], writes=[b_kT])
            else:
                half = piece - 2
                for mc in range(2):
                    bk, bbk = psr.next()
                    for kc in range(8):
                        fw.op("pe", lambda e: e.matmul(bk[:, :], lhsT=mb[:, kc, mc * 128:(mc + 1) * 128],
                                                        rhs=w[:, kc, :], start=(kc == 0), stop=(kc == 7)),
                              reads=[bw, bmb[kc]], writes=[bbk])
                    fw.op("act", lambda e: e.copy(out=vmem[:, mc, half * 512:(half + 1) * 512], in_=bk[:, :]),
                          reads=[bbk], writes=[b_vmem])

        def conv(bk, bbk, fc, n):
            ue, bue = uer.next()
            fw.op("dve", lambda e: e.tensor_copy(ue[:, 0:2], carry[:, fc, :]), reads=[bcar[fc]], writes=[bue])
            fw.op("act", lambda e: e.copy(out=ue[:, 2:2 + n], in_=bk[:, :n]), reads=[bbk], writes=[bue])
            y, by = yr.next()
            cv = V_CONV + fc
            fw.op("act", lambda e: e.activation(out=y[:, :n], in_=ue[:, 2:2 + n], func=AF.Identity,
                                                bias=vec[:, cv + 132:cv + 133], scale=vec[:, cv + 88:cv + 89]),
                  reads=[bue, b_vec], writes=[by])
            fw.op("dve", lambda e: e.scalar_tensor_tensor(out=y[:, :n], in0=ue[:, 1:1 + n],
                                                          scalar=vec[:, cv + 44:cv + 45], in1=y[:, :n],
                                                          op0=ALU.mult, op1=ALU.add),
                  reads=[bue, by, b_vec], writes=[by])
            fw.op("dve", lambda e: e.scalar_tensor_tensor(out=y[:, :n], in0=ue[:, 0:n],
                                                          scalar=vec[:, cv:cv + 1], in1=y[:, :n],
                                                          op0=ALU.mult, op1=ALU.add),
                  reads=[bue, by, b_vec], writes=[by])
            fw.op("dve", lambda e: e.tensor_copy(carry[:, fc, :], ue[:, n:n + 2]), reads=[bue], writes=[bcar[fc]])
            return y, by

        cur_halo = [True]

        def ck(k):
            if dbg == k + (0 if cur_halo[0] else 10):
                raise _Stop()

        def c_tile(n, col0, is_halo):
            cur_halo[0] = is_halo
            ck(1)
            fw.dma("sp", ot[:, :, :n], oT_d[:, col0:col0 + n].rearrange("(c p) t -> p c t", p=128), writes=bot)
            fw.dma("sp", xr[:, :, :n], xT_d[:, col0:col0 + n].rearrange("(c p) t -> p c t", p=128), writes=bxr)
            ck(2)
            proj_resid(n, wout_s, b_wout, ot, bot)
            ck(3)
            layer_norm(n, V_LN1)
            ck(4)
            for oc in range(8):
                bk, bbk = psr.next()
                for kc in range(8):
                    fw.op("pe", lambda e: e.matmul(bk[:, :n], lhsT=wq_s[:, kc, oc * 128:(oc + 1) * 128],
                                                    rhs=xb[:, kc, :n], start=(kc == 0), stop=(kc == 7)),
                          reads=[b_wq, bxb[kc]], writes=[bbk])
                fw.op("act", lambda e: e.copy(out=qb[:, oc, :n], in_=bk[:, :n]), reads=[bbk], writes=[bqb[oc]])
            for h in range(4):
                pts_ = []
                for mc in range(2):
                    bk, bbk = psr.next()
                    for half in range(2):
                        fw.op("pe", lambda e: e.matmul(bk[:, :n], lhsT=kT[:, 2 * h + half, mc * 128:(mc + 1) * 128],
                                                        rhs=qb[:, 2 * h + half, :n], start=(half == 0), stop=(half == 1)),
                              reads=[b_kT, bqb[2 * h + half]], writes=[bbk])
                    p, bp = ptr.next()
                    fw.op("act", lambda e: e.activation(out=p[:, :n], in_=bk[:, :n], func=AF.Exp, scale=1.0 / 16.0),
                          reads=[bbk], writes=[bp])
                    pts_.append((p, bp))
                bd, bbd = psr.next()
                for mc in range(2):
                    fw.op("pe", lambda e: e.matmul(bd[:, :n], lhsT=ones_b[:, :], rhs=pts_[mc][0][:, :n],
                                                    start=(mc == 0), stop=(mc == 1)),
                          reads=[b_ones, pts_[mc][1]], writes=[bbd])
                r, br = rdx.next()
                fw.op("dve", lambda e: e.reciprocal(out=r[:, :n], in_=bd[:, :n]), reads=[bbd], writes=[br])
                for dvh in range(2):
                    bk, bbk = psr.next()
                    for mc in range(2):
                        fw.op("pe", lambda e: e.matmul(bk[:, :n], lhsT=vmem[:, mc, h * 256 + dvh * 128:h * 256 + dvh * 128 + 128],
                                                        rhs=pts_[mc][0][:, :n], start=(mc == 0), stop=(mc == 1)),
                              reads=[b_vmem, pts_[mc][1]], writes=[bbk])
                    fw.op("dve", lambda e: e.tensor_tensor(out=oab[:, 2 * h + dvh, :n], in0=bk[:, :n], in1=r[:, :n],
                                                           op=ALU.mult),
                          reads=[bbk, br], writes=[boab[2 * h + dvh]])
            ck(5)
            proj_resid(n, wo_s, b_wo, oab, boab)
            layer_norm(n, V_LN2)
            ck(6)
            for grp in range(11):
                w, bw = wst.next()
                fw.dma("pool", w[:, :, 0:256], wup_d[:, grp * 256:(grp + 1) * 256].rearrange("(kc p) n -> p kc n", p=128),
                       writes=[bw])
                fw.dma("pool", w[:, :, 256:512],
                       wup_d[:, 2816 + grp * 256:2816 + (grp + 1) * 256].rearrange("(kc p) n -> p kc n", p=128),
                       writes=[bw])
                for jj in range(2):
                    j = 2 * grp + jj
                    ba, bba = psr.next()
                    for kc in range(8):
                        fw.op("pe", lambda e: e.matmul(ba[:, :n], lhsT=w[:, kc, jj * 128:(jj + 1) * 128],
                                                        rhs=xb[:, kc, :n], start=(kc == 0), stop=(kc == 7)),
                              reads=[bw, bxb[kc]], writes=[bba])
                    bg_, bbg = psr.next()
                    for kc in range(8):
                        fw.op("pe", lambda e: e.matmul(bg_[:, :n], lhsT=w[:, kc, 256 + jj * 128:256 + (jj + 1) * 128],
                                                        rhs=xb[:, kc, :n], start=(kc == 0), stop=(kc == 7)),
                              reads=[bw, bxb[kc]], writes=[bbg])
                    if is_halo:
                        fw.op("dve", lambda e: e.tensor_scalar(out=carry[:, j, :], in0=ba[:, 0:2], scalar1=flag[:, 0:1],
                                                               scalar2=None, op0=ALU.mult),
                              reads=[bba, b_flag], writes=[bcar[j]])
                        fw.op("dve", lambda e: e.tensor_scalar(out=carry[:, 22 + j, :], in0=bg_[:, 0:2],
                                                               scalar1=flag[:, 0:1], scalar2=None, op0=ALU.mult),
                              reads=[bbg, b_flag], writes=[bcar[22 + j]])
                    else:
                        ya, bya = conv(ba, bba, j, n)
                        yg, byg = conv(bg_, bbg, 22 + j, n)
                        sg, bsg = sgr.next()
                        fw.op("act", lambda e: e.activation(out=sg[:, :n], in_=yg[:, :n], func=AF.Silu),
                              reads=[byg], writes=[bsg])
                        fw.op("dve", lambda e: e.tensor_tensor(out=hb[:, j, :n], in0=ya[:, :n], in1=sg[:, :n],
                                                               op=ALU.mult),
                              reads=[bya, bsg], writes=[bhb[j]])
            if is_halo:
                ck(7)
                return
            ck(8)
            for grp in range(11):
                w, bw = wdn.next()
                fw.dma("pool", w[:, :, :], wdn_d[grp * 256:(grp + 1) * 256, :].rearrange("(jj p) n -> p jj n", p=128),
                       writes=[bw])
                for jj in range(2):
                    j = 2 * grp + jj
                    for oc in range(8):
                        fw.op("pe", lambda e: e.matmul(banks[oc][:, :n], lhsT=w[:, jj, oc * 128:(oc + 1) * 128],
                                                        rhs=hb[:, j, :n], start=(j == 0), stop=(j == 21)),
                              reads=[bw, bhb[j]], writes=[bbank[oc]])
            for oc in range(8):
                fw.op("dve", lambda e: e.scalar_tensor_tensor(out=xr[:, oc, :n], in0=xr[:, oc, :n], scalar=ALPHA,
                                                              in1=banks[oc][:, :n], op0=ALU.mult, op1=ALU.add),
                      reads=[bxr[oc], bbank[oc]], writes=[bxr[oc]])
            layer_norm(n, V_LN3)

    if do_A:
        wuq_s = sb("wuq_s", [128, 3, 768], BF16)
        wuqr_s = sb("wuqr_s", [128, 3, 768], BF16)
        wukv_s = sb("wukv_s", [128, 2, 1024], BF16)
        b_wuq, b_wuqr, b_wukv = Buf(), Buf(), Buf()
        load_w(wuq_s[:, :, :], wuq_d, b_wuq)
        load_w(wukv_s[:, :, :], wukv_d, b_wukv)
        for kc in range(3):
            fw.op("pool", lambda e: e.tensor_scalar(out=wuq_s[:, kc, :], in0=wuq_s[:, kc, :],
                                                    scalar1=vec[:, V_CQG + kc:V_CQG + kc + 1], scalar2=None, op0=ALU.mult),
                  reads=[b_wuq, b_vec], writes=[b_wuq])
        for kc in range(2):
            fw.op("pool", lambda e: e.tensor_scalar(out=wukv_s[:, kc, :], in0=wukv_s[:, kc, :],
                                                    scalar1=vec[:, V_CKVG + kc:V_CKVG + kc + 1], scalar2=None, op0=ALU.mult),
                  reads=[b_wukv, b_vec], writes=[b_wukv])
        fw.op("pool", lambda e: e.memset(wuqr_s[:, :, :], 0.0), writes=[b_wuqr])
        w4 = wuq_s[:, :, :].rearrange("p k (h c) -> p k h c", c=96)
        r4 = wuqr_s[:, :, :].rearrange("p k (h c) -> p k h c", c=96)
        for kc in range(3):
            fw.op("pool", lambda e: e.tensor_scalar(out=r4[:, kc, :, 64:80], in0=w4[:, kc, :, 80:96], scalar1=-1.0,
                                                    scalar2=None, op0=ALU.mult), reads=[b_wuq], writes=[b_wuqr])
            fw.op("pool", lambda e: e.tensor_copy(r4[:, kc, :, 80:96], w4[:, kc, :, 64:80]), reads=[b_wuq],
                  writes=[b_wuqr])
        wkrr = sb("wkrr", [128, 8, 32], BF16)
        b_wkrr = Buf()
        craw = sb("craw", [128, 5, 512], F32)
        bcraw = [Buf() for _ in range(5)]
        cn = sb("cn", [128, 5, 512], BF16)
        bcn = [Buf() for _ in range(5)]
        st_r = sb("st_r", [128, 512], F32)
        b_str = Buf()
        tabq = sb("tabq", [128, 2, 512], F32)
        tabk = tabq[0:32, :, :]
        b_tab = Buf()
        osb = Rot([(sb("osb%d" % i, [128, 512], BF16), Buf()) for i in range(4)])
        if do_C:
            rt = Rot(yr_items)
        else:
            rt = Rot([(sb("rt%d" % i, [128, 512], F32), Buf()) for i in range(4)])
        glo = Rot([(sb("glo%d" % i, [24, 512], F32), Buf()) for i in range(1)])

        def evac_out(bk, bbk, rows, n, dst, eng="act"):
            o, bo = osb.next()
            if eng == "act":
                fw.op("act", lambda e: e.copy(out=o[0:rows, :n], in_=bk[0:rows, :n]), reads=[bbk], writes=[bo])
            else:
                fw.op("dve", lambda e: e.tensor_copy(o[0:rows, :n], bk[0:rows, :n]), reads=[bbk], writes=[bo])
            fw.dma("sp", dst, o[0:rows, :n], reads=[bo], writes=[Buf()])

        def rms(chunks, nfeat, n):
            bs, bbs = psr.next()
            for i, c in enumerate(chunks):
                tq, btq = lnt.next()
                fw.op("act", lambda e: e.activation(out=tq[:, :n], in_=craw[:, c, :n], func=AF.Square),
                      reads=[bcraw[c]], writes=[btq])
                fw.op("pe", lambda e: e.matmul(bs[:, :n], lhsT=ones_b[:, :], rhs=tq[:, :n], start=(i == 0),
                                                stop=(i == len(chunks) - 1)), reads=[btq, b_ones], writes=[bbs])
            fw.op("dve", lambda e: e.tensor_scalar(out=st_r[:, :n], in0=bs[:, :n], scalar1=1.0 / nfeat, scalar2=1e-6,
                                                   op0=ALU.mult, op1=ALU.add), reads=[bbs], writes=[b_str])
            fw.op("act", lambda e: e.activation(out=st_r[:, :n], in_=st_r[:, :n], func=AF.Sqrt),
                  reads=[b_str], writes=[b_str])
            fw.op("dve", lambda e: e.reciprocal(out=st_r[:, :n], in_=st_r[:, :n]), reads=[b_str], writes=[b_str])
            for c in chunks:
                fw.op("dve", lambda e: e.tensor_tensor(out=cn[:, c, :n], in0=craw[:, c, :n], in1=st_r[:, :n],
                                                       op=ALU.mult), reads=[bcraw[c], b_str], writes=[bcn[c]])

        def a_tile(t0):
            n = 512
            fw.dma("sp", tabq[64:96, 0, :], ropeC_d[:, t0:t0 + n], writes=[b_tab])
            fw.dma("sp", tabq[64:96, 1, :], ropeS_d[:, t0:t0 + n], writes=[b_tab])
            fw.dma("sp", tabk[:, 0, :], ropeC_d[:, t0:t0 + n], writes=[b_tab])
            fw.dma("sp", tabk[:, 1, :], ropeS_d[:, t0:t0 + n], writes=[b_tab])

            def hproj(w, bw, c0, m):
                bk, bbk = psr.next()
                for kc in range(8):
                    fw.op("pe", lambda e: e.matmul(bk[0:m, :n], lhsT=w[:, kc, c0:c0 + m], rhs=xb[:, kc, :n],
                                                    start=(kc == 0), stop=(kc == 7)), reads=[bw, bxb[kc]], writes=[bbk])
                return bk, bbk

            w, bw = wst.next()
            load_w(w[:, :, 0:384], w_in_d[:, 0:384], bw)
            for c in range(3):
                bk, bbk = hproj(w, bw, c * 128, 128)
                fw.op("act", lambda e: e.copy(out=craw[:, c, :n], in_=bk[:, :n]), reads=[bbk], writes=[bcraw[c]])
            rms([0, 1, 2], 384.0, n)
            w, bw = wst.next()
            load_w(w[:, :, 0:288], w_in_d[:, 384:672], bw)
            for c in range(2):
                bk, bbk = hproj(w, bw, c * 128, 128)
                fw.op("act", lambda e: e.copy(out=craw[:, 3 + c, :n], in_=bk[:, :n]), reads=[bbk], writes=[bcraw[3 + c]])
            rms([3, 4], 256.0, n)
            fw.op("dve", lambda e: e.tensor_scalar(out=wkrr[:, :, 0:16], in0=w[:, :, 272:288], scalar1=-1.0,
                                                   scalar2=None, op0=ALU.mult), reads=[bw], writes=[b_wkrr])
            fw.op("dve", lambda e: e.tensor_copy(wkrr[:, :, 16:32], w[:, :, 256:272]), reads=[bw], writes=[b_wkrr])
            bk, bbk = hproj(w, bw, 256, 32)
            bk2, bbk2 = hproj(wkrr, b_wkrr, 0, 32)
            t1, bt1 = rt.next()
            t2, bt2 = rt.next()
            fw.op("dve", lambda e: e.tensor_tensor(out=t1[0:32, :n], in0=bk[0:32, :n], in1=tabk[:, 0, :n], op=ALU.mult),
                  reads=[bbk, b_tab], writes=[bt1])
            fw.op("dve", lambda e: e.tensor_tensor(out=t2[0:32, :n], in0=bk2[0:32, :n], in1=tabk[:, 1, :n], op=ALU.mult),
                  reads=[bbk2, b_tab], writes=[bt2])
            o, bo = osb.next()
            fw.op("dve", lambda e: e.tensor_tensor(out=o[0:32, :n], in0=t1[0:32, :n], in1=t2[0:32, :n], op=ALU.add),
                  reads=[bt1, bt2], writes=[bo])
            fw.dma("sp", kr_d[:, t0:t0 + n], o[0:32, :n], reads=[bo], writes=[Buf()])
            for h in range(8):
                bk, bbk = psr.next()
                bk2, bbk2 = psr.next()
                for kc in range(3):
                    fw.op("pe", lambda e: e.matmul(bk[0:96, :n], lhsT=wuq_s[:, kc, h * 96:(h + 1) * 96], rhs=cn[:, kc, :n],
                                                    start=(kc == 0), stop=(kc == 2)), reads=[b_wuq, bcn[kc]], writes=[bbk])
                for kc in range(3):
                    fw.op("pe", lambda e: e.matmul(bk2[0:96, :n], lhsT=wuqr_s[:, kc, h * 96:(h + 1) * 96], rhs=cn[:, kc, :n],
                                                    start=(kc == 0), stop=(kc == 2)), reads=[b_wuqr, bcn[kc]], writes=[bbk2])
                o, bo = osb.next()
                fw.op("act", lambda e: e.copy(out=o[0:64, :n], in_=bk[0:64, :n]), reads=[bbk], writes=[bo])
                t1, bt1 = rt.next()
                t2, bt2 = rt.next()
                fw.op("dve", lambda e: e.tensor_tensor(out=t1[64:96, :n], in0=bk[64:96, :n], in1=tabq[64:96, 0, :n],
                                                       op=ALU.mult), reads=[bbk, b_tab], writes=[bt1])
                fw.op("dve", lambda e: e.tensor_tensor(out=t2[64:96, :n], in0=bk2[64:96, :n], in1=tabq[64:96, 1, :n],
                                                       op=ALU.mult), reads=[bbk2, b_tab], writes=[bt2])
                fw.op("dve", lambda e: e.tensor_tensor(out=o[64:96, :n], in0=t1[64:96, :n], in1=t2[64:96, :n],
                                                       op=ALU.add), reads=[bt1, bt2, bo], writes=[bo])
                fw.dma("sp", qm_d[h, :, t0:t0 + n], o[0:96, :n], reads=[bo], writes=[Buf()])
            for h in range(8):
                bk, bbk = psr.next()
                for kc in range(2):
                    fw.op("pe", lambda e: e.matmul(bk[0:64, :n], lhsT=wukv_s[:, kc, h * 128:h * 128 + 64],
                                                    rhs=cn[:, 3 + kc, :n], start=(kc == 0), stop=(kc == 1)),
                          reads=[b_wukv, bcn[3 + kc]], writes=[bbk])
                evac_out(bk, bbk, 64, n, kn_d[h, :, t0:t0 + n], eng=("act" if h % 2 else "dve"))
            wv4 = wukv_s[:, :, :].rearrange("p k (h c) -> p k h c", c=128)
            for tb in range(4):
                bk, bbk = psr.next()
                for kc in range(2):
                    fw.op("pe", lambda e: e.matmul(bk[:, :].rearrange("p (h c) -> p h c", c=64),
                                                    lhsT=cn[:, 3 + kc, tb * 128:(tb + 1) * 128],
                                                    rhs=wv4[:, kc, :, 64:128], start=(kc == 0), stop=(kc == 1)),
                          reads=[b_wukv, bcn[3 + kc]], writes=[bbk])
                evac_out(bk, bbk, 128, 512, vm_d[t0 + tb * 128:t0 + (tb + 1) * 128, :], eng="dve")
            w, bw = wst.next()
            load_w(w[:, :, 0:512], w_in_d[:, 672:1184], bw)
            for c in range(4):
                bk, bbk = hproj(w, bw, c * 128, 128)
                evac_out(bk, bbk, 128, n, nq_d[c * 128:(c + 1) * 128, t0:t0 + n], eng=("act" if c % 2 else "dve"))
            w, bw = wst.next()
            load_w(w[:, :, 0:512], w_in_d[:, 1184:1696], bw)
            for c in range(3):
                bk, bbk = hproj(w, bw, c * 128, 128)
                evac_out(bk, bbk, 128, n, nkf_d[c, :, t0:t0 + n], eng=("act" if c % 2 else "dve"))
            wC, bwC = w, bw
            w, bw = wst.next()
            load_w(w[:, :, 0:280], w_in_d[:, 1696:1976], bw)
            bk, bbk = hproj(w, bw, 0, 128)
            evac_out(bk, bbk, 128, n, nkf_d[3, :, t0:t0 + n])
            bk, bbk = hproj(w, bw, 256, 24)
            g, bg = glo.next()
            fw.op("act", lambda e: e.activation(out=g[:, :n], in_=bk[0:24, :n], func=AF.Sigmoid), reads=[bbk], writes=[bg])
            fw.dma("sp", gl_d[:, t0:t0 + n], g[:, :n], reads=[bg], writes=[Buf()])
            for tb in range(4):
                bk, bbk = psr.next()
                for kc in range(8):
                    fw.op("pe", lambda e: e.matmul(bk[:, 0:128], lhsT=xb[:, kc, tb * 128:(tb + 1) * 128],
                                                    rhs=wC[:, kc, 384:512], start=(kc == 0), stop=(kc == 7)),
                          reads=[bwC, bxb[kc]], writes=[bbk])
                for kc in range(8):
                    fw.op("pe", lambda e: e.matmul(bk[:, 128:256], lhsT=xb[:, kc, tb * 128:(tb + 1) * 128],
                                                    rhs=w[:, kc, 128:256], start=(kc == 0), stop=(kc == 7)),
                          reads=[bw, bxb[kc]], writes=[bbk])
                o, bo = osb.next()
                fw.op("dve", lambda e: e.tensor_copy(o[:, 0:256], bk[:, 0:256]), reads=[bbk], writes=[bo])
                fw.dma("sp", nvt_d[t0 + tb * 128:t0 + (tb + 1) * 128, :], o[:, 0:256], reads=[bo], writes=[Buf()])

    def _program():
        if do_C:
            c_tile(2, 0, True)
        for ti in range(TT // 512):
            t0 = ti * 512
            if do_C:
                c_tile(512, HALO + t0, False)
            else:
                fw.dma("sp", xr[:, :, :], xT_d[:, t0:t0 + 512].rearrange("(c p) t -> p c t", p=128), writes=bxr)
                if do_ln_in:
                    layer_norm(512, V_LNIN)
            fw.dma("sp", xo_d[:, t0:t0 + 512].rearrange("(c p) t -> p c t", p=128), xr[:, :, :], reads=bxr,
                   writes=[Buf()])
            if do_A:
                a_tile(t0)

    try:
        _program()
    except _Stop:
        pass
    fw.finish()
    return nc


def _colvec(v):
    v = np.asarray(v, np.float32)
    return v.reshape(-1, 128).T


def _rope_tabs():
    inv = (10000.0 ** (-np.arange(0, 32, 2, dtype=np.float32) / np.float32(32))).astype(np.float32)
    ang = np.arange(S, dtype=np.float32)[:, None] * inv[None, :]
    c = np.cos(ang).astype(np.float32).T
    s = np.sin(ang).astype(np.float32).T
    return np.concatenate([c, c], 0), np.concatenate([s, s], 0)


def _prep_T(P, lC, lA, xT_list, oT_list, do_ln_in):
    vec = np.zeros((128, 264), np.float32)
    if do_ln_in:
        vec[:, 0:8] = _colvec(P["ln_in_g"])
        vec[:, 8:16] = _colvec(P["ln_in_b"])
    if lC is not None:
        for base, nm in ((16, "ln1"), (32, "ln2"), (48, "ln3")):
            vec[:, base:base + 8] = _colvec(P[nm + "_g"][lC])
            vec[:, base + 8:base + 16] = _colvec(P[nm + "_b"][lC])
        vec[:, 64:72] = _colvec(P["ln_mem_g"])
        vec[:, 72:80] = _colvec(P["ln_mem_b"])
        for k in range(3):
            vec[:, 80 + 44 * k:80 + 44 * (k + 1)] = _colvec(P["ffn_conv_w"][lC][k])
        vec[:, 80 + 132:80 + 176] = _colvec(P["ffn_conv_b"][lC])
    if lA is not None:
        vec[:, 256:259] = _colvec(P["mla_cq_g"][lA])
        vec[:, 259:261] = _colvec(P["mla_ckv_g"][lA])
        rc, rs = _rope_tabs()
    maps = []
    f32 = lambda a: np.ascontiguousarray(a, dtype=np.float32)
    for c in range(NCORES):
        b, i = c // 4, c % 4
        m = dict(vec=vec)
        if lC is not None:
            if i == 0:
                halo_x = np.zeros((D, 2), np.float32)
                halo_o = np.zeros((D, 2), NPBF)
            else:
                halo_x = xT_list[c - 1][:, -2:]
                halo_o = oT_list[c - 1][:, -2:]
            m["xT"] = np.ascontiguousarray(np.concatenate([halo_x, xT_list[c]], axis=1))
            m["oT"] = np.ascontiguousarray(np.concatenate([halo_o, oT_list[c]], axis=1))
            m["flag"] = np.full((128, 1), 0.0 if i == 0 else 1.0, np.float32)
            m["memT"] = f32(P["mem"][b].T)
            m["w_out"] = f32(P["w_out"][lC])
            m["xa_wq"] = f32(P["xa_wq"][lC])
            m["xa_wkv"] = f32(P["xa_wkv"][lC])
            m["xa_wo"] = f32(P["xa_wo"][lC])
            m["w_up"] = f32(P["ffn_w_up"][lC])
            m["w_down"] = f32(P["ffn_w_down"][lC])
        else:
            m["xT"] = xT_list[c]
        if lA is not None:
            m["w_in"] = f32(P["w_in"][lA])
            m["w_uq"] = f32(P["mla_w_uq"][lA])
            m["w_ukv"] = f32(P["mla_w_ukv"][lA])
            m["ropeC"] = np.ascontiguousarray(rc[:, i * TT:(i + 1) * TT])
            m["ropeS"] = np.ascontiguousarray(rs[:, i * TT:(i + 1) * TT])
        maps.append(m)
    return maps


def _A_outs(res):
    ao = []
    for r in res:
        ao.append(dict(qm=r["qm"], kn=r["kn"], kr=r["kr"], vm=r["vm"],
                       nq=r["nq"].reshape(8, 64, TT), nkf=r["nkf"].reshape(4, 2, 64, TT),
                       nvt=r["nvt"].reshape(TT, 2, 2, 64), gl=r["gl"]))
    return ao


_PROGS = {}


def _prog(key):
    if key not in _PROGS:
        if key == "B":
            _PROGS[key] = build_B()
        else:
            _PROGS[key] = build_T(*key)
    return _PROGS[key]


def _run(key, maps):
    res = run_bass_kernel_spmd(_prog(key), maps, core_ids=list(range(NCORES)))
    return res.results


def kernel(**inputs):
    P = {k: np.asarray(v) for k, v in inputs.items()}
    x = P["x"]
    xT = [np.ascontiguousarray(x[c // 4, (c % 4) * TT:(c % 4 + 1) * TT].T, dtype=np.float32) for c in range(NCORES)]
    res = _run((True, False, True), _prep_T(P, None, 0, xT, None, True))
    xn = [r["xo"] for r in res]
    ao = _A_outs(res)
    out = np.zeros((2, S, D), np.float32)
    for l in range(2):
        resB = _run("B", _prep_B(ao, l, P))
        oT = _gather_B(resB)
        if l == 0:
            res = _run((False, True, True), _prep_T(P, 0, 1, xn, oT, False))
            xn = [r["xo"] for r in res]
            ao = _A_outs(res)
        else:
            res = _run((False, True, False), _prep_T(P, 1, None, xn, oT, False))
            for c in range(NCORES):
                out[c // 4, (c % 4) * TT:(c % 4 + 1) * TT, :] = res[c]["xo"].T
    return out
```

```python
import math
import numpy as np
import ml_dtypes
import concourse.bass as bass
import concourse.mybir as mybir
from concourse.bass_utils import run_bass_kernel_spmd

F32 = mybir.dt.float32
BF16 = mybir.dt.bfloat16
AF = mybir.ActivationFunctionType
ALU = mybir.AluOpType
NPBF = ml_dtypes.bfloat16

S = 8192
D = 1024
NCORES = 8
ALPHA = (2.0 * 2) ** 0.25
MLA_SCALE = 96 ** -0.5


class Buf:
    __slots__ = ("w", "r", "psum")

    def __init__(self, psum=False):
        self.w = []
        self.r = {}
        self.psum = psum


class Rot:
    def __init__(self, items):
        self.items = items
        self.i = 0

    def next(self):
        it = self.items[self.i]
        self.i = (self.i + 1) % len(self.items)
        return it


class FW:
    COMPUTE = ("pe", "act", "dve", "pool")

    def __init__(self, nc, n_dma_sems=32):
        self.nc = nc
        self.E = {"pe": nc.tensor, "act": nc.scalar, "dve": nc.vector,
                  "pool": nc.gpsimd, "sp": nc.sync}
        self.sem = {}
        self.cnt = {}
        for e in self.COMPUTE:
            self.sem[e] = nc.alloc_semaphore(name="s_" + e)
            self.cnt[e] = 0
        self.waited = {e: {} for e in self.E}
        self.dsem = []
        for i in range(n_dma_sems):
            key = "d%d" % i
            self.sem[key] = nc.alloc_semaphore(name="s_" + key)
            self.cnt[key] = 0
            self.dsem.append(key)
        self.dnext = 0

    def _wait(self, eng, key, val):
        if val <= 0 or self.waited[eng].get(key, 0) >= val:
            return
        self.E[eng].wait_ge(self.sem[key], val)
        self.waited[eng][key] = val

    def _deps(self, eng, reads, writes):
        deps = {}
        for b in reads:
            for (k, v) in b.w:
                if deps.get(k, 0) < v:
                    deps[k] = v
            if b.psum:
                for k, v in b.r.items():
                    if k != eng and deps.get(k, 0) < v:
                        deps[k] = v
        for b in writes:
            for (k, v) in b.w:
                if deps.get(k, 0) < v:
                    deps[k] = v
            for k, v in b.r.items():
                if deps.get(k, 0) < v:
                    deps[k] = v
        for k, v in deps.items():
            if k == eng:
                if eng == "pe":
                    continue
                if v <= self.cnt[eng] - 3:
                    continue
            self._wait(eng, k, v)

    def _mark(self, tok, reads, writes):
        k, v = tok
        for b in reads:
            if b.r.get(k, 0) < v:
                b.r[k] = v
        for b in writes:
            b.w = [tok]
            b.r = {}

    def op(self, eng, fn, reads=(), writes=()):
        self._deps(eng, reads, writes)
        ins = fn(self.E[eng])
        self.cnt[eng] += 1
        ins.then_inc(self.sem[eng], 1)
        self._mark((eng, self.cnt[eng]), reads, writes)
        return ins

    def dma(self, q, out, in_, reads=(), writes=(), **kw):
        self._deps(q, reads, writes)
        key = self.dsem[self.dnext]
        self.dnext = (self.dnext + 1) % len(self.dsem)
        self._wait(q, key, self.cnt[key])
        ins = self.E[q].dma_start(out=out, in_=in_, **kw)
        self.cnt[key] += 16
        ins.then_inc(self.sem[key], 16)
        self._mark((key, self.cnt[key]), reads, writes)

    def finish(self, eng="sp"):
        for key in self.dsem:
            self._wait(eng, key, self.cnt[key])


def _consts_B():
    i = np.arange(128)[:, None]
    u = np.arange(512)[None, :]
    masks = np.zeros((13, 128, 512), np.float32)
    for m in range(4):
        masks[m] = (u - i - 128 * m >= 0)
    for m in range(4):
        o = -512 + 128 * m
        masks[4 + m] = (u - i - o < 512)
    for m in range(5):
        masks[8 + m] = (16 * i + 31 - 512 * m <= u)
    n = np.arange(512)
    cs = 16 * n
    j = np.arange(128)
    ov = np.clip(np.minimum(cs[:, None] + 32, 64 * j[None, :] + 64)
                 - np.maximum(cs[:, None], 64 * j[None, :]), 0, None).astype(np.float32) / 32.0
    ov[511] = 0.0
    ovo = np.concatenate([ov, np.ones((512, 1), np.float32)], axis=1)
    ovo[511] = 0.0
    ovo = ovo.reshape(4, 128, 129)
    p = np.arange(128)[:, None]
    hp = (p >= 64).astype(np.int64)
    m = np.arange(256)[None, :] - 127
    vw = (m <= hp).astype(np.float32)
    cw = 2e4 * (m == hp) + 4e4 * (m == hp - 1) + (vw - 1.0)
    bf0 = np.zeros((128, 128), np.float32)
    bf0[:, 0] = 1e4
    wide = np.concatenate([vw, cw.astype(np.float32), bf0], axis=1)
    ew = (np.arange(128)[:, None] == (np.arange(S)[None, :] // 64)).astype(np.float32)
    ident = np.eye(128, dtype=np.float32)
    ce = 16 * np.arange(512) + 31
    kaugc = np.stack([np.ones(512), np.ones(512), 128.0 * (ce // 128), (ce % 128).astype(np.float64)]).astype(np.float32)
    return dict(cmask=((masks - 1.0) * 30000.0).astype(NPBF), ov=ovo.astype(NPBF), wide=wide.astype(np.float32),
                ew=ew.astype(NPBF), ident=ident.astype(NPBF), kaugc=kaugc.astype(NPBF))


def _aug_tables():
    t = np.arange(S)
    a = (t // 128).astype(np.float64)
    b = (t % 128).astype(np.float64)
    slopes = 2.0 ** (-(np.arange(1, 9)))
    qa = np.zeros((8, 4, S), np.float32)
    for h in range(8):
        s8 = 8.0 * slopes[h]
        qa[h, 0] = -s8 * 128.0 * a
        qa[h, 1] = -s8 * b
        qa[h, 2] = s8
        qa[h, 3] = s8
    ka = np.stack([np.ones(S), np.ones(S), 128.0 * a, b]).astype(np.float32)
    return qa.astype(NPBF), ka.astype(NPBF)


def build_B():
    nc = bass.Bass("TRN2", target_bir_lowering=False)
    fw = FW(nc)

    def din(name, shape, dt):
        return nc.dram_tensor(name, shape, dt, kind="ExternalInput").ap()

    mq = din("mq", [2, 96, S], BF16)
    mk = din("mk", [2, 96, S], BF16)
    mv = din("mv", [2, 128, 64, 64], BF16)
    nq = din("nq", [4, 68, S], BF16)
    nkc = din("nkc", [2, 64, S], BF16)
    nks = din("nks", [68, S], BF16)
    nvs = din("nvs", [128, 64, 64], BF16)
    nkw = din("nkw", [68, S], BF16)
    nvw = din("nvw", [128, 64, 64], BF16)
    gates_h = nc.dram_tensor("gates", [6, S], F32, kind="ExternalInput")
    peT = din("peT", [2, 64, 32], F32)
    w1 = din("w1", [2, 2048, 128], F32)
    b1 = din("b1", [2, 128, 1], F32)
    w2 = din("w2", [2, 128, 64], F32)
    c_cmask = din("cmask_c", [13, 128, 512], BF16)
    c_ov = din("ov_c", [4, 128, 129], BF16)
    c_wide = din("wide_c", [128, 640], F32)
    c_ew = din("ew_c", [128, S], BF16)
    c_ident = din("ident_c", [128, 128], BF16)
    c_kaugc = din("kaugc_c", [4, 512], BF16)
    out = nc.dram_tensor("o", [4, 64, S], BF16, kind="ExternalOutput").ap()

    def sb(name, shape, dt):
        return nc.alloc_sbuf_tensor(name, shape, dt)

    big = [sb("big%d" % i, [128, S], BF16) for i in range(4)]
    bbig = [Buf() for _ in range(4)]
    selT = sb("selT", [128, S], BF16)
    b_selT = Buf()
    ew = sb("ew", [128, S], BF16)
    cmask = sb("cmask_s", [128, 13, 512], BF16)
    ov = sb("ov_s", [128, 4, 129], BF16)
    wide = sb("wide_s", [128, 640], F32)
    ident = sb("ident_s", [128, 128], BF16)
    b_const = Buf()
    cin = big[2]
    b_cin = bbig[2]
    w1s = sb("w1s", [64, 32, 128], BF16)
    w2s = sb("w2s", [128, 64], BF16)
    b1s = sb("b1s", [128, 1], F32)
    peTs = sb("peTs", [64, 32], BF16)
    b_cw = Buf()
    cbias = sb("cbias", [128, 1], F32)
    b_cbias = Buf()
    hid = sb("hid", [128, 512], BF16)
    b_hid = Buf()
    kcaug = sb("kcaug", [68, 512], BF16)
    b_kc = Buf()
    vcaug = sb("vcaug", [128, 4, 128], BF16)
    b_vc = Buf()
    pc = sb("pc", [128, 16, 512], BF16)
    b_pc = [Buf() for _ in range(16)]
    qch = [sb("qch%d" % i, [96, 512], BF16) for i in range(8)]
    qrot = Rot([(qch[i], Buf()) for i in range(8)])
    pts = [sb("pt%d" % i, [128, 512], BF16) for i in range(4)]
    prot = Rot([(pts[i], Buf()) for i in range(4)])
    pre = [sb("pre%d" % i, [128, 512], F32) for i in range(2)]
    prerot = Rot([(pre[i], Buf()) for i in range(2)])
    (zf, b_zf), (zt, b_zt) = prerot.items
    gbc = [sb("gbc%d" % i, [64, 6, 512], F32) for i in range(2)]
    grot = Rot([(gbc[i], Buf()) for i in range(2)])
    rd = [sb("rd%d" % i, [64, 512], F32) for i in range(2)]
    rrot = Rot([(rd[i], Buf()) for i in range(2)])
    tmpc = [sb("tmpc%d" % i, [64, 512], F32) for i in range(2)]
    trot = Rot([(tmpc[i], Buf()) for i in range(2)])
    oacc = [sb("oacc%d" % i, [64, 512], F32) for i in range(4)]
    oarot = Rot([(oacc[i], Buf()) for i in range(4)])
    obf = [sb("obf%d" % i, [64, 512], BF16) for i in range(3)]
    obrot = Rot([(obf[i], Buf()) for i in range(3)])
    negb = sb("negb", [128, 1], F32)
    b_negb = Buf()
    fw.op("pool", lambda e: e.memset(negb[:, :], -30000.0), writes=[b_negb])
    imp4 = sb("imp4", [128, 4, 128], F32)
    b_imp4 = [Buf() for _ in range(4)]
    sc4 = sb("sc4", [128, 4, 128], F32)
    b_sc4 = [Buf() for _ in range(4)]
    sc_u = sb("sc_u", [128, 128], F32)
    m8a = sb("m8a", [128, 8], F32)
    m8b = sb("m8b", [128, 8], F32)
    rd16 = sb("rd16", [128, 16], F32)
    selb = sb("selb", [128, 2, 4, 128], BF16)
    b_selb = [[Buf() for _ in range(4)] for _ in range(2)]
    b_scu, b_m8a, b_m8b, b_rd16 = [Buf() for _ in range(4)]
    zs = sc4[:, :, :].rearrange("p a b -> p (a b)")
    b_zs = Buf()

    banks = [nc.alloc_psum_tensor("bank%d" % i, [128, 512], F32) for i in range(8)]
    work = Rot([(banks[i], Buf(True)) for i in range(4)])
    accr = Rot([(banks[4 + i], Buf(True)) for i in range(4)])
    b_out = Buf()

    fw.dma("sp", cmask[:, :, :], c_cmask.rearrange("m p u -> p m u"), writes=[b_const])
    fw.dma("sp", ov[:, :, :], c_ov.rearrange("c p j -> p c j"), writes=[b_const])
    fw.dma("sp", wide[:, :], c_wide, writes=[b_const])
    fw.dma("sp", ew[:, :], c_ew, writes=[b_const])
    fw.dma("sp", ident[:, :], c_ident, writes=[b_const])
    for bi in (1, 3):
        v4 = big[bi][:, :].rearrange("p (k c) -> p k c", c=128)
        fw.op("pool", lambda e: e.memset(v4[:, :, 64:128], 1.0), writes=[bbig[bi]])

    def load_kv(bk, bv, ksrc, vsrc, krows):
        fw.dma("sp", big[bk][0:krows, :], ksrc, writes=[bbig[bk]])
        v4 = big[bv][:, :].rearrange("p (k c) -> p k c", c=128)
        fw.dma("sp", v4[:, :, 0:64], vsrc, writes=[bbig[bv]])

    load_kv(0, 1, mk[0], mv[0], 96)

    def branch(K, q_ap, b_q, kt_fn, b_k, v_fn, b_v, blocks, scale, acc, b_acc, extra=None):
        n = len(blocks)
        scs = [None] * n

        def qk(i):
            w, bw = work.next()
            fw.op("pe", lambda e: e.matmul(w[:, :], lhsT=kt_fn(blocks[i][0]), rhs=q_ap,
                                            start=True, stop=(extra is None)),
                  reads=[b_k, b_q], writes=[bw])
            if extra is not None:
                el, er, ebufs = extra(blocks[i][0])
                fw.op("pe", lambda e: e.matmul(w[:, :], lhsT=el, rhs=er, start=False, stop=True),
                      reads=ebufs, writes=[bw])
            scs[i] = (w, bw)

        for i in range(min(2, n)):
            qk(i)
        for i in range(n):
            w, bw = scs[i]
            p, bp = prot.next()
            m = blocks[i][1]
            if m is not None:
                t, bt = prerot.next()
                fw.op("dve", lambda e: e.tensor_tensor(out=t[:, :], in0=w[:, :], in1=cmask[:, m, :],
                                                       op=ALU.add),
                      reads=[bw, b_const], writes=[bt])
                fw.op("act", lambda e: e.activation(out=p[:, :], in_=t[:, :], func=AF.Exp, scale=scale),
                      reads=[bt], writes=[bp])
            else:
                fw.op("act", lambda e: e.activation(out=p[:, :], in_=w[:, :], func=AF.Exp, scale=scale),
                      reads=[bw], writes=[bp])
            if i + 2 < n:
                qk(i + 2)
            fw.op("pe", lambda e: e.matmul(acc[:, :], lhsT=v_fn(blocks[i][0]), rhs=p[:, :],
                                            start=(i == 0), stop=(i == n - 1)),
                  reads=[b_v, bp], writes=[b_acc])

    def pipeline(segs, LA=3, DEFER=2):
        items = []
        for si, sg in enumerate(segs):
            nb = len(sg["blocks"])
            for bi in range(nb):
                items.append((si, bi, nb))
        n = len(items)
        scs = [None] * n
        accs = {}
        pending = []

        def qk(k):
            si, bi, nb = items[k]
            sg = segs[si]
            if bi == 0:
                if sg.get("prep"):
                    sg["prep"]()
                accs[si] = accr.next()
                while any(pd[2] is accs[si][0] for pd in pending):
                    _, fn, a, ba = pending.pop(0)
                    fn(a, ba)
            w, bw = work.next()
            kb = sg["blocks"][bi][0]
            ex = sg.get("extra")
            fw.op("pe", lambda e: e.matmul(w[:, :], lhsT=sg["kt"](kb), rhs=sg["q"](), start=True,
                                            stop=(ex is None)),
                  reads=[sg["bk"], sg["bq"]()], writes=[bw])
            if ex is not None:
                el, er, ebufs = ex(kb)
                fw.op("pe", lambda e: e.matmul(w[:, :], lhsT=el, rhs=er, start=False, stop=True),
                      reads=ebufs, writes=[bw])
            scs[k] = (w, bw)

        for k in range(min(LA, n)):
            qk(k)
        for k in range(n):
            si, bi, nb = items[k]
            sg = segs[si]
            w, bw = scs[k]
            p, bp = prot.next()
            m = sg["blocks"][bi][1]
            if m is not None:
                t, bt = prerot.next()
                fw.op("dve", lambda e: e.tensor_tensor(out=t[:, :], in0=w[:, :], in1=cmask[:, m, :],
                                                       op=ALU.add),
                      reads=[bw, b_const], writes=[bt])
                fw.op("act", lambda e: e.activation(out=p[:, :], in_=t[:, :], func=AF.Exp, scale=sg["scale"]),
                      reads=[bt], writes=[bp])
            else:
                fw.op("act", lambda e: e.activation(out=p[:, :], in_=w[:, :], func=AF.Exp, scale=sg["scale"]),
                      reads=[bw], writes=[bp])
            if k + LA < n:
                qk(k + LA)
            acc, bacc = accs[si]
            kb = sg["blocks"][bi][0]
            fw.op("pe", lambda e: e.matmul(acc[:, :], lhsT=sg["v"](kb), rhs=p[:, :], start=(bi == 0),
                                            stop=(bi == nb - 1)),
                  reads=[sg["bv"], bp], writes=[bacc])
            for pd in pending:
                pd[0] -= 1
            while pending and pending[0][0] <= 0:
                _, fn, a, ba = pending.pop(0)
                fn(a, ba)
            if bi == nb - 1:
                pending.append([DEFER, sg["done"], acc, bacc])
        for _, fn, a, ba in pending:
            fn(a, ba)

    def recip_den(acc, b_acc):
        r, br = rrot.next()
        fw.op("dve", lambda e: e.tensor_scalar(out=r[:, :], in0=acc[64:128, :], scalar1=1e-30,
                                               scalar2=None, op0=ALU.max),
              reads=[b_acc], writes=[br])
        fw.op("dve", lambda e: e.reciprocal(out=r[:, :], in_=r[:, :]), reads=[br], writes=[br])
        return r, br

    def gated_accum(acc, b_acc, g_ap, b_g, dst, b_dst, first):
        r, br = recip_den(acc, b_acc)
        fw.op("dve", lambda e: e.tensor_tensor(out=r[:, :], in0=r[:, :], in1=g_ap, op=ALU.mult),
              reads=[br, b_g], writes=[br])
        if first:
            fw.op("dve", lambda e: e.tensor_tensor(out=dst[:, :], in0=acc[0:64, :], in1=r[:, :],
                                                   op=ALU.mult),
                  reads=[b_acc, br], writes=[b_dst])
        else:
            t, bt = trot.next()
            fw.op("dve", lambda e: e.tensor_tensor(out=t[:, :], in0=acc[0:64, :], in1=r[:, :],
                                                   op=ALU.mult),
                  reads=[b_acc, br], writes=[bt])
            fw.op("pool", lambda e: e.tensor_tensor(out=dst[:, :], in0=dst[:, :], in1=t[:, :],
                                                    op=ALU.add),
                  reads=[bt, b_dst], writes=[b_dst])

    def vaug(bi):
        v4 = big[bi][:, :].rearrange("p (k c) -> p k c", c=128)
        return lambda kb: v4[:, kb, :]

    def mla_head(hi, bk, bv):
        vf = vaug(bv)
        ql = {}

        def loadq(c):
            if c < 16:
                q, bq = qrot.next()
                fw.dma("sp", q[0:96, :], mq[hi, :, c * 512:(c + 1) * 512], writes=[bq])
                ql[c] = (q, bq)

        loadq(0)
        loadq(1)
        segs = []
        for qc in range(16):
            def done(acc, bacc, qc=qc):
                r, br = recip_den(acc, bacc)
                o, bo = obrot.next()
                fw.op("dve", lambda e: e.tensor_tensor(out=o[:, :], in0=acc[0:64, :], in1=r[:, :], op=ALU.mult),
                      reads=[bacc, br], writes=[bo])
                fw.dma("sp", out[hi, :, qc * 512:(qc + 1) * 512], o[:, :], reads=[bo], writes=[Buf()])
            segs.append(dict(
                q=lambda qc=qc: ql[qc][0][0:96, :], bq=lambda qc=qc: ql[qc][1],
                kt=lambda kb: big[bk][0:96, kb * 128:(kb + 1) * 128], bk=bbig[bk], v=vf, bv=bbig[bv],
                blocks=[(kb, (kb - 4 * qc) if kb >= 4 * qc else None) for kb in range(4 * qc + 4)],
                scale=MLA_SCALE, prep=(lambda qc=qc: loadq(qc + 2)), done=done))
        pipeline(segs)

    def compress(which):
        fw.dma("sp", cin[0:64, :], nkc[which], writes=[b_cin])
        fw.dma("pool", w1s[:, :, :], w1[which].rearrange("(j d) m -> d j m", d=64), writes=[b_cw])
        fw.dma("pool", w2s[:, :], w2[which], writes=[b_cw])
        fw.dma("sp", b1s[:, :], b1[which], writes=[b_cw])
        fw.dma("pool", peTs[:, :], peT[which], writes=[b_cw])
        wb, bwb = work.next()
        for j in range(32):
            fw.op("pe", lambda e: e.matmul(wb[:, 0:1], lhsT=w1s[:, j, :], rhs=peTs[:, j:j + 1],
                                            start=(j == 0), stop=(j == 31)),
                  reads=[b_cw], writes=[bwb])
        fw.op("dve", lambda e: e.tensor_tensor(out=cbias[:, :], in0=wb[:, 0:1], in1=b1s[:, :], op=ALU.add),
              reads=[bwb, b_cw], writes=[b_cbias])
        c3 = cin[0:64, :].rearrange("p (n s) -> p n s", s=16)
        wh, bwh = work.next()
        for j in range(32):
            rhs = c3[:, 0:511, j] if j < 16 else c3[:, 1:512, j - 16]
            fw.op("pe", lambda e: e.matmul(wh[:, 0:511], lhsT=w1s[:, j, :], rhs=rhs,
                                            start=(j == 0), stop=(j == 31)),
                  reads=[b_cw, b_cin], writes=[bwh])
        fw.op("act", lambda e: e.activation(out=zf[:, 0:511], in_=wh[:, 0:511], func=AF.Identity,
                                            bias=cbias[:, 0:1], scale=1.0),
              reads=[bwh, b_cbias], writes=[b_zf])
        fw.op("dve", lambda e: e.tensor_tensor(out=zt[:, 0:511], in0=zf[:, 0:511], in1=zf[:, 0:511],
                                               op=ALU.mult), reads=[b_zf], writes=[b_zt])
        fw.op("dve", lambda e: e.tensor_scalar(out=zt[:, 0:511], in0=zt[:, 0:511], scalar1=0.044715,
                                               scalar2=1.0, op0=ALU.mult, op1=ALU.add),
              reads=[b_zt], writes=[b_zt])
        fw.op("dve", lambda e: e.tensor_tensor(out=zt[:, 0:511], in0=zt[:, 0:511], in1=zf[:, 0:511],
                                               op=ALU.mult), reads=[b_zt, b_zf], writes=[b_zt])
        fw.op("act", lambda e: e.activation(out=zs[:, 0:511], in_=zt[:, 0:511], func=AF.Sigmoid,
                                            scale=1.5957691216057308),
              reads=[b_zt], writes=[b_zs])
        fw.op("pool", lambda e: e.memset(hid[:, :], 0.0), writes=[b_hid])
        fw.op("dve", lambda e: e.tensor_tensor(out=hid[:, 0:511], in0=zf[:, 0:511], in1=zs[:, 0:511],
                                               op=ALU.mult), reads=[b_zf, b_zs], writes=[b_hid])
        if which == 0:
            fw.op("pool", lambda e: e.memset(kcaug[:, :], 0.0), writes=[b_kc])
            wk, bwk = work.next()
            fw.op("pe", lambda e: e.matmul(wk[0:64, 0:511], lhsT=w2s[:, :], rhs=hid[:, 0:511],
                                            start=True, stop=True), reads=[b_cw, b_hid], writes=[bwk])
            fw.op("dve", lambda e: e.tensor_copy(kcaug[0:64, 0:511], wk[0:64, 0:511]),
                  reads=[bwk], writes=[b_kc])
            fw.dma("sp", kcaug[64:68, :], c_kaugc, writes=[b_kc])
        else:
            fw.op("pool", lambda e: e.memset(vcaug[:, :, 0:64], 0.0), writes=[b_vc])
            fw.op("pool", lambda e: e.memset(vcaug[:, :, 64:128], 1.0), writes=[b_vc])
            for c in range(4):
                rows = 128 if c < 3 else 127
                wv, bwv = work.next()
                fw.op("pe", lambda e: e.matmul(wv[0:rows, 0:64], lhsT=hid[:, c * 128:c * 128 + rows],
                                                rhs=w2s[:, :], start=True, stop=True),
                      reads=[b_cw, b_hid], writes=[bwv])
                fw.op("dve", lambda e: e.tensor_copy(vcaug[0:rows, c, 0:64], wv[0:rows, 0:64]),
                      reads=[bwv], writes=[b_vc])

    def cmp_blocks(qc):
        return [(c, (8 + qc - 4 * c) if (qc - 4 * c) <= 4 else None) for c in range(qc // 4 + 1)]

    def select_phase():
        preq = {}

        def load_sq(c_):
            if c_ >= 16:
                return
            preq[c_] = []
            for r in range(4):
                q, bq = qrot.next()
                fw.dma("sp", q[0:68, :], nq[r, :, c_ * 512:(c_ + 1) * 512], writes=[bq])
                preq[c_].append((q, bq))

        def stage3(qc):
            w, bw = work.next()
            for tb in range(4):
                fw.op("pe", lambda e: e.matmul(w[:, tb * 128:(tb + 1) * 128], lhsT=selb[:, qc % 2, tb, :],
                                                rhs=ident[:, :], start=True, stop=True),
                      reads=[b_selb[qc % 2][tb], b_const], writes=[bw])
            fw.op("act", lambda e: e.activation(out=selT[:, qc * 512:(qc + 1) * 512], in_=w[:, :],
                                                func=AF.Identity, bias=negb[:, 0:1], scale=30000.0),
                  reads=[bw, b_negb], writes=[b_selT])

        load_sq(0)
        for qc in range(16):
            load_sq(qc + 1)
            blocks = cmp_blocks(qc)
            nb = len(blocks)
            qlist = preq.pop(qc)
            for r in range(4):
                q, bq = qlist[r]
                for (c, m) in blocks:
                    w, bw = work.next()
                    fw.op("pe", lambda e: e.matmul(w[:, :], lhsT=kcaug[:, c * 128:(c + 1) * 128],
                                                    rhs=q[0:68, :], start=True, stop=True),
                          reads=[b_kc, bq], writes=[bw])
                    pi = r * 4 + c
                    if m is not None:
                        t, bt = prerot.next()
                        fw.op("dve", lambda e: e.tensor_tensor(out=t[:, :], in0=w[:, :], in1=cmask[:, m, :],
                                                               op=ALU.add),
                              reads=[bw, b_const], writes=[bt])
                        fw.op("act", lambda e: e.activation(out=pc[:, pi, :], in_=t[:, :], func=AF.Exp,
                                                            scale=0.125),
                              reads=[bt], writes=[b_pc[pi]])
                    else:
                        fw.op("act", lambda e: e.activation(out=pc[:, pi, :], in_=w[:, :], func=AF.Exp,
                                                            scale=0.125),
                              reads=[bw], writes=[b_pc[pi]])
            if qc > 0:
                stage3(qc - 1)
            den, bden = work.next()
            ibanks = []
            for r in range(4):
                ib, bib = accr.next()
                ibanks.append((ib, bib))
                for tb in range(4):
                    for bi, (c, m) in enumerate(blocks):
                        pi = r * 4 + c
                        fw.op("pe", lambda e: e.matmul(ib[:, tb * 128:(tb + 1) * 128],
                                                        lhsT=pc[:, pi, tb * 128:(tb + 1) * 128], rhs=ov[:, c, 0:128],
                                                        start=(bi == 0), stop=(bi == nb - 1)),
                              reads=[b_pc[pi], b_const], writes=[bib])
                    for bi, (c, m) in enumerate(blocks):
                        pi = r * 4 + c
                        fw.op("pe", lambda e: e.matmul(den[:, r * 4 + tb:r * 4 + tb + 1],
                                                        lhsT=pc[:, pi, tb * 128:(tb + 1) * 128], rhs=ov[:, c, 128:129],
                                                        start=(bi == 0), stop=(bi == nb - 1)),
                              reads=[b_pc[pi], b_const], writes=[bden])
            fw.op("dve", lambda e: e.tensor_scalar(out=rd16[:, :], in0=den[:, 0:16], scalar1=1e-30, scalar2=None,
                                                   op0=ALU.max), reads=[bden], writes=[b_rd16])
            fw.op("dve", lambda e: e.reciprocal(out=rd16[:, :], in_=rd16[:, :]), reads=[b_rd16], writes=[b_rd16])
            for tb in range(4):
                gi = qc * 4 + tb
                for r in range(4):
                    ib, bib = ibanks[r]
                    if r == 0:
                        fw.op("dve", lambda e: e.tensor_scalar(out=imp4[:, tb, :], in0=ib[:, tb * 128:(tb + 1) * 128],
                                                               scalar1=rd16[:, r * 4 + tb:r * 4 + tb + 1], scalar2=None,
                                                               op0=ALU.mult),
                              reads=[bib, b_rd16], writes=[b_imp4[tb]])
                    else:
                        fw.op("dve", lambda e: e.scalar_tensor_tensor(out=imp4[:, tb, :],
                                                                      in0=ib[:, tb * 128:(tb + 1) * 128],
                                                                      scalar=rd16[:, r * 4 + tb:r * 4 + tb + 1],
                                                                      in1=imp4[:, tb, :], op0=ALU.mult, op1=ALU.add),
                              reads=[bib, b_rd16, b_imp4[tb]], writes=[b_imp4[tb]])
                st = 127 - 2 * gi
                vsl = wide[:, st:st + 128]
                csl = wide[:, 256 + st:256 + st + 128]
                fw.op("pool", lambda e: e.tensor_tensor(out=sc4[:, tb, :], in0=imp4[:, tb, :], in1=vsl, op=ALU.mult),
                      reads=[b_imp4[tb], b_const], writes=[b_sc4[tb]])
                fw.op("pool", lambda e: e.tensor_tensor(out=sc4[:, tb, :], in0=sc4[:, tb, :], in1=csl, op=ALU.add),
                      reads=[b_sc4[tb], b_const], writes=[b_sc4[tb]])
                fw.op("pool", lambda e: e.tensor_tensor(out=sc4[:, tb, :], in0=sc4[:, tb, :], in1=wide[:, 512:640],
                                                        op=ALU.add),
                      reads=[b_sc4[tb], b_const], writes=[b_sc4[tb]])
            for tb in range(4):
                gi = qc * 4 + tb
                st = 127 - 2 * gi
                vsl = wide[:, st:st + 128]
                fw.op("dve", lambda e: e.max(out=m8a[:, :], in_=sc4[:, tb, :]), reads=[b_sc4[tb]], writes=[b_m8a])
                fw.op("dve", lambda e: e.match_replace(out=sc_u[:, :], in_to_replace=m8a[:, :],
                                                       in_values=sc4[:, tb, :], imm_value=-2.0),
                      reads=[b_sc4[tb], b_m8a], writes=[b_scu])
                fw.op("dve", lambda e: e.max(out=m8b[:, :], in_=sc_u[:, :]), reads=[b_scu], writes=[b_m8b])
                fw.op("dve", lambda e: e.scalar_tensor_tensor(out=selb[:, qc % 2, tb, :], in0=sc4[:, tb, :],
                                                              scalar=m8b[:, 7:8], in1=vsl,
                                                              op0=ALU.is_ge, op1=ALU.mult),
                      reads=[b_sc4[tb], b_m8b, b_const], writes=[b_selb[qc % 2][tb]])
        stage3(15)

    def nsa_phase():
        vfs = vaug(1)
        vfw = vaug(3)
        pre_ = {}

        gl_ = {}

        def load_g(c):
            if c >= 16:
                return
            g_, bg_ = grot.next()
            fw.dma("sp", g_[:, :, :], bass.AP(gates_h, c * 512, [[0, 64], [S, 6], [1, 512]]), writes=[bg_])
            gl_[c] = (g_, bg_)

        def load_qc(c):
            if c >= 16:
                return
            qs_ = []
            for h in range(2):
                q, bq = qrot.next()
                fw.dma("sp", q[0:68, :], nq[h, :, c * 512:(c + 1) * 512], writes=[bq])
                qs_.append((q, bq))
            pre_[c] = (None, None, qs_, [oarot.next() for _ in range(2)])

        load_g(0)
        load_qc(0)
        segs = []
        for qc in range(16):
            sblocks = [(kb, (kb - 4 * qc) if kb >= 4 * qc else None) for kb in range(4 * qc + 4)]
            wblocks = []
            for kb in range(max(0, 4 * qc - 4), 4 * qc + 4):
                o_ = 128 * kb - 512 * qc
                wblocks.append((kb, (o_ // 128) if o_ >= 0 else 4 + (o_ + 512) // 128))
            for br, (blocks, ktf, bkb, vf_, bvb) in enumerate((
                    (cmp_blocks(qc), (lambda c: kcaug[:, c * 128:(c + 1) * 128]), b_kc, (lambda c: vcaug[:, c, :]), b_vc),
                    (sblocks, (lambda kb: big[0][0:68, kb * 128:(kb + 1) * 128]), bbig[0], vfs, bbig[1]),
                    (wblocks, (lambda kb: big[2][0:68, kb * 128:(kb + 1) * 128]), bbig[2], vfw, bbig[3]))):
                for h in range(2):
                    def done(acc, bacc, qc=qc, h=h, br=br):
                        _, _, qs, dst = pre_[qc]
                        g, bg = gl_[qc]
                        gated_accum(acc, bacc, g[:, 3 * h + br, :], bg, dst[h][0], dst[h][1], br == 0)
                        if br == 2:
                            o, bo = obrot.next()
                            fw.op("act", lambda e: e.copy(out=o[:, :], in_=dst[h][0][:, :]), reads=[dst[h][1]],
                                  writes=[bo])
                            fw.dma("sp", out[2 + h, :, qc * 512:(qc + 1) * 512], o[:, :], reads=[bo], writes=[Buf()])
                    sg = dict(q=lambda qc=qc, h=h: pre_[qc][2][h][0][0:68, :], bq=lambda qc=qc, h=h: pre_[qc][2][h][1],
                              kt=ktf, bk=bkb, v=vf_, bv=bvb, blocks=blocks, scale=0.125, done=done)
                    if br == 1:
                        sg["extra"] = (lambda kb, qc=qc: (ew[:, kb * 128:(kb + 1) * 128],
                                                          selT[:, qc * 512:(qc + 1) * 512], [b_const, b_selT]))
                    if br == 0 and h == 0:
                        sg["prep"] = (lambda qc=qc: load_qc(qc + 1))
                    if br == 2 and h == 0:
                        sg["prep"] = (lambda qc=qc: load_g(qc + 1))
                    segs.append(sg)
        pipeline(segs)

    compress(0)
    compress(1)
    load_kv(2, 3, mk[1], mv[1], 96)
    mla_head(0, 0, 1)
    select_phase()
    load_kv(0, 1, nks, nvs, 68)
    mla_head(1, 2, 3)
    load_kv(2, 3, nkw, nvw, 68)
    nsa_phase()
    fw.finish()
    return nc


_CB = None


def _prep_B(ao, l, P):
    global _CB
    if _CB is None:
        _CB = _consts_B()
        _CB["qa"], _CB["ka"] = _aug_tables()
    C = _CB
    qa, ka = C["qa"], C["ka"]
    maps = []
    pe = np.stack([P["nsa_pe_k"][l].T, P["nsa_pe_v"][l].T]).astype(np.float32)
    w1 = np.stack([P["nsa_w1_k"][l], P["nsa_w1_v"][l]]).astype(np.float32)
    b1 = np.stack([P["nsa_b1_k"][l], P["nsa_b1_v"][l]]).astype(np.float32)[:, :, None]
    w2 = np.stack([P["nsa_w2_k"][l], P["nsa_w2_v"][l]]).astype(np.float32)
    for b in range(2):
        cs = [ao[4 * b + i] for i in range(4)]
        cat = lambda key, axis: np.concatenate([c[key] for c in cs], axis=axis)
        qm = cat("qm", 2)
        kn = cat("kn", 2)
        kr = cat("kr", 1)
        vm = cat("vm", 0)
        nq = cat("nq", 2)
        nkf = cat("nkf", 3)
        nvt = cat("nvt", 0)
        gl = cat("gl", 1)
        tokmaj = lambda a: np.ascontiguousarray(a.reshape(64, 128, 64).transpose(1, 0, 2))
        for j in range(4):
            heads = [2 * j, 2 * j + 1]
            g = j // 2
            r0 = 2 * (j % 2)
            order = [r0, r0 + 1] + [r for r in range(4) if r not in (r0, r0 + 1)]
            m = dict(
                mq=np.ascontiguousarray(qm[heads]),
                mk=np.stack([np.concatenate([kn[h], kr], axis=0) for h in heads]),
                mv=np.stack([tokmaj(vm[:, h * 64:(h + 1) * 64]) for h in heads]),
                nq=np.stack([np.concatenate([nq[4 * g + r], qa[4 * g + r]], axis=0) for r in order]),
                nkc=np.stack([nkf[0, g], nkf[1, g]]),
                nks=np.concatenate([nkf[2, g], ka], axis=0),
                nkw=np.concatenate([nkf[3, g], ka], axis=0),
                nvs=tokmaj(nvt[:, 0, g]),
                nvw=tokmaj(nvt[:, 1, g]),
                gates=np.stack([gl[(4 * g + r) * 3 + br] for r in (r0, r0 + 1) for br in range(3)]).astype(np.float32),
                peT=pe, w1=w1, b1=b1, w2=w2,
                cmask_c=C["cmask"], ov_c=C["ov"], wide_c=C["wide"], ew_c=C["ew"], ident_c=C["ident"], kaugc_c=C["kaugc"],
            )
            maps.append({k: np.ascontiguousarray(v) for k, v in m.items()})
    return maps


def _gather_B(res):
    outs = []
    for b in range(2):
        mixT = np.zeros((1024, S), NPBF)
        for j in range(4):
            o = res[4 * b + j]["o"]
            g = j // 2
            r0 = 2 * (j % 2)
            for i, h in enumerate((2 * j, 2 * j + 1)):
                mixT[h * 64:(h + 1) * 64] = o[i]
            for i, r in enumerate((r0, r0 + 1)):
                hh = 4 * g + r
                mixT[512 + hh * 64:512 + (hh + 1) * 64] = o[2 + i]
        for i in range(4):
            outs.append(np.ascontiguousarray(mixT[:, i * 2048:(i + 1) * 2048]))
    return outs


IN_OFF = dict(cq=0, ckv=384, kr=640, nq=672, kc=1184, vc=1312, ks=1440, vs=1568, kw=1696, vw=1824, gl=1952)
TT = 2048


class _Stop(Exception):
    pass


def build_T(do_ln_in, do_C, do_A, dbg=None):
    nc = bass.Bass("TRN2", target_bir_lowering=False)
    fw = FW(nc)
    HALO = 2 if do_C else 0
    TW = TT + HALO

    def din(name, shape, dt):
        return nc.dram_tensor(name, shape, dt, kind="ExternalInput").ap()

    def dout(name, shape, dt):
        return nc.dram_tensor(name, shape, dt, kind="ExternalOutput").ap()

    def sb(name, shape, dt):
        return nc.alloc_sbuf_tensor(name, shape, dt)

    xT_d = din("xT", [D, TW], F32)
    vec_d = din("vec", [128, 264], F32)
    if do_C:
        oT_d = din("oT", [D, TW], BF16)
        flag_d = din("flag", [128, 1], F32)
        memT_d = din("memT", [D, 256], F32)
        w_out_d = din("w_out", [D, D], F32)
        wq_d = din("xa_wq", [D, D], F32)
        wkv_d = din("xa_wkv", [D, 2 * D], F32)
        wo_d = din("xa_wo", [D, D], F32)
        wup_d = din("w_up", [D, 5632], F32)
        wdn_d = din("w_down", [2816, D], F32)
    if do_A:
        w_in_d = din("w_in", [D, 1976], F32)
        wuq_d = din("w_uq", [384, 768], F32)
        wukv_d = din("w_ukv", [256, 1024], F32)
        ropeC_d = din("ropeC", [32, TT], F32)
        ropeS_d = din("ropeS", [32, TT], F32)
        qm_d = dout("qm", [8, 96, TT], BF16)
        kn_d = dout("kn", [8, 64, TT], BF16)
        kr_d = dout("kr", [32, TT], BF16)
        vm_d = dout("vm", [TT, 512], BF16)
        nq_d = dout("nq", [512, TT], BF16)
        nkf_d = dout("nkf", [4, 128, TT], BF16)
        nvt_d = dout("nvt", [TT, 256], BF16)
        gl_d = dout("gl", [24, TT], F32)
    xo_d = dout("xo", [D, TT], F32)

    V_LNIN, V_LN1, V_LN2, V_LN3, V_LNM, V_CONV, V_CQG, V_CKVG = 0, 16, 32, 48, 64, 80, 256, 259

    vec = sb("vec_s", [128, 264], F32)
    b_vec = Buf()
    fw.dma("sp", vec[:, :], vec_d, writes=[b_vec])
    ones_b = sb("ones_b", [128, 128], BF16)
    b_ones = Buf()
    fw.op("pool", lambda e: e.memset(ones_b[:, :], 1.0), writes=[b_ones])

    banks = [nc.alloc_psum_tensor("bank%d" % i, [128, 512], F32) for i in range(8)]
    bbank = [Buf(True) for _ in range(8)]
    psr = Rot([(banks[i], bbank[i]) for i in range(8)])

    xr = sb("xr", [128, 8, 512], F32)
    bxr = [Buf() for _ in range(8)]
    xb = sb("xb", [128, 8, 512], BF16)
    bxb = [Buf() for _ in range(8)]
    lnt = Rot([(sb("lnt%d" % i, [128, 512], BF16), Buf()) for i in range(4)])
    st_mean = sb("st_mean", [128, 512], F32)
    st_m2 = sb("st_m2", [128, 512], F32)
    st_rstd = sb("st_rstd", [128, 512], F32)
    st_nmr = sb("st_nmr", [128, 512], F32)
    b_mean, b_m2, b_rstd, b_nmr = Buf(), Buf(), Buf(), Buf()
    wst = Rot([(sb("wst%d" % i, [128, 8, 512], BF16), Buf()) for i in range(2)])

    def layer_norm(n, gi, src=xr, bsrc=bxr, dstb=xb, bdstb=bxb, nch=8, nfeat=1024.0, eps=1e-5):
        bs, bbs = psr.next()
        bq, bbq = psr.next()
        for c in range(nch):
            tb, btb = lnt.next()
            fw.op("act", lambda e: e.copy(out=tb[:, :n], in_=src[:, c, :n]), reads=[bsrc[c]], writes=[btb])
            fw.op("pe", lambda e: e.matmul(bs[:, :n], lhsT=ones_b[:, :], rhs=tb[:, :n], start=(c == 0),
                                            stop=(c == nch - 1)), reads=[btb, b_ones], writes=[bbs])
            tq, btq = lnt.next()
            fw.op("act", lambda e: e.activation(out=tq[:, :n], in_=src[:, c, :n], func=AF.Square),
                  reads=[bsrc[c]], writes=[btq])
            fw.op("pe", lambda e: e.matmul(bq[:, :n], lhsT=ones_b[:, :], rhs=tq[:, :n], start=(c == 0),
                                            stop=(c == nch - 1)), reads=[btq, b_ones], writes=[bbq])
        fw.op("dve", lambda e: e.tensor_scalar(out=st_mean[:, :n], in0=bs[:, :n], scalar1=1.0 / nfeat,
                                               scalar2=None, op0=ALU.mult), reads=[bbs], writes=[b_mean])
        fw.op("dve", lambda e: e.tensor_tensor(out=st_m2[:, :n], in0=st_mean[:, :n], in1=st_mean[:, :n],
                                               op=ALU.mult), reads=[b_mean], writes=[b_m2])
        fw.op("dve", lambda e: e.scalar_tensor_tensor(out=st_m2[:, :n], in0=bq[:, :n], scalar=1.0 / nfeat,
                                                      in1=st_m2[:, :n], op0=ALU.mult, op1=ALU.subtract),
              reads=[bbq, b_m2], writes=[b_m2])
        fw.op("dve", lambda e: e.tensor_scalar(out=st_m2[:, :n], in0=st_m2[:, :n], scalar1=eps,
                                               scalar2=None, op0=ALU.add), reads=[b_m2], writes=[b_m2])
        fw.op("act", lambda e: e.activation(out=st_rstd[:, :n], in_=st_m2[:, :n], func=AF.Sqrt),
              reads=[b_m2], writes=[b_rstd])
        fw.op("dve", lambda e: e.reciprocal(out=st_rstd[:, :n], in_=st_rstd[:, :n]), reads=[b_rstd],
              writes=[b_rstd])
        fw.op("dve", lambda e: e.scalar_tensor_tensor(out=st_nmr[:, :n], in0=st_mean[:, :n], scalar=-1.0,
                                                      in1=st_rstd[:, :n], op0=ALU.mult, op1=ALU.mult),
              reads=[b_mean, b_rstd], writes=[b_nmr])
        for c in range(nch):
            fw.op("dve", lambda e: e.tensor_tensor(out=src[:, c, :n], in0=src[:, c, :n], in1=st_rstd[:, :n],
                                                   op=ALU.mult), reads=[bsrc[c], b_rstd], writes=[bsrc[c]])
            fw.op("dve", lambda e: e.tensor_tensor(out=src[:, c, :n], in0=src[:, c, :n], in1=st_nmr[:, :n],
                                                   op=ALU.add), reads=[bsrc[c], b_nmr], writes=[bsrc[c]])
            fw.op("act", lambda e: e.activation(out=src[:, c, :n], in_=src[:, c, :n], func=AF.Identity,
                                                bias=vec[:, gi + 8 + c:gi + 9 + c], scale=vec[:, gi + c:gi + c + 1]),
                  reads=[bsrc[c], b_vec], writes=[bsrc[c]])
            fw.op("act", lambda e: e.copy(out=dstb[:, c, :n], in_=src[:, c, :n]), reads=[bsrc[c]],
                  writes=[bdstb[c]])

    def proj_resid(n, w_s, b_w, rhs, brhs):
        for oc in range(8):
            bk, bbk = psr.next()
            for kc in range(8):
                fw.op("pe", lambda e: e.matmul(bk[:, :n], lhsT=w_s[:, kc, oc * 128:(oc + 1) * 128],
                                                rhs=rhs[:, kc, :n], start=(kc == 0), stop=(kc == 7)),
                      reads=[b_w, brhs[kc]], writes=[bbk])
            fw.op("dve", lambda e: e.scalar_tensor_tensor(out=xr[:, oc, :n], in0=xr[:, oc, :n], scalar=ALPHA,
                                                          in1=bk[:, :n], op0=ALU.mult, op1=ALU.add),
                  reads=[bxr[oc], bbk], writes=[bxr[oc]])

    def load_w(dst3, src2, bdst, q="pool"):
        fw.dma(q, dst3, src2.rearrange("(kc p) n -> p kc n", p=128), writes=[bdst])

    if do_C:
        wout_s = sb("wout_s", [128, 8, 1024], BF16)
        wq_s = sb("wq_s", [128, 8, 1024], BF16)
        wo_s = sb("wo_s", [128, 8, 1024], BF16)
        b_wout, b_wq, b_wo = Buf(), Buf(), Buf()
        load_w(wout_s[:, :, :], w_out_d, b_wout)
        load_w(wq_s[:, :, :], wq_d, b_wq)
        load_w(wo_s[:, :, :], wo_d, b_wo)
        flag = sb("flag_s", [128, 1], F32)
        b_flag = Buf()
        fw.dma("sp", flag[:, :], flag_d, writes=[b_flag])
        hb = sb("hb", [128, 22, 512], BF16)
        bhb = [Buf() for _ in range(22)]
        ot = hb[:, 0:8, :]
        bot = bhb[0:8]
        qb = hb[:, 8:16, :]
        bqb = bhb[8:16]
        oab = hb[:, 0:8, :]
        boab = bhb[0:8]
        kT = sb("kT", [128, 8, 256], BF16)
        b_kT = Buf()
        vmem = sb("vmem", [128, 2, 1024], BF16)
        b_vmem = Buf()
        carry = sb("carry", [128, 44, 2], F32)
        bcar = [Buf() for _ in range(44)]
        ptr = Rot([(sb("xpt%d" % i, [128, 512], BF16), Buf()) for i in range(4)])
        rdx = Rot([(sb("rdx%d" % i, [128, 512], F32), Buf()) for i in range(2)])
        uer = Rot([(sb("ue%d" % i, [128, 516], F32), Buf()) for i in range(2)])
        yr_items = [(sb("y%d" % i, [128, 512], F32), Buf()) for i in range(3)]
        yr = Rot(yr_items)
        sgr = Rot([(sb("sg%d" % i, [128, 512], F32), Buf()) for i in range(2)])
        wdn = Rot([(sb("wdn%d" % i, [128, 2, 1024], BF16), Buf()) for i in range(2)])

        mr = xr[:, :, 0:256]
        bmr = bxr
        mb = xb[:, :, 0:256]
        bmb = bxb
        fw.dma("sp", mr, memT_d.rearrange("(c p) t -> p c t", p=128), writes=bmr)
        layer_norm(256, V_LNM, src=mr, bsrc=bmr, dstb=mb, bdstb=bmb)
        for piece in range(4):
            w, bw = wst.next()
            load_w(w[:, :, :], wkv_d[:, piece * 512:(piece + 1) * 512], bw)
            if piece < 2:
                for o4 in range(4):
                    oc = piece * 4 + o4
                    bk, bbk = psr.next()
                    for kc in range(8):
                        fw.op("pe", lambda e: e.matmul(bk[:, :256], lhsT=w[:, kc, o4 * 128:(o4 + 1) * 128],
                                                        rhs=mb[:, kc, :], start=(kc == 0), stop=(kc == 7)),
                              reads=[bw, bmb[kc]], writes=[bbk])
                    fw.op("act", lambda e: e.copy(out=kT[:, oc, :], in_=bk[:, :256]), reads=[bbk], writes=[b_kT])
            else:
                half = piece - 2
                for mc in range(2):
                    bk, bbk = psr.next()
                    for kc in range(8):
                        fw.op("pe", lambda e: e.matmul(bk[:, :], lhsT=mb[:, kc, mc * 128:(mc + 1) * 128],
                                                        rhs=w[:, kc, :], start=(kc == 0), stop=(kc == 7)),
                              reads=[bw, bmb[kc]], writes=[bbk])
                    fw.op("act", lambda e: e.copy(out=vmem[:, mc, half * 512:(half + 1) * 512], in_=bk[:, :]),
                          reads=[bbk], writes=[b_vmem])

        def conv(bk, bbk, fc, n):
            ue, bue = uer.next()
            fw.op("dve", lambda e: e.tensor_copy(ue[:, 0:2], carry[:, fc, :]), reads=[bcar[fc]], writes=[bue])
            fw.op("act", lambda e: e.copy(out=ue[:, 2:2 + n], in_=bk[:, :n]), reads=[bbk], writes=[bue])
            y, by = yr.next()
            cv = V_CONV + fc
            fw.op("act", lambda e: e.activation(out=y[:, :n], in_=ue[:, 2:2 + n], func=AF.Identity,
                                                bias=vec[:, cv + 132:cv + 133], scale=vec[:, cv + 88:cv + 89]),
                  reads=[bue, b_vec], writes=[by])
            fw.op("dve", lambda e: e.scalar_tensor_tensor(out=y[:, :n], in0=ue[:, 1:1 + n],
                                                          scalar=vec[:, cv + 44:cv + 45], in1=y[:, :n],
                                                          op0=ALU.mult, op1=ALU.add),
                  reads=[bue, by, b_vec], writes=[by])
            fw.op("dve", lambda e: e.scalar_tensor_tensor(out=y[:, :n], in0=ue[:, 0:n],
                                                          scalar=vec[:, cv:cv + 1], in1=y[:, :n],
                                                          op0=ALU.mult, op1=ALU.add),
                  reads=[bue, by, b_vec], writes=[by])
            fw.op("dve", lambda e: e.tensor_copy(carry[:, fc, :], ue[:, n:n + 2]), reads=[bue], writes=[bcar[fc]])
            return y, by

        cur_halo = [True]

        def ck(k):
            if dbg == k + (0 if cur_halo[0] else 10):
                raise _Stop()

        def c_tile(n, col0, is_halo):
            cur_halo[0] = is_halo
            ck(1)
            fw.dma("sp", ot[:, :, :n], oT_d[:, col0:col0 + n].rearrange("(c p) t -> p c t", p=128), writes=bot)
            fw.dma("sp", xr[:, :, :n], xT_d[:, col0:col0 + n].rearrange("(c p) t -> p c t", p=128), writes=bxr)
            ck(2)
            proj_resid(n, wout_s, b_wout, ot, bot)
            ck(3)
            layer_norm(n, V_LN1)
            ck(4)
            for oc in range(8):
                bk, bbk = psr.next()
                for kc in range(8):
                    fw.op("pe", lambda e: e.matmul(bk[:, :n], lhsT=wq_s[:, kc, oc * 128:(oc + 1) * 128],
                                                    rhs=xb[:, kc, :n], start=(kc == 0), stop=(kc == 7)),
                          reads=[b_wq, bxb[kc]], writes=[bbk])
                fw.op("act", lambda e: e.copy(out=qb[:, oc, :n], in_=bk[:, :n]), reads=[bbk], writes=[bqb[oc]])
            for h in range(4):
                pts_ = []
                for mc in range(2):
                    bk, bbk = psr.next()
                    for half in range(2):
                        fw.op("pe", lambda e: e.matmul(bk[:, :n], lhsT=kT[:, 2 * h + half, mc * 128:(mc + 1) * 128],
                                                        rhs=qb[:, 2 * h + half, :n], start=(half == 0), stop=(half == 1)),
                              reads=[b_kT, bqb[2 * h + half]], writes=[bbk])
                    p, bp = ptr.next()
                    fw.op("act", lambda e: e.activation(out=p[:, :n], in_=bk[:, :n], func=AF.Exp, scale=1.0 / 16.0),
                          reads=[bbk], writes=[bp])
                    pts_.append((p, bp))
                bd, bbd = psr.next()
                for mc in range(2):
                    fw.op("pe", lambda e: e.matmul(bd[:, :n], lhsT=ones_b[:, :], rhs=pts_[mc][0][:, :n],
                                                    start=(mc == 0), stop=(mc == 1)),
                          reads=[b_ones, pts_[mc][1]], writes=[bbd])
                r, br = rdx.next()
                fw.op("dve", lambda e: e.reciprocal(out=r[:, :n], in_=bd[:, :n]), reads=[bbd], writes=[br])
                for dvh in range(2):
                    bk, bbk = psr.next()
                    for mc in range(2):
                        fw.op("pe", lambda e: e.matmul(bk[:, :n], lhsT=vmem[:, mc, h * 256 + dvh * 128:h * 256 + dvh * 128 + 128],
                                                        rhs=pts_[mc][0][:, :n], start=(mc == 0), stop=(mc == 1)),
                              reads=[b_vmem, pts_[mc][1]], writes=[bbk])
                    fw.op("dve", lambda e: e.tensor_tensor(out=oab[:, 2 * h + dvh, :n], in0=bk[:, :n], in1=r[:, :n],
                                                           op=ALU.mult),
                          reads=[bbk, br], writes=[boab[2 * h + dvh]])
            ck(5)
            proj_resid(n, wo_s, b_wo, oab, boab)
            layer_norm(n, V_LN2)
            ck(6)
            for grp in range(11):
                w, bw = wst.next()
                fw.dma("pool", w[:, :, 0:256], wup_d[:, grp * 256:(grp + 1) * 256].rearrange("(kc p) n -> p kc n", p=128),
                       writes=[bw])
                fw.dma("pool", w[:, :, 256:512],
                       wup_d[:, 2816 + grp * 256:2816 + (grp + 1) * 256].rearrange("(kc p) n -> p kc n", p=128),
                       writes=[bw])
                for jj in range(2):
                    j = 2 * grp + jj
                    ba, bba = psr.next()
                    for kc in range(8):
                        fw.op("pe", lambda e: e.matmul(ba[:, :n], lhsT=w[:, kc, jj * 128:(jj + 1) * 128],
                                                        rhs=xb[:, kc, :n], start=(kc == 0), stop=(kc == 7)),
                              reads=[bw, bxb[kc]], writes=[bba])
                    bg_, bbg = psr.next()
                    for kc in range(8):
                        fw.op("pe", lambda e: e.matmul(bg_[:, :n], lhsT=w[:, kc, 256 + jj * 128:256 + (jj + 1) * 128],
                                                        rhs=xb[:, kc, :n], start=(kc == 0), stop=(kc == 7)),
                              reads=[bw, bxb[kc]], writes=[bbg])
                    if is_halo:
                        fw.op("dve", lambda e: e.tensor_scalar(out=carry[:, j, :], in0=ba[:, 0:2], scalar1=flag[:, 0:1],
                                                               scalar2=None, op0=ALU.mult),
                              reads=[bba, b_flag], writes=[bcar[j]])
                        fw.op("dve", lambda e: e.tensor_scalar(out=carry[:, 22 + j, :], in0=bg_[:, 0:2],
                                                               scalar1=flag[:, 0:1], scalar2=None, op0=ALU.mult),
                              reads=[bbg, b_flag], writes=[bcar[22 + j]])
                    else:
                        ya, bya = conv(ba, bba, j, n)
                        yg, byg = conv(bg_, bbg, 22 + j, n)
                        sg, bsg = sgr.next()
                        fw.op("act", lambda e: e.activation(out=sg[:, :n], in_=yg[:, :n], func=AF.Silu),
                              reads=[byg], writes=[bsg])
                        fw.op("dve", lambda e: e.tensor_tensor(out=hb[:, j, :n], in0=ya[:, :n], in1=sg[:, :n],
                                                               op=ALU.mult),
                              reads=[bya, bsg], writes=[bhb[j]])
            if is_halo:
                ck(7)
                return
            ck(8)
            for grp in range(11):
                w, bw = wdn.next()
                fw.dma("pool", w[:, :, :], wdn_d[grp * 256:(grp + 1) * 256, :].rearrange("(jj p) n -> p jj n", p=128),
                       writes=[bw])
                for jj in range(2):
                    j = 2 * grp + jj
                    for oc in range(8):
                        fw.op("pe", lambda e: e.matmul(banks[oc][:, :n], lhsT=w[:, jj, oc * 128:(oc + 1) * 128],
                                                        rhs=hb[:, j, :n], start=(j == 0), stop=(j == 21)),
                              reads=[bw, bhb[j]], writes=[bbank[oc]])
            for oc in range(8):
                fw.op("dve", lambda e: e.scalar_tensor_tensor(out=xr[:, oc, :n], in0=xr[:, oc, :n], scalar=ALPHA,
                                                              in1=banks[oc][:, :n], op0=ALU.mult, op1=ALU.add),
                      reads=[bxr[oc], bbank[oc]], writes=[bxr[oc]])
            layer_norm(n, V_LN3)

    if do_A:
        wuq_s = sb("wuq_s", [128, 3, 768], BF16)
        wuqr_s = sb("wuqr_s", [128, 3, 768], BF16)
        wukv_s = sb("wukv_s", [128, 2, 1024], BF16)
        b_wuq, b_wuqr, b_wukv = Buf(), Buf(), Buf()
        load_w(wuq_s[:, :, :], wuq_d, b_wuq)
        load_w(wukv_s[:, :, :], wukv_d, b_wukv)
        for kc in range(3):
            fw.op("pool", lambda e: e.tensor_scalar(out=wuq_s[:, kc, :], in0=wuq_s[:, kc, :],
                                                    scalar1=vec[:, V_CQG + kc:V_CQG + kc + 1], scalar2=None, op0=ALU.mult),
                  reads=[b_wuq, b_vec], writes=[b_wuq])
        for kc in range(2):
            fw.op("pool", lambda e: e.tensor_scalar(out=wukv_s[:, kc, :], in0=wukv_s[:, kc, :],
                                                    scalar1=vec[:, V_CKVG + kc:V_CKVG + kc + 1], scalar2=None, op0=ALU.mult),
                  reads=[b_wukv, b_vec], writes=[b_wukv])
        fw.op("pool", lambda e: e.memset(wuqr_s[:, :, :], 0.0), writes=[b_wuqr])
        w4 = wuq_s[:, :, :].rearrange("p k (h c) -> p k h c", c=96)
        r4 = wuqr_s[:, :, :].rearrange("p k (h c) -> p k h c", c=96)
        for kc in range(3):
            fw.op("pool", lambda e: e.tensor_scalar(out=r4[:, kc, :, 64:80], in0=w4[:, kc, :, 80:96], scalar1=-1.0,
                                                    scalar2=None, op0=ALU.mult), reads=[b_wuq], writes=[b_wuqr])
            fw.op("pool", lambda e: e.tensor_copy(r4[:, kc, :, 80:96], w4[:, kc, :, 64:80]), reads=[b_wuq],
                  writes=[b_wuqr])
        wkrr = sb("wkrr", [128, 8, 32], BF16)
        b_wkrr = Buf()
        craw = sb("craw", [128, 5, 512], F32)
        bcraw = [Buf() for _ in range(5)]
        cn = sb("cn", [128, 5, 512], BF16)
        bcn = [Buf() for _ in range(5)]
        st_r = sb("st_r", [128, 512], F32)
        b_str = Buf()
        tabq = sb("tabq", [128, 2, 512], F32)
        tabk = tabq[0:32, :, :]
        b_tab = Buf()
        osb = Rot([(sb("osb%d" % i, [128, 512], BF16), Buf()) for i in range(4)])
        if do_C:
            rt = Rot(yr_items)
        else:
            rt = Rot([(sb("rt%d" % i, [128, 512], F32), Buf()) for i in range(4)])
        glo = Rot([(sb("glo%d" % i, [24, 512], F32), Buf()) for i in range(1)])

        def evac_out(bk, bbk, rows, n, dst, eng="act"):
            o, bo = osb.next()
            if eng == "act":
                fw.op("act", lambda e: e.copy(out=o[0:rows, :n], in_=bk[0:rows, :n]), reads=[bbk], writes=[bo])
            else:
                fw.op("dve", lambda e: e.tensor_copy(o[0:rows, :n], bk[0:rows, :n]), reads=[bbk], writes=[bo])
            fw.dma("sp", dst, o[0:rows, :n], reads=[bo], writes=[Buf()])

        def rms(chunks, nfeat, n):
            bs, bbs = psr.next()
            for i, c in enumerate(chunks):
                tq, btq = lnt.next()
                fw.op("act", lambda e: e.activation(out=tq[:, :n], in_=craw[:, c, :n], func=AF.Square),
                      reads=[bcraw[c]], writes=[btq])
                fw.op("pe", lambda e: e.matmul(bs[:, :n], lhsT=ones_b[:, :], rhs=tq[:, :n], start=(i == 0),
                                                stop=(i == len(chunks) - 1)), reads=[btq, b_ones], writes=[bbs])
            fw.op("dve", lambda e: e.tensor_scalar(out=st_r[:, :n], in0=bs[:, :n], scalar1=1.0 / nfeat, scalar2=1e-6,
                                                   op0=ALU.mult, op1=ALU.add), reads=[bbs], writes=[b_str])
            fw.op("act", lambda e: e.activation(out=st_r[:, :n], in_=st_r[:, :n], func=AF.Sqrt),
                  reads=[b_str], writes=[b_str])
            fw.op("dve", lambda e: e.reciprocal(out=st_r[:, :n], in_=st_r[:, :n]), reads=[b_str], writes=[b_str])
            for c in chunks:
                fw.op("dve", lambda e: e.tensor_tensor(out=cn[:, c, :n], in0=craw[:, c, :n], in1=st_r[:, :n],
                                                       op=ALU.mult), reads=[bcraw[c], b_str], writes=[bcn[c]])

        def a_tile(t0):
            n = 512
            fw.dma("sp", tabq[64:96, 0, :], ropeC_d[:, t0:t0 + n], writes=[b_tab])
            fw.dma("sp", tabq[64:96, 1, :], ropeS_d[:, t0:t0 + n], writes=[b_tab])
            fw.dma("sp", tabk[:, 0, :], ropeC_d[:, t0:t0 + n], writes=[b_tab])
            fw.dma("sp", tabk[:, 1, :], ropeS_d[:, t0:t0 + n], writes=[b_tab])

            def hproj(w, bw, c0, m):
                bk, bbk = psr.next()
                for kc in range(8):
                    fw.op("pe", lambda e: e.matmul(bk[0:m, :n], lhsT=w[:, kc, c0:c0 + m], rhs=xb[:, kc, :n],
                                                    start=(kc == 0), stop=(kc == 7)), reads=[bw, bxb[kc]], writes=[bbk])
                return bk, bbk

            w, bw = wst.next()
            load_w(w[:, :, 0:384], w_in_d[:, 0:384], bw)
            for c in range(3):
                bk, bbk = hproj(w, bw, c * 128, 128)
                fw.op("act", lambda e: e.copy(out=craw[:, c, :n], in_=bk[:, :n]), reads=[bbk], writes=[bcraw[c]])
            rms([0, 1, 2], 384.0, n)
            w, bw = wst.next()
            load_w(w[:, :, 0:288], w_in_d[:, 384:672], bw)
            for c in range(2):
                bk, bbk = hproj(w, bw, c * 128, 128)
                fw.op("act", lambda e: e.copy(out=craw[:, 3 + c, :n], in_=bk[:, :n]), reads=[bbk], writes=[bcraw[3 + c]])
            rms([3, 4], 256.0, n)
            fw.op("dve", lambda e: e.tensor_scalar(out=wkrr[:, :, 0:16], in0=w[:, :, 272:288], scalar1=-1.0,
                                                   scalar2=None, op0=ALU.mult), reads=[bw], writes=[b_wkrr])
            fw.op("dve", lambda e: e.tensor_copy(wkrr[:, :, 16:32], w[:, :, 256:272]), reads=[bw], writes=[b_wkrr])
            bk, bbk = hproj(w, bw, 256, 32)
            bk2, bbk2 = hproj(wkrr, b_wkrr, 0, 32)
            t1, bt1 = rt.next()
            t2, bt2 = rt.next()
            fw.op("dve", lambda e: e.tensor_tensor(out=t1[0:32, :n], in0=bk[0:32, :n], in1=tabk[:, 0, :n], op=ALU.mult),
                  reads=[bbk, b_tab], writes=[bt1])
            fw.op("dve", lambda e: e.tensor_tensor(out=t2[0:32, :n], in0=bk2[0:32, :n], in1=tabk[:, 1, :n], op=ALU.mult),
                  reads=[bbk2, b_tab], writes=[bt2])
            o, bo = osb.next()
            fw.op("dve", lambda e: e.tensor_tensor(out=o[0:32, :n], in0=t1[0:32, :n], in1=t2[0:32, :n], op=ALU.add),
                  reads=[bt1, bt2], writes=[bo])
            fw.dma("sp", kr_d[:, t0:t0 + n], o[0:32, :n], reads=[bo], writes=[Buf()])
            for h in range(8):
                bk, bbk = psr.next()
                bk2, bbk2 = psr.next()
                for kc in range(3):
                    fw.op("pe", lambda e: e.matmul(bk[0:96, :n], lhsT=wuq_s[:, kc, h * 96:(h + 1) * 96], rhs=cn[:, kc, :n],
                                                    start=(kc == 0), stop=(kc == 2)), reads=[b_wuq, bcn[kc]], writes=[bbk])
                for kc in range(3):
                    fw.op("pe", lambda e: e.matmul(bk2[0:96, :n], lhsT=wuqr_s[:, kc, h * 96:(h + 1) * 96], rhs=cn[:, kc, :n],
                                                    start=(kc == 0), stop=(kc == 2)), reads=[b_wuqr, bcn[kc]], writes=[bbk2])
                o, bo = osb.next()
                fw.op("act", lambda e: e.copy(out=o[0:64, :n], in_=bk[0:64, :n]), reads=[bbk], writes=[bo])
                t1, bt1 = rt.next()
                t2, bt2 = rt.next()
                fw.op("dve", lambda e: e.tensor_tensor(out=t1[64:96, :n], in0=bk[64:96, :n], in1=tabq[64:96, 0, :n],
                                                       op=ALU.mult), reads=[bbk, b_tab], writes=[bt1])
                fw.op("dve", lambda e: e.tensor_tensor(out=t2[64:96, :n], in0=bk2[64:96, :n], in1=tabq[64:96, 1, :n],
                                                       op=ALU.mult), reads=[bbk2, b_tab], writes=[bt2])
                fw.op("dve", lambda e: e.tensor_tensor(out=o[64:96, :n], in0=t1[64:96, :n], in1=t2[64:96, :n],
                                                       op=ALU.add), reads=[bt1, bt2, bo], writes=[bo])
                fw.dma("sp", qm_d[h, :, t0:t0 + n], o[0:96, :n], reads=[bo], writes=[Buf()])
            for h in range(8):
                bk, bbk = psr.next()
                for kc in range(2):
                    fw.op("pe", lambda e: e.matmul(bk[0:64, :n], lhsT=wukv_s[:, kc, h * 128:h * 128 + 64],
                                                    rhs=cn[:, 3 + kc, :n], start=(kc == 0), stop=(kc == 1)),
                          reads=[b_wukv, bcn[3 + kc]], writes=[bbk])
                evac_out(bk, bbk, 64, n, kn_d[h, :, t0:t0 + n], eng=("act" if h % 2 else "dve"))
            wv4 = wukv_s[:, :, :].rearrange("p k (h c) -> p k h c", c=128)
            for tb in range(4):
                bk, bbk = psr.next()
                for kc in range(2):
                    fw.op("pe", lambda e: e.matmul(bk[:, :].rearrange("p (h c) -> p h c", c=64),
                                                    lhsT=cn[:, 3 + kc, tb * 128:(tb + 1) * 128],
                                                    rhs=wv4[:, kc, :, 64:128], start=(kc == 0), stop=(kc == 1)),
                          reads=[b_wukv, bcn[3 + kc]], writes=[bbk])
                evac_out(bk, bbk, 128, 512, vm_d[t0 + tb * 128:t0 + (tb + 1) * 128, :], eng="dve")
            w, bw = wst.next()
            load_w(w[:, :, 0:512], w_in_d[:, 672:1184], bw)
            for c in range(4):
                bk, bbk = hproj(w, bw, c * 128, 128)
                evac_out(bk, bbk, 128, n, nq_d[c * 128:(c + 1) * 128, t0:t0 + n], eng=("act" if c % 2 else "dve"))
            w, bw = wst.next()
            load_w(w[:, :, 0:512], w_in_d[:, 1184:1696], bw)
            for c in range(3):
                bk, bbk = hproj(w, bw, c * 128, 128)
                evac_out(bk, bbk, 128, n, nkf_d[c, :, t0:t0 + n], eng=("act" if c % 2 else "dve"))
            wC, bwC = w, bw
            w, bw = wst.next()
            load_w(w[:, :, 0:280], w_in_d[:, 1696:1976], bw)
            bk, bbk = hproj(w, bw, 0, 128)
            evac_out(bk, bbk, 128, n, nkf_d[3, :, t0:t0 + n])
            bk, bbk = hproj(w, bw, 256, 24)
            g, bg = glo.next()
            fw.op("act", lambda e: e.activation(out=g[:, :n], in_=bk[0:24, :n], func=AF.Sigmoid), reads=[bbk], writes=[bg])
            fw.dma("sp", gl_d[:, t0:t0 + n], g[:, :n], reads=[bg], writes=[Buf()])
            for tb in range(4):
                bk, bbk = psr.next()
                for kc in range(8):
                    fw.op("pe", lambda e: e.matmul(bk[:, 0:128], lhsT=xb[:, kc, tb * 128:(tb + 1) * 128],
                                                    rhs=wC[:, kc, 384:512], start=(kc == 0), stop=(kc == 7)),
                          reads=[bwC, bxb[kc]], writes=[bbk])
                for kc in range(8):
                    fw.op("pe", lambda e: e.matmul(bk[:, 128:256], lhsT=xb[:, kc, tb * 128:(tb + 1) * 128],
                                                    rhs=w[:, kc, 128:256], start=(kc == 0), stop=(kc == 7)),
                          reads=[bw, bxb[kc]], writes=[bbk])
                o, bo = osb.next()
                fw.op("dve", lambda e: e.tensor_copy(o[:, 0:256], bk[:, 0:256]), reads=[bbk], writes=[bo])
                fw.dma("sp", nvt_d[t0 + tb * 128:t0 + (tb + 1) * 128, :], o[:, 0:256], reads=[bo], writes=[Buf()])

    def _program():
        if do_C:
            c_tile(2, 0, True)
        for ti in range(TT // 512):
            t0 = ti * 512
            if do_C:
                c_tile(512, HALO + t0, False)
            else:
                fw.dma("sp", xr[:, :, :], xT_d[:, t0:t0 + 512].rearrange("(c p) t -> p c t", p=128), writes=bxr)
                if do_ln_in:
                    layer_norm(512, V_LNIN)
            fw.dma("sp", xo_d[:, t0:t0 + 512].rearrange("(c p) t -> p c t", p=128), xr[:, :, :], reads=bxr,
                   writes=[Buf()])
            if do_A:
                a_tile(t0)

    try:
        _program()
    except _Stop:
        pass
    fw.finish()
    return nc


def _colvec(v):
    v = np.asarray(v, np.float32)
    return v.reshape(-1, 128).T


def _rope_tabs():
    inv = (10000.0 ** (-np.arange(0, 32, 2, dtype=np.float32) / np.float32(32))).astype(np.float32)
    ang = np.arange(S, dtype=np.float32)[:, None] * inv[None, :]
    c = np.cos(ang).astype(np.float32).T
    s = np.sin(ang).astype(np.float32).T
    return np.concatenate([c, c], 0), np.concatenate([s, s], 0)


def _prep_T(P, lC, lA, xT_list, oT_list, do_ln_in):
    vec = np.zeros((128, 264), np.float32)
    if do_ln_in:
        vec[:, 0:8] = _colvec(P["ln_in_g"])
        vec[:, 8:16] = _colvec(P["ln_in_b"])
    if lC is not None:
        for base, nm in ((16, "ln1"), (32, "ln2"), (48, "ln3")):
            vec[:, base:base + 8] = _colvec(P[nm + "_g"][lC])
            vec[:, base + 8:base + 16] = _colvec(P[nm + "_b"][lC])
        vec[:, 64:72] = _colvec(P["ln_mem_g"])
        vec[:, 72:80] = _colvec(P["ln_mem_b"])
        for k in range(3):
            vec[:, 80 + 44 * k:80 + 44 * (k + 1)] = _colvec(P["ffn_conv_w"][lC][k])
        vec[:, 80 + 132:80 + 176] = _colvec(P["ffn_conv_b"][lC])
    if lA is not None:
        vec[:, 256:259] = _colvec(P["mla_cq_g"][lA])
        vec[:, 259:261] = _colvec(P["mla_ckv_g"][lA])
        rc, rs = _rope_tabs()
    maps = []
    f32 = lambda a: np.ascontiguousarray(a, dtype=np.float32)
    for c in range(NCORES):
        b, i = c // 4, c % 4
        m = dict(vec=vec)
        if lC is not None:
            if i == 0:
                halo_x = np.zeros((D, 2), np.float32)
                halo_o = np.zeros((D, 2), NPBF)
            else:
                halo_x = xT_list[c - 1][:, -2:]
                halo_o = oT_list[c - 1][:, -2:]
            m["xT"] = np.ascontiguousarray(np.concatenate([halo_x, xT_list[c]], axis=1))
            m["oT"] = np.ascontiguousarray(np.concatenate([halo_o, oT_list[c]], axis=1))
            m["flag"] = np.full((128, 1), 0.0 if i == 0 else 1.0, np.float32)
            m["memT"] = f32(P["mem"][b].T)
            m["w_out"] = f32(P["w_out"][lC])
            m["xa_wq"] = f32(P["xa_wq"][lC])
            m["xa_wkv"] = f32(P["xa_wkv"][lC])
            m["xa_wo"] = f32(P["xa_wo"][lC])
            m["w_up"] = f32(P["ffn_w_up"][lC])
            m["w_down"] = f32(P["ffn_w_down"][lC])
        else:
            m["xT"] = xT_list[c]
        if lA is not None:
            m["w_in"] = f32(P["w_in"][lA])
            m["w_uq"] = f32(P["mla_w_uq"][lA])
            m["w_ukv"] = f32(P["mla_w_ukv"][lA])
            m["ropeC"] = np.ascontiguousarray(rc[:, i * TT:(i + 1) * TT])
            m["ropeS"] = np.ascontiguousarray(rs[:, i * TT:(i + 1) * TT])
        maps.append(m)
    return maps


def _A_outs(res):
    ao = []
    for r in res:
        ao.append(dict(qm=r["qm"], kn=r["kn"], kr=r["kr"], vm=r["vm"],
                       nq=r["nq"].reshape(8, 64, TT), nkf=r["nkf"].reshape(4, 2, 64, TT),
                       nvt=r["nvt"].reshape(TT, 2, 2, 64), gl=r["gl"]))
    return ao


_PROGS = {}


def _prog(key):
    if key not in _PROGS:
        if key == "B":
            _PROGS[key] = build_B()
        else:
            _PROGS[key] = build_T(*key)
    return _PROGS[key]


def _run(key, maps):
    res = run_bass_kernel_spmd(_prog(key), maps, core_ids=list(range(NCORES)))
    return res.results


def kernel(**inputs):
    P = {k: np.asarray(v) for k, v in inputs.items()}
    x = P["x"]
    xT = [np.ascontiguousarray(x[c // 4, (c % 4) * TT:(c % 4 + 1) * TT].T, dtype=np.float32) for c in range(NCORES)]
    res = _run((True, False, True), _prep_T(P, None, 0, xT, None, True))
    xn = [r["xo"] for r in res]
    ao = _A_outs(res)
    out = np.zeros((2, S, D), np.float32)
    for l in range(2):
        resB = _run("B", _prep_B(ao, l, P))
        oT = _gather_B(resB)
        if l == 0:
            res = _run((False, True, True), _prep_T(P, 0, 1, xn, oT, False))
            xn = [r["xo"] for r in res]
            ao = _A_outs(res)
        else:
            res = _run((False, True, False), _prep_T(P, 1, None, xn, oT, False))
            for c in range(NCORES):
                out[c // 4, (c % 4) * TT:(c % 4 + 1) * TT, :] = res[c]["xo"].T
    return out
```
